# Optimizing a Trainium2 kernel written in Bass

```python
import math
import jax, jax.numpy as jnp
from jax import lax
import numpy as np

D_MODEL = 4096
BATCH = 4
SEQ = 2048
DEPTH = 2

N_EVEN = (DEPTH + 1) // 2
N_ODD = DEPTH // 2
EPS = 1e-6
ROPE_THETA = 500000.0
HEAD_DIM = 128
ROT_DIM = HEAD_DIM // 4

S5_WIDTH = D_MODEL // 4
S5_GROUP = 16
S5_GROUPS = S5_WIDTH // S5_GROUP
S5_STATE = 64

DIL_PATTERNS = ((128, 1), (512, 4), (2048, 16))
N_DIL = len(DIL_PATTERNS)
DIL_HEADS = D_MODEL // 512
DIL_WIDTH = DIL_HEADS * HEAD_DIM

EVEN_IN = 2 * S5_WIDTH + 3 * N_DIL * DIL_WIDTH + DIL_WIDTH
EVEN_MIX = S5_WIDTH + DIL_WIDTH

MLA_HEADS = D_MODEL // HEAD_DIM
MLA_NOPE = HEAD_DIM - ROT_DIM
MLA_V = HEAD_DIM
Q_RANK = 1536
KV_RANK = 512
IDX_HEADS = D_MODEL // 128
IDX_DIM = 128
IDX_TOPK_MAX = 256
Q_BLOCK = 128
C_WIDTH = MLA_HEADS * MLA_V
ODD_IN = Q_RANK + KV_RANK + ROT_DIM + IDX_DIM + IDX_HEADS + C_WIDTH

kernel_name = 'hybrid_s5_dilated_dsa_trunk'


def rmsnorm(x, g):
    xf = x.astype(jnp.float32)
    y = xf * lax.rsqrt(jnp.mean(xf * xf, axis=-1, keepdims=True) + EPS)
    return (y * g.astype(jnp.float32)).astype(x.dtype)


def rope_tables(L):
    inv = ROPE_THETA ** (-jnp.arange(0, ROT_DIM, 2, dtype=jnp.float32) / ROT_DIM)
    ang = jnp.arange(L, dtype=jnp.float32)[:, None] * inv[None, :]
    return jnp.cos(ang), jnp.sin(ang)


def apply_rope(x, cos, sin):
    half = ROT_DIM // 2
    c = cos[None, :, None, :].astype(x.dtype)
    s = sin[None, :, None, :].astype(x.dtype)
    x1 = x[..., :half]
    x2 = x[..., half:ROT_DIM]
    return jnp.concatenate([x1 * c - x2 * s, x2 * c + x1 * s, x[..., ROT_DIM:]], axis=-1)


def _complex_affine_combine(e1, e2):
    a1r, a1i, b1r, b1i = e1
    a2r, a2i, b2r, b2i = e2
    ar = a2r * a1r - a2i * a1i
    ai = a2r * a1i + a2i * a1r
    br = a2r * b1r - a2i * b1i + b2r
    bi = a2r * b1i + a2i * b1r + b2i
    return ar, ai, br, bi


def s5_ssm(u, lam_re, lam_im, log_step, b_re, b_im, c_re, c_im, d_skip):
    bsz, L, _ = u.shape
    uf = u.astype(jnp.float32).reshape(bsz, L, S5_GROUPS, S5_GROUP)
    lr = jnp.minimum(lam_re.astype(jnp.float32), -1e-4)
    li = lam_im.astype(jnp.float32)
    dt = jnp.exp(log_step.astype(jnp.float32))[:, None]
    mag = jnp.exp(lr * dt)
    ar = mag * jnp.cos(li * dt)
    ai = mag * jnp.sin(li * dt)
    den = lr * lr + li * li
    nr, ni = ar - 1.0, ai
    cr = (nr * lr + ni * li) / den
    ci = (ni * lr - nr * li) / den
    br32, bi32 = b_re.astype(jnp.float32), b_im.astype(jnp.float32)
    bbr = cr[..., None] * br32 - ci[..., None] * bi32
    bbi = cr[..., None] * bi32 + ci[..., None] * br32
    bu_r = jnp.einsum('blgj,gpj->blgp', uf, bbr)
    bu_i = jnp.einsum('blgj,gpj->blgp', uf, bbi)
    a_r = jnp.broadcast_to(ar, bu_r.shape)
    a_i = jnp.broadcast_to(ai, bu_i.shape)
    _, _, s_r, s_i = lax.associative_scan(_complex_affine_combine, (a_r, a_i, bu_r, bu_i), axis=1)
    y = (jnp.einsum('gjp,blgp->blgj', c_re.astype(jnp.float32), s_r)
         - jnp.einsum('gjp,blgp->blgj', c_im.astype(jnp.float32), s_i))
    y = y.reshape(bsz, L, S5_WIDTH) + d_skip.astype(jnp.float32) * uf.reshape(bsz, L, S5_WIDTH)
    return y.astype(u.dtype)


def dilated_window_attention(q, k, v, dil, sub_window):
    bsz, L, H, D = q.shape
    n = L // dil
    qb = min(sub_window, n)
    nb = -(-n // qb)
    n_pad = nb * qb

    def to_sub(t):
        t = t.reshape(bsz, n, dil, H, D).transpose(0, 2, 3, 1, 4)
        return jnp.pad(t, ((0, 0), (0, 0), (0, 0), (0, n_pad - n), (0, 0)))

    def prev_cur(t):
        cur = t.reshape(bsz, dil, H, nb, qb, D)
        prev = jnp.pad(t, ((0, 0), (0, 0), (0, 0), (qb, 0), (0, 0)))[:, :, :, :n_pad]
        prev = prev.reshape(bsz, dil, H, nb, qb, D)
        return jnp.concatenate([prev, cur], axis=-2)

    qs = to_sub(q).reshape(bsz, dil, H, nb, qb, D)
    kb = prev_cur(to_sub(k))
    vb = prev_cur(to_sub(v))
    s = jnp.einsum('brhnqe,brhnke->brhnqk', qs, kb).astype(jnp.float32) * (HEAD_DIM ** -0.5)
    qi = jnp.arange(qb)[:, None]
    kj = jnp.arange(2 * qb)[None, :]
    dist = qi + qb - kj
    blk = jnp.arange(nb)[:, None, None]
    mask = (dist >= 0) & (dist <= sub_window) & ((blk > 0) | (kj >= qb))
    s = jnp.where(mask, s, -jnp.inf)
    lse = jax.nn.logsumexp(s, axis=-1)
    p = jnp.exp(s - lse[..., None]).astype(v.dtype)
    o = jnp.einsum('brhnqk,brhnke->brhnqe', p, vb)
    o = o.reshape(bsz, dil, H, n_pad, D)[:, :, :, :n]
    o = o.transpose(0, 3, 1, 2, 4).reshape(bsz, L, H, D)
    lse = lse.reshape(bsz, dil, H, n_pad)[..., :n].transpose(0, 3, 1, 2).reshape(bsz, L, H)
    return o, lse


def even_mixer(h, w_in, lam_re, lam_im, log_step, b_re, b_im, c_re, c_im, d_skip, glu_w, glu_b, w_out):
    bsz, L, _ = h.shape
    proj = h @ w_in
    u, z_a, qkv, z_b = jnp.split(
        proj, [S5_WIDTH, 2 * S5_WIDTH, 2 * S5_WIDTH + 3 * N_DIL * DIL_WIDTH], axis=-1)
    y = jax.nn.gelu(s5_ssm(u, lam_re, lam_im, log_step, b_re, b_im, c_re, c_im, d_skip))
    y = y * jax.nn.sigmoid(y @ glu_w + glu_b)
    a_out = y * jax.nn.silu(z_a)
    qkv = qkv.reshape(bsz, L, N_DIL, 3, DIL_HEADS, HEAD_DIM)
    cos, sin = rope_tables(L)
    outs, lses = [], []
    for g, (window, dil) in enumerate(DIL_PATTERNS):
        q = apply_rope(qkv[:, :, g, 0], cos, sin)
        k = apply_rope(qkv[:, :, g, 1], cos, sin)
        o, lse = dilated_window_attention(q, k, qkv[:, :, g, 2], dil, window // dil)
        outs.append(o)
        lses.append(lse)
    alpha = jax.nn.softmax(jnp.stack(lses, axis=0), axis=0).astype(h.dtype)
    o = jnp.sum(alpha[..., None] * jnp.stack(outs, axis=0), axis=0).reshape(bsz, L, DIL_WIDTH)
    b_out = o * jax.nn.silu(z_b)
    return jnp.concatenate([a_out, b_out], axis=-1) @ w_out


def odd_mixer(h, w_in, q_norm, kv_norm, k_idx_norm, w_uq, w_uk, w_uv, w_iq, w_out):
    bsz, L, _ = h.shape
    proj = h @ w_in
    c_q, c_kv, k_r, k_i, w_i, gate = jnp.split(
        proj, np.cumsum([Q_RANK, KV_RANK, ROT_DIM, IDX_DIM, IDX_HEADS]).tolist(), axis=-1)
    c_q = rmsnorm(c_q, q_norm)
    c_kv = rmsnorm(c_kv, kv_norm)
    cos, sin = rope_tables(L)
    q = apply_rope((c_q @ w_uq).reshape(bsz, L, MLA_HEADS, HEAD_DIM), cos, sin)
    q_rope, q_nope = q[..., :ROT_DIM], q[..., ROT_DIM:]
    k_rope = apply_rope(k_r[:, :, None, :], cos, sin)[:, :, 0]
    q_idx = apply_rope((c_q @ w_iq).reshape(bsz, L, IDX_HEADS, IDX_DIM), cos, sin)
    k_idx = apply_rope(rmsnorm(k_i, k_idx_norm)[:, :, None, :], cos, sin)[:, :, 0]
    topk = min(IDX_TOPK_MAX, L // 4)
    nb = L // Q_BLOCK
    kpos = jnp.arange(L)

    def to_blocks(t):
        return t.reshape(bsz, nb, Q_BLOCK, *t.shape[2:]).swapaxes(0, 1)

    def block_fn(args):
        start, qn, qr, qi, wi = args
        qpos = start + jnp.arange(Q_BLOCK)
        si = jnp.einsum('bqhd,bkd->bqhk', qi, k_idx).astype(jnp.float32) * (IDX_DIM ** -0.5)
        si = jnp.einsum('bqhk,bqh->bqk', jax.nn.relu(si),
                        wi.astype(jnp.float32) * (IDX_HEADS ** -0.5))
        si = jnp.where((kpos[None, :] <= qpos[:, None])[None], si, -jnp.inf)
        _, sel = lax.top_k(si, topk)
        valid = sel <= qpos[None, :, None]
        c_sel = jax.vmap(lambda c, i: c[i])(c_kv, sel)
        kr_sel = jax.vmap(lambda c, i: c[i])(k_rope, sel)
        q_lat = jnp.einsum('bqhd,chd->bqhc', qn, w_uk)
        s = (jnp.einsum('bqhc,bqkc->bqhk', q_lat, c_sel)
             + jnp.einsum('bqhr,bqkr->bqhk', qr, kr_sel)).astype(jnp.float32) * (HEAD_DIM ** -0.5)
        s = jnp.where(valid[:, :, None, :], s, -jnp.inf)
        p = jax.nn.softmax(s, axis=-1).astype(c_sel.dtype)
        o_lat = jnp.einsum('bqhk,bqkc->bqhc', p, c_sel)
        return jnp.einsum('bqhc,chd->bqhd', o_lat, w_uv)

    starts = jnp.arange(nb) * Q_BLOCK
    o = lax.map(block_fn, (starts, to_blocks(q_nope), to_blocks(q_rope), to_blocks(q_idx), to_blocks(w_i)))
    o = o.swapaxes(0, 1).reshape(bsz, L, C_WIDTH)
    return (o * jax.nn.silu(gate)) @ w_out


def setup_inputs(seed: int = 0) -> dict:
    key = jax.random.key(seed)
    ks = jax.random.split(key, 26)
    f32 = jnp.float32

    def nrm(k, shape, scale):
        return jax.random.normal(k, shape, f32) * scale

    lam_im_base = jnp.pi * jnp.arange(S5_STATE, dtype=f32)
    return {
        'x': nrm(ks[0], (BATCH, SEQ, D_MODEL), 1.0),
        'even_norm': 1.0 + nrm(ks[1], (N_EVEN, D_MODEL), 0.02),
        'even_w_in': nrm(ks[2], (N_EVEN, D_MODEL, EVEN_IN), D_MODEL ** -0.5),
        's5_lambda_re': -0.5 + nrm(ks[3], (N_EVEN, S5_GROUPS, S5_STATE), 0.01),
        's5_lambda_im': lam_im_base + nrm(ks[4], (N_EVEN, S5_GROUPS, S5_STATE), 0.01),
        's5_log_step': jax.random.uniform(ks[5], (N_EVEN, S5_GROUPS), f32, math.log(1e-3), math.log(1e-1)),
        's5_b_re': nrm(ks[6], (N_EVEN, S5_GROUPS, S5_STATE, S5_GROUP), (2 * S5_GROUP) ** -0.5),
        's5_b_im': nrm(ks[7], (N_EVEN, S5_GROUPS, S5_STATE, S5_GROUP), (2 * S5_GROUP) ** -0.5),
        's5_c_re': nrm(ks[8], (N_EVEN, S5_GROUPS, S5_GROUP, S5_STATE), 0.5),
        's5_c_im': nrm(ks[9], (N_EVEN, S5_GROUPS, S5_GROUP, S5_STATE), 0.5),
        's5_d': nrm(ks[10], (N_EVEN, S5_WIDTH), 1.0),
        's5_glu_w': nrm(ks[11], (N_EVEN, S5_WIDTH, S5_WIDTH), S5_WIDTH ** -0.5),
        's5_glu_b': nrm(ks[12], (N_EVEN, S5_WIDTH), 0.01),
        'even_w_out': nrm(ks[13], (N_EVEN, EVEN_MIX, D_MODEL), EVEN_MIX ** -0.5),
        'odd_norm': 1.0 + nrm(ks[14], (N_ODD, D_MODEL), 0.02),
        'odd_w_in': nrm(ks[15], (N_ODD, D_MODEL, ODD_IN), D_MODEL ** -0.5),
        'mla_q_norm': 1.0 + nrm(ks[16], (N_ODD, Q_RANK), 0.02),
        'mla_kv_norm': 1.0 + nrm(ks[17], (N_ODD, KV_RANK), 0.02),
        'idx_k_norm': 1.0 + nrm(ks[18], (N_ODD, IDX_DIM), 0.02),
        'mla_w_uq': nrm(ks[19], (N_ODD, Q_RANK, MLA_HEADS * HEAD_DIM), Q_RANK ** -0.5),
        'mla_w_uk': nrm(ks[20], (N_ODD, KV_RANK, MLA_HEADS, MLA_NOPE), KV_RANK ** -0.5),
        'mla_w_uv': nrm(ks[21], (N_ODD, KV_RANK, MLA_HEADS, MLA_V), KV_RANK ** -0.5),
        'idx_w_q': nrm(ks[22], (N_ODD, Q_RANK, IDX_HEADS * IDX_DIM), Q_RANK ** -0.5),
        'odd_w_out': nrm(ks[23], (N_ODD, C_WIDTH, D_MODEL), C_WIDTH ** -0.5),
        'final_norm': 1.0 + nrm(ks[24], (D_MODEL,), 0.02),
    }


def reference(x, even_norm, even_w_in, s5_lambda_re, s5_lambda_im, s5_log_step, s5_b_re, s5_b_im,
              s5_c_re, s5_c_im, s5_d, s5_glu_w, s5_glu_b, even_w_out, odd_norm, odd_w_in,
              mla_q_norm, mla_kv_norm, idx_k_norm, mla_w_uq, mla_w_uk, mla_w_uv, idx_w_q,
              odd_w_out, final_norm):
    for layer in range(DEPTH):
        i = layer // 2
        if layer % 2 == 0:
            h = rmsnorm(x, even_norm[i])
            x = x + even_mixer(h, even_w_in[i], s5_lambda_re[i], s5_lambda_im[i], s5_log_step[i],
                               s5_b_re[i], s5_b_im[i], s5_c_re[i], s5_c_im[i], s5_d[i],
                               s5_glu_w[i], s5_glu_b[i], even_w_out[i])
        else:
            h = rmsnorm(x, odd_norm[i])
            x = x + odd_mixer(h, odd_w_in[i], mla_q_norm[i], mla_kv_norm[i], idx_k_norm[i],
                              mla_w_uq[i], mla_w_uk[i], mla_w_uv[i], idx_w_q[i], odd_w_out[i])
    return rmsnorm(x, final_norm)
```

```python
import math
import numpy as np
import concourse.bass as bass
import concourse.mybir as mybir
from concourse.bass_utils import run_bass_kernel_spmd

F32 = mybir.dt.float32
BF16 = mybir.dt.bfloat16
AF = mybir.ActivationFunctionType
ALU = mybir.AluOpType
AX = mybir.AxisListType

D_MODEL = 4096
SEQ = 2048
BATCH = 4
HALF = 1024
EPS = 1e-6
NDS = 20


class Buf:
    def __init__(self, name, t=None):
        self.name = name
        self.t = t
        self.w = {}
        self.r = {}
        self.track = True

    def __getitem__(self, k):
        return self.t[k]


class KB:
    def __init__(self, nc, stack):
        self.nc = nc
        self.eobj = {"pe": nc.tensor, "act": nc.scalar, "dve": nc.vector, "pool": nc.gpsimd, "sp": nc.sync}
        self.semsets = [{n: stack.enter_context(nc.semaphore("s%d_%s" % (i, n))) for n in self.eobj}
                        for i in range(3)]
        self.epoch = 0
        self.sem = self.semsets[0]
        self.cnt = {n: 0 for n in self.eobj}
        self.tot = {n: 0 for n in self.eobj}
        self.waited = {n: {} for n in self.eobj}
        self.dsem = {}
        self.dcnt = {}
        for qn in ("sp", "pool"):
            self.dsem[qn] = [stack.enter_context(nc.semaphore("d_%s%d" % (qn, i))) for i in range(NDS)]
            self.dcnt[qn] = 0
        self.nops = 0
        self.nuid = 0
        self.active = {}

    def sb(self, ctx, name, shape, dt):
        self.nuid += 1
        name = "%s_%d" % (name, self.nuid)
        t = ctx.enter_context(self.nc.sbuf_tensor(name, list(shape), dt))
        return Buf(name, t)

    def ps(self, ctx, name, shape, dt=F32):
        self.nuid += 1
        name = "%s_%d" % (name, self.nuid)
        t = ctx.enter_context(self.nc.psum_tensor(name, list(shape), dt))
        return Buf(name, t)

    def dram(self, name, shape, dt, kind="Internal"):
        t = self.nc.dram_tensor(name, list(shape), dt, kind=kind).ap()
        b = Buf(name, t)
        b.track = False
        return b

    def _deps(self, reads, writes):
        deps = {}
        ep = self.epoch

        def add(d):
            for k, (s, v, e) in d.items():
                if e < ep:
                    continue
                if k not in deps or deps[k][1] < v:
                    deps[k] = (s, v)
        for b in reads:
            if b.track:
                add(b.w)
        for b in writes:
            if b.track:
                add(b.w)
                add(b.r)
        return deps

    def _do_waits(self, eng, deps):
        wd = self.waited[eng]
        e = self.eobj[eng]
        for k, (s, v) in deps.items():
            if eng == "pe" and k == "pe":
                continue
            if wd.get(k, 0) >= v:
                continue
            wd[k] = v
            e.wait_ge(s, v)

    def op(self, eng, fn, reads=(), writes=()):
        self._do_waits(eng, self._deps(reads, writes))
        self.cnt[eng] += 1
        tok = (self.sem[eng], self.cnt[eng], self.epoch)
        fn(self.eobj[eng]).then_inc(tok[0], 1)
        self.nops += 1
        self.active[eng] = True
        for b in writes:
            b.w = {eng: tok}
            b.r = {}
        for b in reads:
            b.r[eng] = tok

    def dma(self, qn, out_b, out_ap, in_b, in_ap, **kw):
        self._do_waits(qn, self._deps([in_b], [out_b]))
        i = self.dcnt[qn]
        self.dcnt[qn] += 1
        s = self.dsem[qn][i % NDS]
        prev = 16 * (i // NDS)
        key = "d_%s%d" % (qn, i % NDS)
        e = self.eobj[qn]
        if prev > 0 and self.waited[qn].get(key, 0) < prev:
            self.waited[qn][key] = prev
            e.wait_ge(s, prev)
        e.dma_start(out=out_ap, in_=in_ap, **kw).then_inc(s, 16)
        self.nops += 1
        self.active[qn] = True
        tok = (s, prev + 16, self.epoch)
        if out_b.track:
            out_b.w = {key: tok}
            out_b.r = {}
        if in_b.track:
            in_b.r[key] = tok

    def _all_tokens(self):
        toks = {}
        for n in self.eobj:
            if self.cnt[n] > 0:
                toks[n] = (self.sem[n], self.cnt[n])
        for qn in self.dsem:
            for j in range(NDS):
                uses = (self.dcnt[qn] - j + NDS - 1) // NDS if self.dcnt[qn] > j else 0
                if uses > 0:
                    toks["d_%s%d" % (qn, j)] = (self.dsem[qn][j], 16 * uses)
        return toks

    def barrier(self):
        toks = self._all_tokens()
        rotate = all(self.active.get(n, False) for n in self.eobj)
        nxt2 = self.semsets[(self.epoch + 2) % 3]
        for n in self.eobj:
            wd = self.waited[n]
            e = self.eobj[n]
            for k, (s, v) in toks.items():
                if k == n or wd.get(k, 0) >= v:
                    continue
                wd[k] = v
                e.wait_ge(s, v)
            if rotate:
                e.sem_clear(nxt2[n])
        if not rotate:
            return
        self.epoch += 1
        self.active = {}
        self.sem = self.semsets[self.epoch % 3]
        for n in self.eobj:
            self.tot[n] += self.cnt[n]
            self.cnt[n] = 0
            for k in list(self.waited[n].keys()):
                if k in self.eobj:
                    del self.waited[n][k]


from contextlib import ExitStack


def phase_norm_T(kb, G, xsrc, t0, gsrc, hT):
    with ExitStack() as c:
        gbc = kb.sb(c, "gbc", [128, D_MODEL], F32)
        xt = [kb.sb(c, "xt%d" % i, [128, D_MODEL], F32) for i in range(2)]
        junk = kb.sb(c, "junk", [128, D_MODEL], BF16)
        xn = [kb.sb(c, "xn%d" % i, [128, D_MODEL], BF16) for i in range(2)]
        ss = [kb.sb(c, "ss%d" % i, [128, 4], F32) for i in range(2)]
        tp = [kb.ps(c, "tp%d" % i, [128, 8, 128], BF16) for i in range(2)]
        kb.dma("sp", gbc, gbc[:, :], gsrc, gsrc.t.partition_broadcast(128))
        ntp = 0
        for ti in range(8):
            x_b, n_b, s_b = xt[ti % 2], xn[ti % 2], ss[ti % 2]
            kb.dma("sp", x_b, x_b[:, :], xsrc, xsrc.t[t0 + ti * 128: t0 + (ti + 1) * 128, :])
            kb.op("act", lambda e: e.activation(out=junk[:, :], in_=x_b[:, :], func=AF.Square,
                                                accum_out=s_b[:, 0:1]), reads=[x_b], writes=[junk, s_b])
            kb.op("dve", lambda e: e.tensor_scalar(out=s_b[:, 1:2], in0=s_b[:, 0:1], scalar1=1.0 / D_MODEL,
                                                   scalar2=EPS, op0=ALU.mult, op1=ALU.add),
                  reads=[s_b], writes=[s_b])
            kb.op("act", lambda e: e.activation(out=s_b[:, 2:3], in_=s_b[:, 1:2], func=AF.Sqrt),
                  reads=[s_b], writes=[s_b])
            kb.op("dve", lambda e: e.reciprocal(out=s_b[:, 3:4], in_=s_b[:, 2:3]), reads=[s_b], writes=[s_b])
            kb.op("dve", lambda e: e.scalar_tensor_tensor(out=n_b[:, :], in0=x_b[:, :], scalar=s_b[:, 3:4],
                                                          in1=gbc[:, :], op0=ALU.mult, op1=ALU.mult),
                  reads=[x_b, s_b, gbc], writes=[n_b])
            for k0 in range(0, 32, 8):
                p_b = tp[ntp % 2]
                ntp += 1
                for kk in range(8):
                    k = k0 + kk
                    kb.op("pe", lambda e: e.transpose(out=p_b[:, kk, :], in_=n_b[:, k * 128:(k + 1) * 128],
                                                      identity=G["ident"][:, :]),
                          reads=[n_b, G["ident"]], writes=[p_b])
                if (k0 // 8) % 2 == 0:
                    kb.op("act", lambda e: e.activation(out=hT[:, k0:k0 + 8, ti * 128:(ti + 1) * 128],
                                                        in_=p_b[:, :, :], func=AF.Copy),
                          reads=[p_b], writes=[hT])
                else:
                    kb.op("dve", lambda e: e.tensor_copy(out=hT[:, k0:k0 + 8, ti * 128:(ti + 1) * 128],
                                                         in_=p_b[:, :, :]), reads=[p_b], writes=[hT])
    kb.barrier()


def stream_proj(kb, wsrc, KT, jobs, wbufs, cw=256):
    chunks = []
    cur = None
    for (c0, c1, fn) in jobs:
        if cur is not None and cur[1] == c0 and (c1 - cur[0]) <= cw:
            cur[1] = c1
            cur[2].append((c0, c1, fn))
        else:
            cur = [c0, c1, [(c0, c1, fn)]]
            chunks.append(cur)
    wv = wsrc.t.rearrange("(k p) n -> p k n", p=128)
    for ci, (a, b, fl) in enumerate(chunks):
        wb = wbufs[ci % len(wbufs)]
        kb.dma("pool", wb, wb[:, 0:KT, 0:b - a], wsrc, wv[:, :, a:b])
        for (c0, c1, fn) in fl:
            fn(wb, c0 - a)


DILS = (1, 4, 16)
L0_QKV0 = 2048
L0_ZB0 = 11264
ATT_SCALE = 128 ** -0.5


def rope_evac(kb, G, src, s_out_view_fn, tok0, xr_b, psw, rt_b, D, prow=32):
    kb.op("act", lambda e: e.activation(out=xr_b[:, :], in_=src[0], func=AF.Copy),
          reads=[src[1]], writes=[xr_b])
    kb.op("pe", lambda e: e.matmul(psw[:, :], lhsT=G["perm32"][:, :], rhs=xr_b[:, :], start=True, stop=True),
          reads=[xr_b, G["perm32"]], writes=[psw])
    kb.op("dve", lambda e: e.tensor_tensor(out=rt_b[:, 0, :], in0=xr_b[:, :],
                                           in1=G["ropeC"][:, tok0:tok0 + 512], op=ALU.mult),
          reads=[xr_b, G["ropeC"]], writes=[rt_b])
    kb.op("dve", lambda e: e.tensor_tensor(out=rt_b[:, 1, :], in0=psw[:, :],
                                           in1=G["ropeS"][:, tok0:tok0 + 512], op=ALU.mult),
          reads=[psw, G["ropeS"], rt_b], writes=[rt_b])
    ov, ob = s_out_view_fn(0, 32)
    kb.op("dve", lambda e: e.tensor_tensor(
        out=ov, in0=rt_b.t[:, 0, :].rearrange("p (m r) -> p r m", r=D),
        in1=rt_b.t[:, 1, :].rearrange("p (m r) -> p r m", r=D), op=ALU.add),
        reads=[rt_b], writes=[ob])


def layer0_inproj(kb, G, half):
    t0 = half * HALF
    with ExitStack() as c:
        hT = kb.sb(c, "hT", [128, 32, HALF], BF16)
        phase_norm_T(kb, G, G["xin"], t0, G["even_norm"], hT)
        wbufs = [kb.sb(c, "wb%d" % i, [128, 32, 256], BF16) for i in range(3)]
        pacc = [kb.ps(c, "pacc%d" % i, [128, 512], F32) for i in range(4)]
        psw = [kb.ps(c, "psw%d" % i, [32, 512], F32) for i in range(2)]
        stg = [kb.sb(c, "stg%d" % i, [128, HALF], BF16) for i in range(3)]
        vst = [kb.sb(c, "vst%d" % i, [128, 256], BF16) for i in range(3)]
        xr = [kb.sb(c, "xr%d" % i, [32, 512], F32) for i in range(2)]
        rt = [kb.sb(c, "rt%d" % i, [32, 2, 512], F32) for i in range(2)]
        st = {"np": 0, "ns": 0, "nr": 0, "nv": 0}

        def mm_cm(wb, off, ps, tc):
            for k in range(32):
                kb.op("pe", lambda e: e.matmul(ps[:, :], lhsT=wb[:, k, off:off + 128],
                                               rhs=hT[:, k, tc * 512:(tc + 1) * 512],
                                               start=(k == 0), stop=(k == 31)),
                      reads=[wb, hT], writes=[ps])

        def cm_simple(c0, func, dst, ct):
            def fn(wb, off):
                s = stg[st["ns"] % 3]
                st["ns"] += 1
                for tc in range(2):
                    ps = pacc[st["np"] % 4]
                    st["np"] += 1
                    mm_cm(wb, off, ps, tc)
                    kb.op("act", lambda e: e.activation(out=s[:, tc * 512:(tc + 1) * 512], in_=ps[:, :],
                                                        func=func), reads=[ps], writes=[s])
                kb.dma("sp", dst, dst.t[ct, :, t0:t0 + HALF], s, s[:, :])
            return (c0, c0 + 128, fn)

        def cm_rope(c0, dst, g, hd):
            D = DILS[g]
            nl = HALF // D

            def fn(wb, off):
                s = stg[st["ns"] % 3]
                st["ns"] += 1
                sv = s.t[:, :].rearrange("p (r m) -> p r m", r=D)
                for tc in range(2):
                    ps = pacc[st["np"] % 4]
                    st["np"] += 1
                    mm_cm(wb, off, ps, tc)
                    m0 = tc * 512 // D
                    for (lo, hi) in ((32, 64), (64, 128)):
                        kb.op("act", lambda e: e.activation(
                            out=sv[lo:hi, :, m0:m0 + 512 // D],
                            in_=ps.t[lo:hi, :].rearrange("p (m r) -> p r m", r=D), func=AF.Copy),
                            reads=[ps], writes=[s])
                    i = st["nr"] % 2
                    st["nr"] += 1
                    rope_evac(kb, G, (ps[0:32, :], ps), lambda lo, hi: (sv[lo:hi, :, m0:m0 + 512 // D], s),
                              t0 + tc * 512, xr[i], psw[i], rt[i], D)
                dv = dst.t[g, hd, :, :].rearrange("p (r m) -> p r m", r=D)
                kb.dma("sp", dst, dv[:, :, half * nl:(half + 1) * nl], s, sv)
            return (c0, c0 + 128, fn)

        def tm_v(c0, g, cc):
            D = DILS[g]
            nl = HALF // D

            def fn(wb, off):
                units = []
                for vt in range(8):
                    if D == 1:
                        units.append((lambda k, vt=vt: hT[:, k, vt * 128:(vt + 1) * 128], 128, t0 + vt * 128))
                    elif D == 4:
                        r, ml0 = vt // 2, 128 * (vt % 2)
                        units.append((lambda k, r=r, ml0=ml0: hT.t[:, k, :].rearrange(
                            "p (m r) -> p r m", r=4)[:, r, ml0:ml0 + 128], 128, r * 512 + half * 256 + ml0))
                    else:
                        for a in range(2):
                            r = 2 * vt + a
                            units.append((lambda k, r=r: hT.t[:, k, :].rearrange(
                                "p (m r) -> p r m", r=16)[:, r, :], 64, r * 128 + half * 64))
                for ui, (ltf, nr, row) in enumerate(units):
                    ps = pacc[st["np"] % 4]
                    st["np"] += 1
                    for k in range(32):
                        kb.op("pe", lambda e: e.matmul(ps[0:nr, 0:256], lhsT=ltf(k), rhs=wb[:, k, off:off + 256],
                                                       start=(k == 0), stop=(k == 31)),
                              reads=[wb, hT], writes=[ps])
                    vs = vst[st["nv"] % 3]
                    st["nv"] += 1
                    if ui % 2 == 0:
                        kb.op("act", lambda e: e.activation(out=vs[0:nr, :], in_=ps[0:nr, 0:256], func=AF.Copy),
                              reads=[ps], writes=[vs])
                    else:
                        kb.op("dve", lambda e: e.tensor_copy(out=vs[0:nr, :], in_=ps[0:nr, 0:256]),
                              reads=[ps], writes=[vs])
                    dv = G["v"].t[g, :, cc * 256:(cc + 1) * 256]
                    kb.dma("sp", G["v"], dv[row:row + nr, :], vs, vs[0:nr, :])
            return (c0, c0 + 256, fn)

        jobs = []
        for ct in range(8):
            jobs.append(cm_simple(ct * 128, AF.Copy, G["uT"], ct))
        for ct in range(8):
            jobs.append(cm_simple(1024 + ct * 128, AF.Silu, G["zaT"], ct))
        for g in range(3):
            base = L0_QKV0 + g * 3072
            for hd in range(8):
                jobs.append(cm_rope(base + hd * 128, G["qT"], g, hd))
            for hd in range(8):
                jobs.append(cm_rope(base + 1024 + hd * 128, G["kT"], g, hd))
            for cc in range(4):
                jobs.append(tm_v(base + 2048 + cc * 256, g, cc))
        for ct in range(8):
            jobs.append(cm_simple(L0_ZB0 + ct * 128, AF.Silu, G["zbT"], ct))
        stream_proj(kb, G["even_w_in"], 32, jobs, wbufs)
    kb.barrier()


IN_SPECS = [
    ("xin", [SEQ, D_MODEL]),
    ("even_norm", [D_MODEL]), ("even_w_in", [D_MODEL, 12288]),
    ("s5_lambda_re", [64, 64]), ("s5_lambda_im", [64, 64]), ("s5_log_step", [64]),
    ("s5_b_re", [64, 64, 16]), ("s5_b_im", [64, 64, 16]), ("s5_c_re", [64, 16, 64]), ("s5_c_im", [64, 16, 64]),
    ("s5_d", [1024]), ("s5_glu_w", [1024, 1024]), ("s5_glu_b", [1024]), ("even_w_out", [2048, D_MODEL]),
    ("odd_norm", [D_MODEL]), ("odd_w_in", [D_MODEL, 6336]), ("mla_q_norm", [1536]), ("mla_kv_norm", [512]),
    ("idx_k_norm", [128]), ("mla_w_uq", [1536, 4096]), ("mla_w_uk", [512, 32, 96]), ("mla_w_uv", [512, 32, 128]),
    ("idx_w_q", [1536, 4096]), ("odd_w_out", [4096, D_MODEL]), ("final_norm", [D_MODEL]),
    ("c_ident", [128, 128]), ("c_perm32", [32, 32]), ("c_ropeC", [32, SEQ]), ("c_ropeS", [32, SEQ]),
    ("c_maskT", [128, 2, 128]), ("c_rowmask4", [128, 4]), ("c_negmask", [128, 128]),
]


def host_consts():
    half = 16
    inv = (500000.0 ** (-np.arange(0, 32, 2, dtype=np.float32) / np.float32(32))).astype(np.float32)
    ang = np.arange(SEQ, dtype=np.float32)[None, :] * inv[:, None]
    cos = np.cos(ang).astype(np.float32)
    sin = np.sin(ang).astype(np.float32)
    ropeC = np.concatenate([cos, cos], 0)
    ropeS = np.concatenate([-sin, sin], 0)
    perm = np.zeros((32, 32), np.float32)
    for m in range(32):
        perm[(m + 16) % 32, m] = 1.0
    k = np.arange(128)[:, None]
    q = np.arange(128)[None, :]
    maskT = np.stack([(k >= q), (k <= q)], 1).astype(np.float32)
    rowmask4 = (np.arange(128)[:, None] // 32 == np.arange(4)[None, :]).astype(np.float32)
    negmask = np.where(q.T >= k.T, 0.0, -1.0e30).astype(np.float32)
    return {"c_ident": np.eye(128, dtype=np.float32), "c_perm32": perm, "c_ropeC": ropeC, "c_ropeS": ropeS,
            "c_maskT": maskT, "c_rowmask4": rowmask4, "c_negmask": negmask}


def build_program(stop_after=None, debug_out=(), dbg=None, only=None):
    nc = bass.Bass("TRN2", target_bir_lowering=False)
    with ExitStack() as stack:
        kb = KB(nc, stack)
        G = {"_dbg": dbg, "_eng": "dve"}
        for name, shape in IN_SPECS:
            b = Buf(name, nc.dram_tensor(name, list(shape), F32, kind="ExternalInput").ap())
            b.track = False
            G[name] = b

        def scratch(name, shape, dt):
            kind = "ExternalOutput" if name in debug_out else "Internal"
            G[name] = kb.dram(name, shape, dt, kind=kind)

        scratch("uT", [8, 128, SEQ], BF16)
        scratch("zaT", [8, 128, SEQ], BF16)
        scratch("qT", [3, 8, 128, SEQ], BF16)
        scratch("kT", [3, 8, 128, SEQ], BF16)
        scratch("v", [3, SEQ, 1024], BF16)
        scratch("zbT", [8, 128, SEQ], BF16)
        scratch("mixT", [16, 128, SEQ], BF16)
        scratch("x1", [SEQ, D_MODEL], F32)
        scratch("cqT", [12, 128, SEQ], BF16)
        scratch("ckvT", [4, 128, SEQ], BF16)
        scratch("krT", [32, SEQ], BF16)
        scratch("kiT", [128, SEQ], BF16)
        scratch("wi", [SEQ, 32], F32)
        scratch("sgT", [32, 128, SEQ], BF16)
        scratch("q1T", [32, 128, SEQ], BF16)
        scratch("qiT", [32, 128, SEQ], BF16)
        scratch("selT", [16, 128, SEQ], BF16)
        scratch("ogT", [32, 128, SEQ], BF16)
        scratch("x2", [SEQ, D_MODEL], F32)
        G["out"] = kb.dram("out", [SEQ, D_MODEL], F32, kind="ExternalOutput")

        G["ident"] = kb.sb(stack, "ident", [128, 128], BF16)
        G["identf"] = kb.sb(stack, "identf", [128, 128], F32)
        G["perm32"] = kb.sb(stack, "perm32", [32, 32], F32)
        G["ropeC"] = kb.sb(stack, "ropeC", [32, SEQ], F32)
        G["ropeS"] = kb.sb(stack, "ropeS", [32, SEQ], F32)
        G["maskT"] = kb.sb(stack, "maskT", [128, 2, 128], BF16)
        G["rowmask4"] = kb.sb(stack, "rowmask4", [128, 4], F32)
        G["onesb"] = kb.sb(stack, "onesb", [128, 128], BF16)
        G["onesf"] = kb.sb(stack, "onesf", [128, 128], F32)
        kb.dma("pool", G["ident"], G["ident"][:, :], G["c_ident"], G["c_ident"].t[:, :])
        kb.dma("sp", G["identf"], G["identf"][:, :], G["c_ident"], G["c_ident"].t[:, :])
        kb.dma("sp", G["perm32"], G["perm32"][:, :], G["c_perm32"], G["c_perm32"].t[:, :])
        kb.dma("sp", G["ropeC"], G["ropeC"][:, :], G["c_ropeC"], G["c_ropeC"].t[:, :])
        kb.dma("sp", G["ropeS"], G["ropeS"][:, :], G["c_ropeS"], G["c_ropeS"].t[:, :])
        kb.dma("pool", G["maskT"], G["maskT"][:, :, :], G["c_maskT"], G["c_maskT"].t[:, :, :])
        kb.dma("sp", G["rowmask4"], G["rowmask4"][:, :], G["c_rowmask4"], G["c_rowmask4"].t[:, :])
        kb.op("pool", lambda e: e.memset(G["onesb"][:, :], 1.0), writes=[G["onesb"]])
        kb.op("pool", lambda e: e.memset(G["onesf"][:, :], 1.0), writes=[G["onesf"]])

        phases = [
            ("l0_in0", lambda: layer0_inproj(kb, G, 0)),
            ("l0_in1", lambda: layer0_inproj(kb, G, 1)),
            ("l0_s5", lambda: layer0_s5(kb, G)),
            ("l0_attn", lambda: layer0_attn(kb, G)),
            ("l0_out0", lambda: outproj(kb, G, 0, "mixT", 16, "even_w_out", "xin", "x1")),
            ("l0_out1", lambda: outproj(kb, G, 1, "mixT", 16, "even_w_out", "xin", "x1")),
            ("l1_in0", lambda: layer1_inproj(kb, G, 0)),
            ("l1_in1", lambda: layer1_inproj(kb, G, 1)),
            ("l1_q0", lambda: layer1_qproj(kb, G, 0)),
            ("l1_q1", lambda: layer1_qproj(kb, G, 1)),
            ("l1_idx", lambda: layer1_index(kb, G)),
            ("l1_attn", lambda: layer1_attn(kb, G)),
            ("l1_out0", lambda: outproj(kb, G, 0, "ogT", 32, "odd_w_out", "x1", "x2")),
            ("l1_out1", lambda: outproj(kb, G, 1, "ogT", 32, "odd_w_out", "x1", "x2")),
            ("final", lambda: final_norm(kb, G)),
        ]
        for name, fn in phases:
            if only is None or name in only:
                fn()
            if stop_after == name:
                break
        kb.barrier()
    return nc, kb


def make_in_maps(inputs, cores):
    consts = host_consts()
    shared = {}
    for name, shape in IN_SPECS:
        if name == "xin" or name.startswith("c_"):
            continue
        a = np.asarray(inputs[name], dtype=np.float32)
        if name != "final_norm":
            a = a[0]
        shared[name] = np.ascontiguousarray(a).reshape(shape)
    x = np.asarray(inputs["x"], dtype=np.float32)
    maps = []
    for c in cores:
        m = dict(shared)
        m.update(consts)
        m["xin"] = np.ascontiguousarray(x[c % BATCH])
        maps.append(m)
    return maps


def kernel(**inputs):
    nc, kb = build_program()
    cores = list(range(8))
    res = run_bass_kernel_spmd(nc, make_in_maps(inputs, cores), core_ids=cores)
    out = np.stack([np.asarray(res.results[b]["out"], dtype=np.float32) for b in range(BATCH)], 0)
    return out


def TT(kb, eng, out, a, b, op):
    kb.op(eng, lambda e: e.tensor_tensor(out=out[0], in0=a[0], in1=b[0], op=op),
          reads=[a[1], b[1]], writes=[out[1]])


def TS(kb, eng, out, a, s1, op0, s2=None, op1=None):
    rd = [a[1]]
    s1v, s2v = s1, s2
    if isinstance(s1, tuple):
        rd.append(s1[1])
        s1v = s1[0]
    if isinstance(s2, tuple):
        rd.append(s2[1])
        s2v = s2[0]
    if op1 is None:
        kb.op(eng, lambda e: e.tensor_scalar(out=out[0], in0=a[0], scalar1=s1v, scalar2=None, op0=op0),
              reads=rd, writes=[out[1]])
    else:
        kb.op(eng, lambda e: e.tensor_scalar(out=out[0], in0=a[0], scalar1=s1v, scalar2=s2v, op0=op0, op1=op1),
              reads=rd, writes=[out[1]])


def ACTF(kb, out, a, func, scale=None, bias=None):
    rd = [a[1]]
    kw = {}
    if scale is not None:
        if isinstance(scale, tuple):
            rd.append(scale[1])
            kw["scale"] = scale[0]
        else:
            kw["scale"] = scale
    if bias is not None:
        rd.append(bias[1])
        kw["bias"] = bias[0]
    kb.op("act", lambda e: e.activation(out=out[0], in_=a[0], func=func, **kw), reads=rd, writes=[out[1]])


def W(b, ap=None):
    if ap is None:
        nd = len(b.t.shape)
        ap = b.t[tuple(slice(None) for _ in range(nd))]
    return (ap, b)


def layer0_s5(kb, G):
    TWO_PI = 2.0 * math.pi
    with ExitStack() as c:
        def T(name, shape=(128, 32), dt=F32):
            return kb.sb(c, name, list(shape), dt)
        EBre = T("EBre", [128, 32, 128], BF16)
        EBim = T("EBim", [128, 32, 128], BF16)
        ECre = T("ECre", [128, 32, 128], BF16)
        ECim = T("ECim", [128, 32, 128], BF16)
        Ec = T("Ecos", [128, 32, 128])
        Es = T("Esin", [128, 32, 128])
        mag = T("mag")
        dsk = T("dsk", [128, 8])
        glb = T("glb", [128, 8])
        gw = T("gw", [128, 8, 1024], BF16)
        carry = [T("carry%d" % i, [128, 4, 2]) for i in range(8)]
        kb.dma("pool", gw, gw[:, :, :], G["s5_glu_w"], G["s5_glu_w"].t.rearrange("(k p) n -> p k n", p=128))
        for cb_ in carry:
            kb.op("pool", lambda e: e.memset(cb_[:, :, :], 0.0), writes=[cb_])

        with ExitStack() as c2:
            def T2(name, shape=(128, 32), dt=F32):
                return kb.sb(c2, name, list(shape), dt)
            lamR, lamI, dtl = T2("lamR"), T2("lamI"), T2("dtl")
            PT = T2("PT", [32, 3, 128])
            LS = T2("LS", [32, 2])
            PD = T2("PD", [8, 2, 128])
            kb.dma("sp", PT, PT[:, 0, :], G["s5_lambda_re"],
                   G["s5_lambda_re"].t.rearrange("(pr g2) p -> pr (g2 p)", g2=2))
            kb.dma("sp", PT, PT[:, 1, :], G["s5_lambda_im"],
                   G["s5_lambda_im"].t.rearrange("(pr g2) p -> pr (g2 p)", g2=2))
            kb.dma("sp", LS, LS[:, :], G["s5_log_step"], G["s5_log_step"].t.rearrange("(pr g2) -> pr g2", g2=2))
            kb.dma("sp", PD, PD[:, 0, :], G["s5_d"], G["s5_d"].t.rearrange("(ct c) -> ct c", c=128))
            kb.dma("sp", PD, PD[:, 1, :], G["s5_glu_b"], G["s5_glu_b"].t.rearrange("(ct c) -> ct c", c=128))
            kb.op("dve", lambda e: e.tensor_copy(
                out=PT.t[:, 2, :].rearrange("pr (g2 p) -> pr g2 p", g2=2),
                in_=LS.t[:, :].unsqueeze(2).broadcast_to([32, 2, 64])), reads=[LS, PT], writes=[PT])
            ppt = kb.ps(c2, "ppt", [128, 3, 32], F32)
            ppd = kb.ps(c2, "ppd", [128, 2, 8], F32)
            for j in range(3):
                kb.op("pe", lambda e: e.transpose(out=ppt[:, j, :], in_=PT[:, j, :], identity=G["identf"][0:32, 0:32]),
                      reads=[PT, G["identf"]], writes=[ppt])
            for j in range(2):
                kb.op("pe", lambda e: e.transpose(out=ppd[:, j, :], in_=PD[:, j, :], identity=G["identf"][0:8, 0:8]),
                      reads=[PD, G["identf"]], writes=[ppd])
            for j, dst in enumerate((lamR, lamI, dtl)):
                kb.op("dve", lambda e: e.tensor_copy(out=dst[:, :], in_=ppt[:, j, :]), reads=[ppt], writes=[dst])
            for j, dst in enumerate((dsk, glb)):
                kb.op("dve", lambda e: e.tensor_copy(out=dst[:, :], in_=ppd[:, j, :]), reads=[ppd], writes=[dst])
            lr, dt_, xx, pp, ang, ff, kf, th, ath = (T2(n) for n in
                                                     ("lr", "dt", "xx", "pp", "ang", "ff", "kf", "th", "ath"))
            ki = T2("ki", dt=mybir.dt.int32)
            cth, sth, ar, ai, den, rden, nr, t1, t2, cr, ci = (T2(n) for n in (
                "cth", "sth", "ar", "ai", "den", "rden", "nr", "t1", "t2", "cr", "ci"))
            TS(kb, "dve", W(lr), W(lamR), -1e-4, ALU.min)
            TS(kb, "dve", W(xx), W(dtl), 0.125, ALU.mult)
            TS(kb, "dve", W(pp), W(xx), 1.0 / 12.0, ALU.mult, 1.0, ALU.add)
            for kk in range(11, 0, -1):
                TT(kb, "dve", W(pp), W(pp), W(xx), ALU.mult)
                TS(kb, "dve", W(pp), W(pp), 1.0 / kk, ALU.mult, 1.0, ALU.add)
            TT(kb, "dve", W(dt_), W(pp), W(pp), ALU.mult)
            TT(kb, "dve", W(dt_), W(dt_), W(dt_), ALU.mult)
            TT(kb, "dve", W(dt_), W(dt_), W(dt_), ALU.mult)
            TT(kb, "dve", W(xx), W(lr), W(dt_), ALU.mult)
            TS(kb, "dve", W(pp), W(xx), 1.0 / 7.0, ALU.mult, 1.0, ALU.add)
            for kk in (6, 5, 4, 3, 2, 1):
                TT(kb, "dve", W(pp), W(pp), W(xx), ALU.mult)
                TS(kb, "dve", W(pp), W(pp), 1.0 / kk, ALU.mult, 1.0, ALU.add)
            kb.op("dve", lambda e: e.tensor_copy(out=mag[:, :], in_=pp[:, :]), reads=[pp], writes=[mag])
            TT(kb, "dve", W(ang), W(lamI), W(dt_), ALU.mult)
            TS(kb, "dve", W(ff), W(ang), 1.0 / TWO_PI, ALU.mult)
            kb.op("dve", lambda e: e.tensor_copy(out=ki[:, :], in_=ff[:, :]), reads=[ff], writes=[ki])
            kb.op("dve", lambda e: e.tensor_copy(out=kf[:, :], in_=ki[:, :]), reads=[ki], writes=[kf])
            TT(kb, "dve", W(ff), W(ff), W(kf), ALU.subtract)
            TS(kb, "dve", W(th), W(ff), TWO_PI, ALU.mult, math.pi, ALU.min)
            TS(kb, "dve", W(th), W(th), -math.pi, ALU.max)
            ACTF(kb, W(sth), W(th), AF.Sin)
            ACTF(kb, W(ath), W(th), AF.Abs)
            TS(kb, "dve", W(ath), W(ath), -1.0, ALU.mult, math.pi / 2.0, ALU.add)
            ACTF(kb, W(cth), W(ath), AF.Sin)
            TT(kb, "dve", W(ar), W(mag), W(cth), ALU.mult)
            TT(kb, "dve", W(ai), W(mag), W(sth), ALU.mult)
            TT(kb, "dve", W(den), W(lr), W(lr), ALU.mult)
            TT(kb, "dve", W(t1), W(lamI), W(lamI), ALU.mult)
            TT(kb, "dve", W(den), W(den), W(t1), ALU.add)
            kb.op("dve", lambda e: e.reciprocal(out=rden[:, :], in_=den[:, :]), reads=[den], writes=[rden])
            TS(kb, "dve", W(nr), W(ar), -1.0, ALU.add)
            TT(kb, "dve", W(t1), W(nr), W(lr), ALU.mult)
            TT(kb, "dve", W(t2), W(ai), W(lamI), ALU.mult)
            TT(kb, "dve", W(t1), W(t1), W(t2), ALU.add)
            TT(kb, "dve", W(cr), W(t1), W(rden), ALU.mult)
            TT(kb, "dve", W(t1), W(ai), W(lr), ALU.mult)
            TT(kb, "dve", W(t2), W(nr), W(lamI), ALU.mult)
            TT(kb, "dve", W(t1), W(t1), W(t2), ALU.subtract)
            TT(kb, "dve", W(ci), W(t1), W(rden), ALU.mult)
            bre, bim = T2("bre", [128, 32, 16]), T2("bim", [128, 32, 16])
            for (dst, nm) in ((bre, "s5_b_re"), (bim, "s5_b_im")):
                kb.dma("sp", dst, dst[:, :, :], G[nm], G[nm].t.rearrange("(pr g2) p j -> (g2 p) pr j", g2=2))
            u1, u2 = T2("u1", [128, 32, 16]), T2("u2", [128, 32, 16])
            X2r, X2i = T2("X2r", [128, 32, 32], BF16), T2("X2i", [128, 32, 32], BF16)
            kb.op("pool", lambda e: e.memset(X2r[:, :, :], 0.0), writes=[X2r])
            kb.op("pool", lambda e: e.memset(X2i[:, :, :], 0.0), writes=[X2i])
            crb = (cr.t[:, :].unsqueeze(2).broadcast_to([128, 32, 16]), cr)
            cib = (ci.t[:, :].unsqueeze(2).broadcast_to([128, 32, 16]), ci)
            for (X2, pa, pb, opc) in ((X2r, bre, bim, ALU.subtract), (X2i, bim, bre, ALU.add)):
                TT(kb, "dve", W(u1), W(pa), crb, ALU.mult)
                TT(kb, "dve", W(u2), W(pb), cib, ALU.mult)
                for g2 in range(2):
                    lo, hi = 64 * g2, 64 * g2 + 64
                    TT(kb, "dve", (X2[lo:hi, :, 16 * g2:16 * g2 + 16], X2), (u1[lo:hi, :, :], u1),
                       (u2[lo:hi, :, :], u2), opc)
            ptp = [kb.ps(c2, "s5tp%d" % i, [128, 128], BF16) for i in range(2)]
            ntp = 0
            for (X2, EB) in ((X2r, EBre), (X2i, EBim)):
                for ct in range(8):
                    p_b = ptp[ntp % 2]
                    ntp += 1
                    kb.op("pe", lambda e: e.transpose(out=p_b[:, :], in_=X2.t.rearrange("q pr x -> q (pr x)")[:, ct * 128:(ct + 1) * 128],
                                                      identity=G["ident"][:, :]),
                          reads=[X2, G["ident"]], writes=[p_b])
                    for a in range(4):
                        TS(kb, "dve", (EB[:, 4 * ct + a, :], EB), W(p_b),
                           (G["rowmask4"][:, a:a + 1], G["rowmask4"]), ALU.mult)
            Cn = T2("Cn", [128, 8, 2, 64])
            Cb = T2("Cb", [128, 8, 128], BF16)
            for (nm, EC, sc) in (("s5_c_re", ECre, 1.0), ("s5_c_im", ECim, -1.0)):
                srcv = G[nm].t.rearrange("g j p -> (g j) p").rearrange("(ct c) p -> c ct p", c=128)
                for dup in range(2):
                    kb.dma("sp", Cn, Cn[:, :, dup, :], G[nm], srcv)
                kb.op("act", lambda e: e.activation(out=Cb[:, :, :], in_=Cn.t.rearrange("c ct d p -> c ct (d p)"),
                                                    func=AF.Copy, scale=sc), reads=[Cn], writes=[Cb])
                kb.op("pool", lambda e: e.memset(EC[:, :, :], 0.0), writes=[EC])
                for ct in range(8):
                    p_b = ptp[ntp % 2]
                    ntp += 1
                    kb.op("pe", lambda e: e.transpose(out=p_b[:, :], in_=Cb[:, ct, :], identity=G["ident"][:, :]),
                          reads=[Cb, G["ident"]], writes=[p_b])
                    for a in range(4):
                        for g2 in range(2):
                            lo, hi = 64 * g2, 64 * g2 + 64
                            c0 = 32 * a + 16 * g2
                            kb.op("act" if g2 == 0 else "dve",
                                  (lambda e: e.activation(out=EC[lo:hi, 4 * ct + a, c0:c0 + 16],
                                                          in_=p_b[lo:hi, c0:c0 + 16], func=AF.Copy)) if g2 == 0 else
                                  (lambda e: e.tensor_copy(out=EC[lo:hi, 4 * ct + a, c0:c0 + 16],
                                                           in_=p_b[lo:hi, c0:c0 + 16])),
                                  reads=[p_b], writes=[EC])
            wc, ws, wt = T2("wc"), T2("ws"), T2("wt")
            m1, m2 = T2("m1", [128, 32, 64]), T2("m2", [128, 32, 64])
            kb.op("dve", lambda e: e.tensor_copy(out=wc[:, :], in_=cth[:, :]), reads=[cth], writes=[wc])
            kb.op("dve", lambda e: e.tensor_copy(out=ws[:, :], in_=sth[:, :]), reads=[sth], writes=[ws])
            kb.op("dve", lambda e: e.tensor_copy(out=Ec[:, :, 0], in_=cth[:, :]), reads=[cth], writes=[Ec])
            kb.op("dve", lambda e: e.tensor_copy(out=Es[:, :, 0], in_=sth[:, :]), reads=[sth], writes=[Es])
            n = 1
            while n < 128:
                wcb = (wc.t[:, :].unsqueeze(2).broadcast_to([128, 32, n]), wc)
                wsb = (ws.t[:, :].unsqueeze(2).broadcast_to([128, 32, n]), ws)
                TT(kb, "dve", (m1[:, :, 0:n], m1), (Ec[:, :, 0:n], Ec), wcb, ALU.mult)
                TT(kb, "dve", (m2[:, :, 0:n], m2), (Es[:, :, 0:n], Es), wsb, ALU.mult)
                TT(kb, "dve", (Ec[:, :, n:2 * n], Ec), (m1[:, :, 0:n], m1), (m2[:, :, 0:n], m2), ALU.subtract)
                TT(kb, "dve", (m1[:, :, 0:n], m1), (Ec[:, :, 0:n], Ec), wsb, ALU.mult)
                TT(kb, "dve", (m2[:, :, 0:n], m2), (Es[:, :, 0:n], Es), wcb, ALU.mult)
                TT(kb, "dve", (Es[:, :, n:2 * n], Es), (m1[:, :, 0:n], m1), (m2[:, :, 0:n], m2), ALU.add)
                TT(kb, "dve", W(wt), W(wc), W(ws), ALU.mult)
                TT(kb, "dve", W(wc), W(wc), W(wc), ALU.mult)
                TT(kb, "dve", W(ws), W(ws), W(ws), ALU.mult)
                TT(kb, "dve", W(wc), W(wc), W(ws), ALU.subtract)
                TS(kb, "dve", W(ws), W(wt), 2.0, ALU.mult)
                n *= 2
        kb.barrier()
        if G.get("_dbg") == "s5prep":
            for nm, tl in (("EBre", EBre), ("EBim", EBim), ("ECre", ECre), ("ECim", ECim), ("Ecos", Ec), ("Esin", Es)):
                d = kb.dram("dbg_" + nm, list(tl.t.shape), tl.t.dtype, kind="ExternalOutput")
                kb.dma("sp", d, d.t[:, :, :], tl, tl[:, :, :])
            for nm, tl in (("mag", mag), ("dsk", dsk), ("glb", glb)):
                d = kb.dram("dbg_" + nm, list(tl.t.shape), tl.t.dtype, kind="ExternalOutput")
                kb.dma("sp", d, d.t[:, :], tl, tl[:, :])
            kb.barrier()
            return

        uTc = [T("uTc%d" % i, [128, 8, 512], BF16) for i in range(2)]
        zac = [T("zac%d" % i, [128, 8, 512], BF16) for i in range(2)]
        Yt = [T("Yt%d" % i, [128, 8, 512], BF16) for i in range(2)]
        psR = [kb.ps(c, "psR%d" % i, [128, 4, 128]) for i in range(2)]
        psI = [kb.ps(c, "psI%d" % i, [128, 4, 128]) for i in range(2)]
        psY = [kb.ps(c, "psY%d" % i, [128, 128]) for i in range(2)]
        psG = [kb.ps(c, "psG%d" % i, [128, 512]) for i in range(2)]
        NB = 2
        A = [[T("A%d_%d" % (j, i), [128, 4, 128]) for j in range(4)] for i in range(NB)]
        Rin = [[T("Rin%d_%d" % (j, i), [128, 4, 128]) for j in range(2)] for i in range(NB)]
        Rs = [[T("Rs%d_%d" % (j, i), [128, 4, 128]) for j in range(2)] for i in range(NB)]
        Bm = A
        S = [[T("S%d_%d" % (j, i), [128, 4, 128]) for j in range(2)] for i in range(NB)]
        Sb = [[T("Sb%d_%d" % (j, i), [128, 4, 128], BF16) for j in range(2)] for i in range(NB)]
        y1 = [T("y1_%d" % i, [128, 128]) for i in range(2)]
        sg = [T("sg%d" % i, [128, 512]) for i in range(2)]
        a1 = [T("a1_%d" % i, [128, 512]) for i in range(2)]
        ost = [T("ost%d" % i, [128, 512], BF16) for i in range(3)]
        it = 0
        ny = 0
        ng = 0
        for blk in range(4):
            ub, zb_, yb = uTc[blk % 2], zac[blk % 2], Yt[blk % 2]
            tsl = slice(blk * 512, (blk + 1) * 512)
            kb.dma("sp", ub, ub[:, :, :], G["uT"], G["uT"].t[:, :, tsl].rearrange("ct p t -> p ct t"))
            kb.dma("sp", zb_, zb_[:, :, :], G["zaT"], G["zaT"].t[:, :, tsl].rearrange("ct p t -> p ct t"))
            for ch in range(4):
                csl = slice(ch * 128, (ch + 1) * 128)
                for pg in range(8):
                    i = it % NB
                    it += 1
                    pR, pI = psR[i % 2], psI[i % 2]
                    Ec4, Es4 = (Ec[:, 4 * pg:4 * pg + 4, :], Ec), (Es[:, 4 * pg:4 * pg + 4, :], Es)
                    for a in range(4):
                        pr = 4 * pg + a
                        kb.op("pe", lambda e: e.matmul(pR[:, a, :], lhsT=EBre[:, pr, :], rhs=ub[:, pg, csl],
                                                       start=True, stop=True), reads=[EBre, ub], writes=[pR])
                    for a in range(4):
                        pr = 4 * pg + a
                        kb.op("pe", lambda e: e.matmul(pI[:, a, :], lhsT=EBim[:, pr, :], rhs=ub[:, pg, csl],
                                                       start=True, stop=True), reads=[EBim, ub], writes=[pI])
                    A0, A1, A2, A3 = A[i]
                    TT(kb, "dve", W(A0), W(pR), Ec4, ALU.mult)
                    TT(kb, "dve", W(A1), W(pI), Es4, ALU.mult)
                    TT(kb, "dve", W(A2), W(pI), Ec4, ALU.mult)
                    TT(kb, "dve", W(A3), W(pR), Es4, ALU.mult)
                    TT(kb, "pool", W(Rin[i][0]), W(A0), W(A1), ALU.add)
                    TT(kb, "pool", W(Rin[i][1]), W(A2), W(A3), ALU.subtract)
                    if G.get("_dbg") == "s5a":
                        continue
                    for a in range(4):
                        pr = 4 * pg + a
                        for j in range(2):
                            kb.op("dve", lambda e: e.tensor_tensor_scan(
                                out=Rs[i][j][:, a, :], data0=mag.t[:, pr:pr + 1].to_broadcast([128, 128]),
                                data1=Rin[i][j][:, a, :], initial=carry[pg][:, a, j:j + 1],
                                op0=ALU.mult, op1=ALU.add), reads=[mag, Rin[i][j], carry[pg]], writes=[Rs[i][j]])
                    if G.get("_dbg") == "s5b":
                        continue
                    B0, B1, B2, B3 = Bm[i]
                    TT(kb, "pool", W(B0), W(Rs[i][0]), Ec4, ALU.mult)
                    TT(kb, "pool", W(B1), W(Rs[i][1]), Es4, ALU.mult)
                    TT(kb, "pool", W(B2), W(Rs[i][0]), Es4, ALU.mult)
                    TT(kb, "pool", W(B3), W(Rs[i][1]), Ec4, ALU.mult)
                    TT(kb, "dve", W(S[i][0]), W(B0), W(B1), ALU.subtract)
                    TT(kb, "dve", W(S[i][1]), W(B2), W(B3), ALU.add)
                    for j in range(2):
                        kb.op("act", lambda e: e.activation(out=carry[pg][:, :, j], in_=S[i][j][:, :, 127],
                                                            func=AF.Copy), reads=[S[i][j]], writes=[carry[pg]])
                        kb.op("act", lambda e: e.activation(out=Sb[i][j][:, :, :], in_=S[i][j][:, :, :],
                                                            func=AF.Copy), reads=[S[i][j]], writes=[Sb[i][j]])
                    if G.get("_dbg") == "s5c":
                        continue
                    pY = psY[ny % 2]
                    yy = y1[ny % 2]
                    ny += 1
                    for a in range(4):
                        pr = 4 * pg + a
                        kb.op("pe", lambda e: e.matmul(pY[:, :], lhsT=ECre[:, pr, :], rhs=Sb[i][0][:, a, :],
                                                       start=(a == 0), stop=False), reads=[ECre, Sb[i][0]], writes=[pY])
                        kb.op("pe", lambda e: e.matmul(pY[:, :], lhsT=ECim[:, pr, :], rhs=Sb[i][1][:, a, :],
                                                       start=False, stop=(a == 3)), reads=[ECim, Sb[i][1]], writes=[pY])
                    kb.op("dve", lambda e: e.scalar_tensor_tensor(out=yy[:, :], in0=ub[:, pg, csl],
                                                                  scalar=dsk[:, pg:pg + 1], in1=pY[:, :],
                                                                  op0=ALU.mult, op1=ALU.add),
                          reads=[ub, dsk, pY], writes=[yy])
                    kb.op("act", lambda e: e.activation(out=yb[:, pg, csl], in_=yy[:, :], func=AF.Gelu_apprx_tanh),
                          reads=[yy], writes=[yb])
            if G.get("_dbg") in ("s5a", "s5b", "s5c", "s5d"):
                continue
            for co in range(8):
                pG = psG[ng % 2]
                sgb, a1b = sg[ng % 2], a1[ng % 2]
                ob = ost[ng % 3]
                ng += 1
                for ct in range(8):
                    kb.op("pe", lambda e: e.matmul(pG[:, :], lhsT=gw[:, ct, co * 128:(co + 1) * 128], rhs=yb[:, ct, :],
                                                   start=(ct == 0), stop=(ct == 7)), reads=[gw, yb], writes=[pG])
                kb.op("act", lambda e: e.activation(out=sgb[:, :], in_=pG[:, :], func=AF.Sigmoid,
                                                    bias=glb[:, co:co + 1]), reads=[pG, glb], writes=[sgb])
                if G.get("_dbg") == "s5e":
                    continue
                TT(kb, "pool", W(a1b), W(sgb), (yb[:, co, :], yb), ALU.mult)
                TT(kb, G.get("_eng", "pool"), W(ob), W(a1b), (zb_[:, co, :], zb_), ALU.mult)
                if G.get("_dbg") == "s5f":
                    continue
                kb.dma("sp", G["mixT"], G["mixT"].t[co, :, tsl], ob, ob[:, :])
    kb.barrier()


def layer0_attn(kb, G):
    with ExitStack() as c:
        QT = [[kb.sb(c, "QT%d_%d" % (g, i), [128, SEQ], BF16) for g in range(3)] for i in range(2)]
        KT = [[kb.sb(c, "KT%d_%d" % (g, i), [128, SEQ], BF16) for g in range(3)] for i in range(2)]
        VV = [[kb.sb(c, "VV%d_%d" % (g, i), [128, 16, 128], BF16) for g in range(3)] for i in range(2)]
        zbh = [kb.sb(c, "zbh%d" % i, [128, SEQ], BF16) for i in range(2)]
        Oacc = [kb.sb(c, "Oacc%d" % i, [128, SEQ], F32) for i in range(2)]
        Dacc = [kb.sb(c, "Dacc%d" % i, [1, SEQ], F32) for i in range(2)]
        rD = [kb.sb(c, "rD%d" % i, [1, SEQ], F32) for i in range(2)]
        ao = [kb.sb(c, "ao%d" % i, [128, SEQ], BF16) for i in range(2)]
        pT = [kb.sb(c, "pT%d" % i, [128, 2, 128], BF16) for i in range(3)]
        ps_s = [kb.ps(c, "pss%d" % i, [128, 2, 128]) for i in range(2)]
        ps_o = [kb.ps(c, "pso%d" % i, [128, 4, 128]) for i in range(2)]
        ps_d = [kb.ps(c, "psd%d" % i, [1, 4, 128]) for i in range(2)]
        ps_b = [kb.ps(c, "psb%d" % i, [128, 512]) for i in range(2)]
        nu = 0
        nb_ = 0
        nbb = 0
        for hd in range(8):
            s = hd % 2
            for g in range(3):
                kb.dma("sp", QT[s][g], QT[s][g][:, :], G["qT"], G["qT"].t[g, hd, :, :])
                kb.dma("sp", KT[s][g], KT[s][g][:, :], G["kT"], G["kT"].t[g, hd, :, :])
                kb.dma("sp", VV[s][g], VV[s][g][:, :, :], G["v"],
                       G["v"].t[g, :, hd * 128:(hd + 1) * 128].rearrange("(t p) d -> p t d", p=128))
            kb.dma("sp", zbh[s], zbh[s][:, :], G["zbT"], G["zbT"].t[hd, :, :])
            O, Dn = Oacc[s], Dacc[s]
            for g in range(3):
                Q, Kt, V = QT[s][g], KT[s][g], VV[s][g]
                for bt in range(4):
                    po, pd = ps_o[nb_ % 2], ps_d[nb_ % 2]
                    nb_ += 1
                    for j in range(4):
                        if g == 0:
                            cur = 4 * bt + j
                            prev = cur - 1 if cur > 0 else None
                        elif g == 1:
                            cur = 4 * bt + j
                            prev = cur - 1 if j > 0 else None
                        else:
                            cur = 4 * bt + j
                            prev = None
                        qs = slice(cur * 128, (cur + 1) * 128)
                        ss = ps_s[nu % 2]
                        p = pT[nu % 3]
                        nu += 1
                        kb.op("pe", lambda e: e.matmul(ss[:, 1, :], lhsT=Kt[:, qs], rhs=Q[:, qs], start=True, stop=True),
                              reads=[Kt, Q], writes=[ss])
                        lo = 1
                        if prev is not None:
                            lo = 0
                            kb.op("pe", lambda e: e.matmul(ss[:, 0, :], lhsT=Kt[:, prev * 128:(prev + 1) * 128],
                                                           rhs=Q[:, qs], start=True, stop=True),
                                  reads=[Kt, Q], writes=[ss])
                        kb.op("act", lambda e: e.activation(out=p[:, lo:2, :], in_=ss[:, lo:2, :], func=AF.Exp,
                                                            scale=ATT_SCALE), reads=[ss], writes=[p])
                        kb.op("dve", lambda e: e.tensor_tensor(out=p[:, lo:2, :], in0=p[:, lo:2, :],
                                                               in1=G["maskT"][:, lo:2, :], op=ALU.mult),
                              reads=[p, G["maskT"]], writes=[p])
                        kb.op("pe", lambda e: e.matmul(po[:, j, :], lhsT=V[:, cur, :], rhs=p[:, 1, :],
                                                       start=True, stop=(prev is None)), reads=[V, p], writes=[po])
                        if prev is not None:
                            kb.op("pe", lambda e: e.matmul(po[:, j, :], lhsT=V[:, prev, :], rhs=p[:, 0, :],
                                                           start=False, stop=True), reads=[V, p], writes=[po])
                        kb.op("pe", lambda e: e.matmul(pd[0:1, j, :], lhsT=G["onesb"][:, 0:1], rhs=p[:, 1, :],
                                                       start=True, stop=(prev is None)), reads=[G["onesb"], p], writes=[pd])
                        if prev is not None:
                            kb.op("pe", lambda e: e.matmul(pd[0:1, j, :], lhsT=G["onesb"][:, 0:1], rhs=p[:, 0, :],
                                                           start=False, stop=True), reads=[G["onesb"], p], writes=[pd])
                    pof = po.t.rearrange("p a q -> p (a q)")
                    pdf = pd.t.rearrange("p a q -> p (a q)")
                    if g == 0:
                        cs = slice(bt * 512, (bt + 1) * 512)
                        kb.op("act", lambda e: e.activation(out=O[:, cs], in_=pof, func=AF.Copy), reads=[po], writes=[O])
                        kb.op("act", lambda e: e.activation(out=Dn[0:1, cs], in_=pdf, func=AF.Copy),
                              reads=[pd], writes=[Dn])
                    elif g == 1:
                        ov = O.t.rearrange("p (m r) -> p r m", r=4)[:, bt, :]
                        dv = Dn.t.rearrange("p (m r) -> p r m", r=4)[:, bt, :]
                        kb.op("dve", lambda e: e.tensor_tensor(out=ov, in0=pof, in1=ov, op=ALU.add),
                              reads=[po, O], writes=[O])
                        kb.op("dve", lambda e: e.tensor_tensor(out=dv, in0=pdf, in1=dv, op=ALU.add),
                              reads=[pd, Dn], writes=[Dn])
                    else:
                        ov = O.t.rearrange("p (m r) -> p r m", r=16)[:, 4 * bt:4 * bt + 4, :]
                        dv = Dn.t.rearrange("p (m r) -> p r m", r=16)[:, 4 * bt:4 * bt + 4, :]
                        kb.op("dve", lambda e: e.tensor_tensor(out=ov, in0=po[:, :, :], in1=ov, op=ALU.add),
                              reads=[po, O], writes=[O])
                        kb.op("dve", lambda e: e.tensor_tensor(out=dv, in0=pd[0:1, :, :], in1=dv, op=ALU.add),
                              reads=[pd, Dn], writes=[Dn])
            r_ = rD[s]
            kb.op("dve", lambda e: e.reciprocal(out=r_[0:1, :], in_=Dn[0:1, :]), reads=[Dn], writes=[r_])
            a_ = ao[s]
            for tcx in range(4):
                cs = slice(tcx * 512, (tcx + 1) * 512)
                pb = ps_b[nbb % 2]
                nbb += 1
                kb.op("pe", lambda e: e.matmul(pb[:, :], lhsT=G["onesf"][0:1, 0:128], rhs=r_[0:1, cs],
                                               start=True, stop=True), reads=[G["onesf"], r_], writes=[pb])
                kb.op("dve", lambda e: e.tensor_tensor(out=O[:, cs], in0=O[:, cs], in1=pb[:, :], op=ALU.mult),
                      reads=[O, pb], writes=[O])
                kb.op("dve", lambda e: e.tensor_tensor(out=a_[:, cs], in0=O[:, cs], in1=zbh[s][:, cs], op=ALU.mult),
                      reads=[O, zbh[s]], writes=[a_])
            kb.dma("sp", G["mixT"], G["mixT"].t[8 + hd, :, :], a_, a_[:, :])
    kb.barrier()


def outproj(kb, G, half, src, KT, wname, resid, dst):
    t0 = half * HALF
    with ExitStack() as c:
        mT = kb.sb(c, "mT", [128, KT, HALF], BF16)
        kb.dma("sp", mT, mT[:, :, :], G[src], G[src].t[:, :, t0:t0 + HALF].rearrange("k p t -> p k t"))
        wbufs = [kb.sb(c, "wo%d" % i, [128, KT, 512], BF16) for i in range(2)]
        xr = [kb.sb(c, "xo%d" % i, [128, 512], F32) for i in range(3)]
        pacc = [kb.ps(c, "po%d" % i, [128, 512]) for i in range(4)]
        wv = G[wname].t.rearrange("(k p) n -> p k n", p=128)
        n = 0
        for cc in range(8):
            wb = wbufs[cc % 2]
            cs = slice(cc * 512, (cc + 1) * 512)
            kb.dma("pool", wb, wb[:, :, :], G[wname], wv[:, :, cs])
            for ti in range(8):
                x_b = xr[n % 3]
                ps = pacc[n % 4]
                n += 1
                rows = slice(t0 + ti * 128, t0 + (ti + 1) * 128)
                kb.dma("sp", x_b, x_b[:, :], G[resid], G[resid].t[rows, cs])
                for k in range(KT):
                    kb.op("pe", lambda e: e.matmul(ps[:, :], lhsT=mT[:, k, ti * 128:(ti + 1) * 128], rhs=wb[:, k, :],
                                                   start=(k == 0), stop=(k == KT - 1)), reads=[mT, wb], writes=[ps])
                kb.op("dve", lambda e: e.tensor_tensor(out=x_b[:, :], in0=ps[:, :], in1=x_b[:, :], op=ALU.add),
                      reads=[ps, x_b], writes=[x_b])
                kb.dma("sp", G[dst], G[dst].t[rows, cs], x_b, x_b[:, :])
    kb.barrier()


Q_RANK, KV_RANK = 1536, 512
L1_KR0, L1_KI0, L1_WI0, L1_GATE0 = 2048, 2080, 2208, 2240
NEG = -1.0e30


def load_cols_T(kb, c, srcs, ncol):
    outs = [kb.sb(c, "lc_o%d" % j, [128, ncol], F32) for j in range(len(srcs))]
    with ExitStack() as c2:
        st = kb.sb(c2, "lc_st", [ncol, len(srcs), 128], F32)
        pp = kb.ps(c2, "lc_ps", [128, len(srcs), ncol], F32)
        for j, sb_ in enumerate(srcs):
            kb.dma("sp", st, st[:, j, :], sb_, sb_.t.rearrange("(ct c) -> ct c", c=128))
        for j in range(len(srcs)):
            kb.op("pe", lambda e: e.transpose(out=pp[:, j, :], in_=st[:, j, :], identity=G_IDENTF[0][0:ncol, 0:ncol]),
                  reads=[st, G_IDENTF[0]], writes=[pp])
            o = outs[j]
            kb.op("dve", lambda e: e.tensor_copy(out=o[:, :], in_=pp[:, j, :]), reads=[pp], writes=[o])
        kb.barrier()
    return outs


G_IDENTF = [None]


def layer1_inproj(kb, G, half):
    t0 = half * HALF
    G_IDENTF[0] = G["identf"]
    with ExitStack() as c:
        hT = kb.sb(c, "hT1", [128, 32, HALF], BF16)
        phase_norm_T(kb, G, G["x1"], t0, G["odd_norm"], hT)
        (gq,) = load_cols_T(kb, c, [G["mla_q_norm"]], 12)
        (gkv,) = load_cols_T(kb, c, [G["mla_kv_norm"]], 4)
        (gki,) = load_cols_T(kb, c, [G["idx_k_norm"]], 1)
        wbufs = [kb.sb(c, "wb%d" % i, [128, 32, 256], BF16) for i in range(3)]
        pacc = [kb.ps(c, "pacc%d" % i, [128, 512], F32) for i in range(3)]
        pssq = [kb.ps(c, "pssq%d" % i, [128, 512], F32) for i in range(2)]
        psw = kb.ps(c, "psw", [32, 512], F32)
        ptr = kb.ps(c, "ptr", [128, 4, 32], F32)
        raw = kb.sb(c, "raw", [128, 12, HALF], F32)
        sq = [kb.sb(c, "sq%d" % i, [128, 512], BF16) for i in range(2)]
        rstd = kb.sb(c, "rstd", [128, HALF], F32)
        stg = [kb.sb(c, "stg%d" % i, [128, HALF], BF16) for i in range(3)]
        xr = kb.sb(c, "xr", [32, 512], F32)
        rt = kb.sb(c, "rt", [32, 2, 512], F32)
        kin = kb.sb(c, "kin", [128, HALF], F32)
        wis = kb.sb(c, "wis", [32, HALF], F32)
        wit = kb.sb(c, "wit", [128, 8, 32], F32)
        st = {"np": 0, "ns": 0, "nq": 0}

        def mm(wb, off, M, ps, tc):
            for k in range(32):
                kb.op("pe", lambda e: e.matmul(ps[0:M, :], lhsT=wb[:, k, off:off + M],
                                               rhs=hT[:, k, tc * 512:(tc + 1) * 512],
                                               start=(k == 0), stop=(k == 31)), reads=[wb, hT], writes=[ps])

        def finish_norm(ntile, rank, gcol, dst):
            for tc in range(2):
                cs = slice(tc * 512, (tc + 1) * 512)
                kb.op("dve", lambda e: e.tensor_scalar(out=rstd[:, cs], in0=pssq[tc][:, :], scalar1=1.0 / rank,
                                                       scalar2=EPS, op0=ALU.mult, op1=ALU.add),
                      reads=[pssq[tc]], writes=[rstd])
            kb.op("act", lambda e: e.activation(out=rstd[:, :], in_=rstd[:, :], func=AF.Sqrt), reads=[rstd], writes=[rstd])
            kb.op("dve", lambda e: e.reciprocal(out=rstd[:, :], in_=rstd[:, :]), reads=[rstd], writes=[rstd])
            for ct in range(ntile):
                s = stg[st["ns"] % 3]
                st["ns"] += 1
                kb.op("dve", lambda e: e.scalar_tensor_tensor(out=s[:, :], in0=raw[:, ct, :], scalar=gcol[:, ct:ct + 1],
                                                              in1=rstd[:, :], op0=ALU.mult, op1=ALU.mult),
                      reads=[raw, gcol, rstd], writes=[s])
                kb.dma("sp", dst, dst.t[ct, :, t0:t0 + HALF], s, s[:, :])

        def norm_tile(c0, ct, ntile, rank, gcol, dst):
            def fn(wb, off):
                for tc in range(2):
                    ps = pacc[st["np"] % 3]
                    st["np"] += 1
                    mm(wb, off, 128, ps, tc)
                    cs = slice(tc * 512, (tc + 1) * 512)
                    kb.op("act", lambda e: e.activation(out=raw[:, ct, cs], in_=ps[:, :], func=AF.Copy),
                          reads=[ps], writes=[raw])
                    sq_b = sq[st["nq"] % 2]
                    st["nq"] += 1
                    kb.op("act", lambda e: e.activation(out=sq_b[:, :], in_=ps[:, :], func=AF.Square),
                          reads=[ps], writes=[sq_b])
                    kb.op("pe", lambda e: e.matmul(pssq[tc][:, :], lhsT=G["onesb"][:, :], rhs=sq_b[:, :],
                                                   start=(ct == 0), stop=(ct == ntile - 1)),
                          reads=[G["onesb"], sq_b], writes=[pssq[tc]])
                if ct == ntile - 1:
                    finish_norm(ntile, rank, gcol, dst)
            return (c0, c0 + 128, fn)

        def kr_job():
            def fn(wb, off):
                s = stg[st["ns"] % 3]
                st["ns"] += 1
                for tc in range(2):
                    ps = pacc[st["np"] % 3]
                    st["np"] += 1
                    mm(wb, off, 32, ps, tc)
                    rope_evac(kb, G, (ps[0:32, :], ps),
                              lambda lo, hi: (s.t[lo:hi, tc * 512:(tc + 1) * 512].rearrange("p (r m) -> p r m", r=1), s),
                              t0 + tc * 512, xr, psw, rt, 1)
                kb.dma("sp", G["krT"], G["krT"].t[:, t0:t0 + HALF], s, s[0:32, :])
            return (L1_KR0, L1_KR0 + 32, fn)

        def ki_job():
            def fn(wb, off):
                s = stg[st["ns"] % 3]
                st["ns"] += 1
                for tc in range(2):
                    ps = pacc[st["np"] % 3]
                    st["np"] += 1
                    mm(wb, off, 128, ps, tc)
                    cs = slice(tc * 512, (tc + 1) * 512)
                    kb.op("act", lambda e: e.activation(out=kin[:, cs], in_=ps[:, :], func=AF.Copy),
                          reads=[ps], writes=[kin])
                    sq_b = sq[st["nq"] % 2]
                    st["nq"] += 1
                    kb.op("act", lambda e: e.activation(out=sq_b[:, :], in_=ps[:, :], func=AF.Square),
                          reads=[ps], writes=[sq_b])
                    kb.op("pe", lambda e: e.matmul(pssq[tc][:, :], lhsT=G["onesb"][:, :], rhs=sq_b[:, :],
                                                   start=True, stop=True), reads=[G["onesb"], sq_b], writes=[pssq[tc]])
                    kb.op("dve", lambda e: e.tensor_scalar(out=rstd[:, cs], in0=pssq[tc][:, :], scalar1=1.0 / 128,
                                                           scalar2=EPS, op0=ALU.mult, op1=ALU.add),
                          reads=[pssq[tc]], writes=[rstd])
                    kb.op("act", lambda e: e.activation(out=rstd[:, cs], in_=rstd[:, cs], func=AF.Sqrt),
                          reads=[rstd], writes=[rstd])
                    kb.op("dve", lambda e: e.reciprocal(out=rstd[:, cs], in_=rstd[:, cs]), reads=[rstd], writes=[rstd])
                    kb.op("dve", lambda e: e.scalar_tensor_tensor(out=kin[:, cs], in0=kin[:, cs], scalar=gki[:, 0:1],
                                                                  in1=rstd[:, cs], op0=ALU.mult, op1=ALU.mult),
                          reads=[kin, gki, rstd], writes=[kin])
                    for (lo, hi) in ((32, 64), (64, 128)):
                        kb.op("act", lambda e: e.activation(out=s[lo:hi, cs], in_=kin[lo:hi, cs], func=AF.Copy),
                              reads=[kin], writes=[s])
                    rope_evac(kb, G, (kin[0:32, cs], kin),
                              lambda lo, hi: (s.t[lo:hi, cs].rearrange("p (r m) -> p r m", r=1), s),
                              t0 + tc * 512, xr, psw, rt, 1)
                kb.dma("sp", G["kiT"], G["kiT"].t[:, t0:t0 + HALF], s, s[:, :])
            return (L1_KI0, L1_KI0 + 128, fn)

        def wi_job():
            def fn(wb, off):
                for tc in range(2):
                    ps = pacc[st["np"] % 3]
                    st["np"] += 1
                    mm(wb, off, 32, ps, tc)
                    kb.op("act", lambda e: e.activation(out=wis[:, tc * 512:(tc + 1) * 512], in_=ps[0:32, :],
                                                        func=AF.Copy, scale=float((128 * 32) ** -0.5)),
                          reads=[ps], writes=[wis])
                for tg in range(2):
                    for j in range(4):
                        ti = 4 * tg + j
                        kb.op("pe", lambda e: e.transpose(out=ptr[:, j, :], in_=wis[:, ti * 128:(ti + 1) * 128],
                                                          identity=G["identf"][0:32, 0:32]),
                              reads=[wis, G["identf"]], writes=[ptr])
                    kb.op("dve", lambda e: e.tensor_copy(out=wit[:, 4 * tg:4 * tg + 4, :], in_=ptr[:, :, :]),
                          reads=[ptr], writes=[wit])
                kb.dma("sp", G["wi"], G["wi"].t[t0:t0 + HALF, :].rearrange("(t p) h -> p t h", p=128), wit, wit[:, :, :])
            return (L1_WI0, L1_WI0 + 32, fn)

        def gate_job(ct):
            c0 = L1_GATE0 + ct * 128

            def fn(wb, off):
                s = stg[st["ns"] % 3]
                st["ns"] += 1
                for tc in range(2):
                    ps = pacc[st["np"] % 3]
                    st["np"] += 1
                    mm(wb, off, 128, ps, tc)
                    kb.op("act", lambda e: e.activation(out=s[:, tc * 512:(tc + 1) * 512], in_=ps[:, :], func=AF.Silu),
                          reads=[ps], writes=[s])
                kb.dma("sp", G["sgT"], G["sgT"].t[ct, :, t0:t0 + HALF], s, s[:, :])
            return (c0, c0 + 128, fn)

        jobs = [norm_tile(ct * 128, ct, 12, Q_RANK, gq, G["cqT"]) for ct in range(12)]
        jobs += [norm_tile(Q_RANK + ct * 128, ct, 4, KV_RANK, gkv, G["ckvT"]) for ct in range(4)]
        jobs += [kr_job(), ki_job(), wi_job()]
        jobs += [gate_job(ct) for ct in range(32)]
        stream_proj(kb, G["odd_w_in"], 32, jobs, wbufs)
    kb.barrier()


def layer1_qproj(kb, G, half):
    t0 = half * HALF
    with ExitStack() as c:
        cq = kb.sb(c, "cq", [128, 12, HALF], BF16)
        kb.dma("sp", cq, cq[:, :, :], G["cqT"], G["cqT"].t[:, :, t0:t0 + HALF].rearrange("k p t -> p k t"))
        wbufs = [kb.sb(c, "wq%d" % i, [128, 12, 256], BF16) for i in range(3)]
        pacc = [kb.ps(c, "pacc%d" % i, [128, 512], F32) for i in range(4)]
        psw = [kb.ps(c, "psw%d" % i, [32, 512], F32) for i in range(2)]
        stg = [kb.sb(c, "stg%d" % i, [128, HALF], BF16) for i in range(3)]
        xr = [kb.sb(c, "xr%d" % i, [32, 512], F32) for i in range(2)]
        rt = [kb.sb(c, "rt%d" % i, [32, 2, 512], F32) for i in range(2)]
        st = {"np": 0, "ns": 0, "nr": 0}

        def job(h, dst):
            def fn(wb, off):
                s = stg[st["ns"] % 3]
                st["ns"] += 1
                for tc in range(2):
                    ps = pacc[st["np"] % 4]
                    st["np"] += 1
                    cs = slice(tc * 512, (tc + 1) * 512)
                    for k in range(12):
                        kb.op("pe", lambda e: e.matmul(ps[:, :], lhsT=wb[:, k, off:off + 128], rhs=cq[:, k, cs],
                                                       start=(k == 0), stop=(k == 11)), reads=[wb, cq], writes=[ps])
                    for (lo, hi) in ((32, 64), (64, 128)):
                        kb.op("act", lambda e: e.activation(out=s[lo:hi, cs], in_=ps[lo:hi, :], func=AF.Copy),
                              reads=[ps], writes=[s])
                    i = st["nr"] % 2
                    st["nr"] += 1
                    rope_evac(kb, G, (ps[0:32, :], ps),
                              lambda lo, hi: (s.t[lo:hi, cs].rearrange("p (r m) -> p r m", r=1), s),
                              t0 + tc * 512, xr[i], psw[i], rt[i], 1)
                kb.dma("sp", dst, dst.t[h, :, t0:t0 + HALF], s, s[:, :])
            return (h * 128, (h + 1) * 128, fn)

        stream_proj(kb, G["mla_w_uq"], 12, [job(h, G["q1T"]) for h in range(32)], wbufs)
        stream_proj(kb, G["idx_w_q"], 12, [job(h, G["qiT"]) for h in range(32)], wbufs)
    kb.barrier()


def layer1_index(kb, G):
    with ExitStack() as c:
        kidx = kb.sb(c, "kidx", [128, SEQ], BF16)
        kb.dma("sp", kidx, kidx[:, :], G["kiT"], G["kiT"].t[:, :])
        negm = kb.sb(c, "negm", [128, 128], F32)
        kb.dma("sp", negm, negm[:, :], G["c_negmask"], G["c_negmask"].t[:, :])
        qi = [kb.sb(c, "qi%d" % i, [128, 32, 128], BF16) for i in range(2)]
        wq = [kb.sb(c, "wq%d" % i, [128, 32], F32) for i in range(2)]
        acc = [kb.sb(c, "acc%d" % i, [128, SEQ], F32) for i in range(2)]
        acc2 = [kb.sb(c, "accp%d" % i, [128, SEQ], F32) for i in range(2)]
        rl = [kb.sb(c, "rl%d" % i, [128, 512], F32) for i in range(4)]
        tmpp = [kb.sb(c, "tmpp%d" % i, [128, 512], F32) for i in range(2)]
        wk = kb.sb(c, "wk", [128, SEQ], F32)
        m8 = kb.sb(c, "m8", [128, 8], F32)
        msk = [kb.sb(c, "msk%d" % i, [128, SEQ], BF16) for i in range(2)]
        mTs = [kb.sb(c, "mTs%d" % i, [128, 16, 128], BF16) for i in range(2)]
        ps = [kb.ps(c, "pix%d" % i, [128, 512], F32) for i in range(4)]
        ptp = [kb.ps(c, "pmt%d" % i, [128, 8, 128], BF16) for i in range(2)]
        nps = 0
        nrl = 0
        ntp = 0
        for i in range(16):
            n = 128 * (i + 1)
            s = i % 2
            q_b, w_b, a_b, a2_b = qi[s], wq[s], acc[s], acc2[s]
            qs = slice(i * 128, (i + 1) * 128)
            kb.dma("sp", q_b, q_b[:, :, :], G["qiT"], G["qiT"].t[:, :, qs].rearrange("h d q -> d h q"))
            kb.dma("sp", w_b, w_b[:, :], G["wi"], G["wi"].t[qs, :])
            nch = (n + 511) // 512
            for h in range(32):
                for kc in range(nch):
                    N = min(512, n - 512 * kc)
                    ks = slice(512 * kc, 512 * kc + N)
                    p_b = ps[nps % 4]
                    nps += 1
                    r_b = rl[nrl % 4]
                    nrl += 1
                    kb.op("pe", lambda e: e.matmul(p_b[:, 0:N], lhsT=q_b[:, h, :], rhs=kidx[:, ks], start=True, stop=True),
                          reads=[q_b, kidx], writes=[p_b])
                    kb.op("act", lambda e: e.activation(out=r_b[:, 0:N], in_=p_b[:, 0:N], func=AF.Relu),
                          reads=[p_b], writes=[r_b])
                    if h % 2 == 0:
                        if h == 0:
                            kb.op("dve", lambda e: e.tensor_scalar(out=a_b[:, ks], in0=r_b[:, 0:N], scalar1=w_b[:, 0:1],
                                                                   scalar2=None, op0=ALU.mult),
                                  reads=[r_b, w_b], writes=[a_b])
                        else:
                            kb.op("dve", lambda e: e.scalar_tensor_tensor(out=a_b[:, ks], in0=r_b[:, 0:N],
                                                                          scalar=w_b[:, h:h + 1], in1=a_b[:, ks],
                                                                          op0=ALU.mult, op1=ALU.add),
                                  reads=[r_b, w_b, a_b], writes=[a_b])
                    else:
                        if h == 1:
                            kb.op("pool", lambda e: e.tensor_scalar(out=a2_b[:, ks], in0=r_b[:, 0:N], scalar1=w_b[:, 1:2],
                                                                    scalar2=None, op0=ALU.mult),
                                  reads=[r_b, w_b], writes=[a2_b])
                        else:
                            t_b = tmpp[(h // 2) % 2]
                            kb.op("pool", lambda e: e.tensor_scalar(out=t_b[:, 0:N], in0=r_b[:, 0:N],
                                                                    scalar1=w_b[:, h:h + 1], scalar2=None, op0=ALU.mult),
                                  reads=[r_b, w_b], writes=[t_b])
                            kb.op("pool", lambda e: e.tensor_tensor(out=a2_b[:, ks], in0=a2_b[:, ks], in1=t_b[:, 0:N],
                                                                    op=ALU.add), reads=[a2_b, t_b], writes=[a2_b])
            kb.op("dve", lambda e: e.tensor_tensor(out=a_b[:, 0:n], in0=a_b[:, 0:n], in1=a2_b[:, 0:n], op=ALU.add),
                  reads=[a_b, a2_b], writes=[a_b])
            kb.op("dve", lambda e: e.tensor_tensor(out=a_b[:, qs], in0=a_b[:, qs], in1=negm[:, :], op=ALU.add),
                  reads=[a_b, negm], writes=[a_b])
            m_b = msk[s]
            if i >= 2:
                kb.op("dve", lambda e: e.tensor_copy(out=wk[:, 0:n], in_=a_b[:, 0:n]), reads=[a_b], writes=[wk])
                for r in range(32):
                    kb.op("dve", lambda e: e.max(out=m8[:, :], in_=wk[:, 0:n]), reads=[wk], writes=[m8])
                    if r < 31:
                        kb.op("dve", lambda e: e.match_replace(out=wk[:, 0:n], in_to_replace=m8[:, :],
                                                               in_values=wk[:, 0:n], imm_value=NEG),
                              reads=[wk, m8], writes=[wk])
                kb.op("dve", lambda e: e.tensor_scalar(out=m_b[:, 0:n], in0=a_b[:, 0:n], scalar1=m8[:, 7:8],
                                                       scalar2=None, op0=ALU.is_ge), reads=[a_b, m8], writes=[m_b])
            else:
                kb.op("dve", lambda e: e.tensor_scalar(out=m_b[:, 0:n], in0=a_b[:, 0:n], scalar1=-1.0e29,
                                                       scalar2=None, op0=ALU.is_ge), reads=[a_b], writes=[m_b])
            mt_b = mTs[s]
            for j0 in range(0, i + 1, 8):
                nj = min(8, i + 1 - j0)
                t_b = ptp[ntp % 2]
                ntp += 1
                for jj in range(nj):
                    j = j0 + jj
                    kb.op("pe", lambda e: e.transpose(out=t_b[:, jj, :], in_=m_b[:, j * 128:(j + 1) * 128],
                                                      identity=G["ident"][:, :]), reads=[m_b, G["ident"]], writes=[t_b])
                kb.op("act", lambda e: e.activation(out=mt_b[:, j0:j0 + nj, :], in_=t_b[:, 0:nj, :], func=AF.Copy),
                      reads=[t_b], writes=[mt_b])
            kb.dma("sp", G["selT"], G["selT"].t[0:i + 1, :, qs].rearrange("j k q -> k j q"), mt_b, mt_b[:, 0:i + 1, :])
    kb.barrier()


def layer1_attn(kb, G):
    with ExitStack() as c:
        ckv = kb.sb(c, "ckv", [128, 4, SEQ], BF16)
        kb.dma("sp", ckv, ckv[:, :, :], G["ckvT"], G["ckvT"].t[:, :, :].rearrange("k p t -> p k t"))
        krp = kb.sb(c, "krp", [128, SEQ], BF16)
        kb.op("dve", lambda e: e.memset(krp[:, :], 0.0), writes=[krp])
        kb.dma("sp", krp, krp[0:32, :], G["krT"], G["krT"].t[:, :])
        sel = kb.sb(c, "sel", [128, 16, SEQ], BF16)
        for j in range(16):
            kb.dma("sp", sel, sel[:, j, j * 128:SEQ], G["selT"], G["selT"].t[j, :, j * 128:SEQ])
        wuk = [kb.sb(c, "wuk%d" % i, [128, 4, 128], BF16) for i in range(2)]
        wuv = [kb.sb(c, "wuv%d" % i, [128, 4, 128], BF16) for i in range(2)]
        Qh = [kb.sb(c, "Qh%d" % i, [128, SEQ], BF16) for i in range(2)]
        sgh = [kb.sb(c, "sgh%d" % i, [128, SEQ], BF16) for i in range(2)]
        KhT = [kb.sb(c, "KhT%d" % i, [128, SEQ], BF16) for i in range(2)]
        Va = [kb.sb(c, "Va%d" % i, [128, 16, 130], BF16) for i in range(2)]
        ogh = [kb.sb(c, "ogh%d" % i, [128, SEQ], BF16) for i in range(2)]
        PT = [kb.sb(c, "PT%d" % i, [128, 16, 128], BF16) for i in range(2)]
        on = [kb.sb(c, "on%d" % i, [128, 128], BF16) for i in range(2)]
        rd = [kb.sb(c, "rd%d" % i, [128, 2], F32) for i in range(2)]
        pk = [kb.ps(c, "pk%d" % i, [128, 512], F32) for i in range(2)]
        pst = [kb.ps(c, "pst%d" % i, [128, 4, 128], F32) for i in range(2)]
        po = [kb.ps(c, "pov%d" % i, [128, 130], F32) for i in range(2)]
        ptr = [kb.ps(c, "ptr%d" % i, [128, 128], BF16) for i in range(2)]
        for i in range(2):
            kb.op("dve", lambda e: e.memset(wuk[i][:, :, :], 0.0), writes=[wuk[i]])
            kb.op("dve", lambda e: e.memset(Va[i][:, :, 128:130], 1.0), writes=[Va[i]])
        ukv = G["mla_w_uk"].t.rearrange("(k p) h d -> p k h d", p=128)
        uvv = G["mla_w_uv"].t.rearrange("(k p) h d -> p k h d", p=128)
        npk = 0
        nst = 0
        npo = 0
        nq = 0
        for h in range(32):
            s = h % 2
            kb.dma("pool", wuk[s], wuk[s][:, :, 32:128], G["mla_w_uk"], ukv[:, :, h, :])
            kb.dma("pool", wuv[s], wuv[s][:, :, :], G["mla_w_uv"], uvv[:, :, h, :])
            kb.dma("sp", Qh[s], Qh[s][:, :], G["q1T"], G["q1T"].t[h, :, :])
            kb.dma("sp", sgh[s], sgh[s][:, :], G["sgT"], G["sgT"].t[h, :, :])
            K_, V_ = KhT[s], Va[s]
            for kc in range(4):
                cs = slice(kc * 512, (kc + 1) * 512)
                p_b = pk[npk % 2]
                npk += 1
                for c4 in range(4):
                    kb.op("pe", lambda e: e.matmul(p_b[:, :], lhsT=wuk[s][:, c4, :], rhs=ckv[:, c4, cs],
                                                   start=(c4 == 0), stop=(c4 == 3)), reads=[wuk[s], ckv], writes=[p_b])
                kb.op("dve", lambda e: e.tensor_tensor(out=K_[:, cs], in0=p_b[:, :], in1=krp[:, cs], op=ALU.add),
                      reads=[p_b, krp], writes=[K_])
            for j0 in range(0, 16, 4):
                p_b = pk[npk % 2]
                npk += 1
                for jj in range(4):
                    j = j0 + jj
                    for c4 in range(4):
                        kb.op("pe", lambda e: e.matmul(p_b[:, jj * 128:(jj + 1) * 128], lhsT=ckv[:, c4, j * 128:(j + 1) * 128],
                                                       rhs=wuv[s][:, c4, :], start=(c4 == 0), stop=(c4 == 3)),
                              reads=[ckv, wuv[s]], writes=[p_b])
                kb.op("act", lambda e: e.activation(out=V_[:, j0:j0 + 4, 0:128],
                                                    in_=p_b.t[:, :].rearrange("p (a v) -> p a v", a=4), func=AF.Copy),
                      reads=[p_b], writes=[V_])
            og = ogh[s]
            for i in range(16):
                qs = slice(i * 128, (i + 1) * 128)
                P_ = PT[nq % 2]
                nq += 1
                for j0 in range(0, i + 1, 4):
                    nj = min(4, i + 1 - j0)
                    s_b = pst[nst % 2]
                    nst += 1
                    for jj in range(nj):
                        j = j0 + jj
                        kb.op("pe", lambda e: e.matmul(s_b[:, jj, :], lhsT=K_[:, j * 128:(j + 1) * 128], rhs=Qh[s][:, qs],
                                                       start=True, stop=True), reads=[K_, Qh[s]], writes=[s_b])
                    kb.op("act", lambda e: e.activation(out=P_[:, j0:j0 + nj, :], in_=s_b[:, 0:nj, :], func=AF.Exp,
                                                        scale=ATT_SCALE), reads=[s_b], writes=[P_])
                kb.op("dve", lambda e: e.tensor_tensor(out=P_[:, 0:i + 1, :], in0=P_[:, 0:i + 1, :],
                                                       in1=sel[:, 0:i + 1, qs], op=ALU.mult), reads=[P_, sel], writes=[P_])
                o_b = po[npo % 2]
                t_b = ptr[npo % 2]
                on_b = on[npo % 2]
                r_b = rd[npo % 2]
                npo += 1
                for j in range(i + 1):
                    kb.op("pe", lambda e: e.matmul(o_b[:, 0:129], lhsT=P_[:, j, :], rhs=V_[:, j, 0:129],
                                                   start=(j == 0), stop=(j == i)), reads=[P_, V_], writes=[o_b])
                kb.op("dve", lambda e: e.reciprocal(out=r_b[:, 0:1], in_=o_b[:, 128:129]), reads=[o_b], writes=[r_b])
                kb.op("act", lambda e: e.activation(out=on_b[:, :], in_=o_b[:, 0:128], func=AF.Copy, scale=r_b[:, 0:1]),
                      reads=[o_b, r_b], writes=[on_b])
                kb.op("pe", lambda e: e.transpose(out=t_b[:, :], in_=on_b[:, :], identity=G["ident"][:, :]),
                      reads=[on_b, G["ident"]], writes=[t_b])
                kb.op("dve", lambda e: e.tensor_tensor(out=og[:, qs], in0=t_b[:, :], in1=sgh[s][:, qs], op=ALU.mult),
                      reads=[t_b, sgh[s]], writes=[og])
            kb.dma("sp", G["ogT"], G["ogT"].t[h, :, :], og, og[:, :])
    kb.barrier()


def final_norm(kb, G):
    with ExitStack() as c:
        gbc = kb.sb(c, "fgbc", [128, D_MODEL], F32)
        xt = [kb.sb(c, "fxt%d" % i, [128, D_MODEL], F32) for i in range(2)]
        junk = kb.sb(c, "fjunk", [128, D_MODEL], BF16)
        yo = [kb.sb(c, "fyo%d" % i, [128, D_MODEL], F32) for i in range(2)]
        ss = [kb.sb(c, "fss%d" % i, [128, 4], F32) for i in range(2)]
        kb.dma("sp", gbc, gbc[:, :], G["final_norm"], G["final_norm"].t.partition_broadcast(128))
        for ti in range(16):
            x_b, y_b, s_b = xt[ti % 2], yo[ti % 2], ss[ti % 2]
            rows = slice(ti * 128, (ti + 1) * 128)
            kb.dma("sp", x_b, x_b[:, :], G["x2"], G["x2"].t[rows, :])
            kb.op("act", lambda e: e.activation(out=junk[:, :], in_=x_b[:, :], func=AF.Square, accum_out=s_b[:, 0:1]),
                  reads=[x_b], writes=[junk, s_b])
            kb.op("dve", lambda e: e.tensor_scalar(out=s_b[:, 1:2], in0=s_b[:, 0:1], scalar1=1.0 / D_MODEL, scalar2=EPS,
                                                   op0=ALU.mult, op1=ALU.add), reads=[s_b], writes=[s_b])
            kb.op("act", lambda e: e.activation(out=s_b[:, 2:3], in_=s_b[:, 1:2], func=AF.Sqrt), reads=[s_b], writes=[s_b])
            kb.op("dve", lambda e: e.reciprocal(out=s_b[:, 3:4], in_=s_b[:, 2:3]), reads=[s_b], writes=[s_b])
            kb.op("dve", lambda e: e.scalar_tensor_tensor(out=y_b[:, :], in0=x_b[:, :], scalar=s_b[:, 3:4], in1=gbc[:, :],
                                                          op0=ALU.mult, op1=ALU.mult), reads=[x_b, s_b, gbc], writes=[y_b])
            kb.dma("sp", G["out"], G["out"].t[rows, :], y_b, y_b[:, :])
    kb.barrier()
```

```python
import math
import numpy as np
import concourse.bass as bass
import concourse.mybir as mybir
from concourse.bass_utils import run_bass_kernel_spmd

F32 = mybir.dt.float32
BF16 = mybir.dt.bfloat16
AF = mybir.ActivationFunctionType
ALU = mybir.AluOpType
AX = mybir.AxisListType

D_MODEL = 4096
SEQ = 2048
BATCH = 4
HALF = 1024
EPS = 1e-6
NDS = 20
SAME_ENGINE_SYNC = True


class Buf:
    def __init__(self, name, t=None):
        self.name = name
        self.t = t
        self.w = {}
        self.r = {}
        self.track = True

    def __getitem__(self, k):
        return self.t[k]


class KB:
    def __init__(self, nc, stack):
        self.nc = nc
        self.eobj = {"pe": nc.tensor, "act": nc.scalar, "dve": nc.vector, "pool": nc.gpsimd, "sp": nc.sync}
        self.semsets = [{n: stack.enter_context(nc.semaphore("s%d_%s" % (i, n))) for n in self.eobj}
                        for i in range(3)]
        self.epoch = 0
        self.sem = self.semsets[0]
        self.cnt = {n: 0 for n in self.eobj}
        self.tot = {n: 0 for n in self.eobj}
        self.waited = {n: {} for n in self.eobj}
        self.dsem = {}
        self.dcnt = {}
        for qn in ("sp", "pool"):
            self.dsem[qn] = [stack.enter_context(nc.semaphore("d_%s%d" % (qn, i))) for i in range(NDS)]
            self.dcnt[qn] = 0
        self.nops = 0
        self.nuid = 0
        self.active = {}

    def sb(self, ctx, name, shape, dt):
        self.nuid += 1
        name = "%s_%d" % (name, self.nuid)
        t = ctx.enter_context(self.nc.sbuf_tensor(name, list(shape), dt))
        return Buf(name, t)

    def ps(self, ctx, name, shape, dt=F32):
        self.nuid += 1
        name = "%s_%d" % (name, self.nuid)
        t = ctx.enter_context(self.nc.psum_tensor(name, list(shape), dt))
        return Buf(name, t)

    def dram(self, name, shape, dt, kind="Internal"):
        t = self.nc.dram_tensor(name, list(shape), dt, kind=kind).ap()
        b = Buf(name, t)
        b.track = False
        return b

    def _deps(self, reads, writes):
        deps = {}
        ep = self.epoch

        def add(d):
            for k, (s, v, e) in d.items():
                if e < ep:
                    continue
                if k not in deps or deps[k][1] < v:
                    deps[k] = (s, v)
        for b in reads:
            if b.track:
                add(b.w)
        for b in writes:
            if b.track:
                add(b.w)
                add(b.r)
        return deps

    def _do_waits(self, eng, deps):
        wd = self.waited[eng]
        e = self.eobj[eng]
        for k, (s, v) in deps.items():
            if k == eng and (eng == "pe" or not SAME_ENGINE_SYNC):
                continue
            if wd.get(k, 0) >= v:
                continue
            wd[k] = v
            e.wait_ge(s, v)

    def op(self, eng, fn, reads=(), writes=()):
        self._do_waits(eng, self._deps(reads, writes))
        self.cnt[eng] += 1
        tok = (self.sem[eng], self.cnt[eng], self.epoch)
        fn(self.eobj[eng]).then_inc(tok[0], 1)
        self.nops += 1
        self.active[eng] = True
        for b in writes:
            b.w = {eng: tok}
            b.r = {}
        for b in reads:
            b.r[eng] = tok

    def dma(self, qn, out_b, out_ap, in_b, in_ap, **kw):
        self._do_waits(qn, self._deps([in_b], [out_b]))
        i = self.dcnt[qn]
        self.dcnt[qn] += 1
        s = self.dsem[qn][i % NDS]
        prev = 16 * (i // NDS)
        key = "d_%s%d" % (qn, i % NDS)
        e = self.eobj[qn]
        if prev > 0 and self.waited[qn].get(key, 0) < prev:
            self.waited[qn][key] = prev
            e.wait_ge(s, prev)
        e.dma_start(out=out_ap, in_=in_ap, **kw).then_inc(s, 16)
        self.nops += 1
        self.active[qn] = True
        tok = (s, prev + 16, self.epoch)
        if out_b.track:
            out_b.w = {key: tok}
            out_b.r = {}
        if in_b.track:
            in_b.r[key] = tok

    def _all_tokens(self):
        toks = {}
        for n in self.eobj:
            if self.cnt[n] > 0:
                toks[n] = (self.sem[n], self.cnt[n])
        for qn in self.dsem:
            for j in range(NDS):
                uses = (self.dcnt[qn] - j + NDS - 1) // NDS if self.dcnt[qn] > j else 0
                if uses > 0:
                    toks["d_%s%d" % (qn, j)] = (self.dsem[qn][j], 16 * uses)
        return toks

    def barrier(self):
        toks = self._all_tokens()
        rotate = all(self.active.get(n, False) for n in self.eobj)
        nxt2 = self.semsets[(self.epoch + 2) % 3]
        for n in self.eobj:
            wd = self.waited[n]
            e = self.eobj[n]
            for k, (s, v) in toks.items():
                if k == n or wd.get(k, 0) >= v:
                    continue
                wd[k] = v
                e.wait_ge(s, v)
            if rotate:
                e.sem_clear(nxt2[n])
        if not rotate:
            return
        self.epoch += 1
        self.active = {}
        self.sem = self.semsets[self.epoch % 3]
        for n in self.eobj:
            self.tot[n] += self.cnt[n]
            self.cnt[n] = 0
            for k in list(self.waited[n].keys()):
                if k in self.eobj:
                    del self.waited[n][k]


from contextlib import ExitStack


def phase_norm_T(kb, G, xsrc, t0, gsrc, hT):
    with ExitStack() as c:
        gbc = kb.sb(c, "gbc", [128, D_MODEL], F32)
        xt = [kb.sb(c, "xt%d" % i, [128, D_MODEL], F32) for i in range(2)]
        junk = kb.sb(c, "junk", [128, D_MODEL], BF16)
        xn = [kb.sb(c, "xn%d" % i, [128, D_MODEL], BF16) for i in range(2)]
        ss = [kb.sb(c, "ss%d" % i, [128, 4], F32) for i in range(2)]
        tp = [kb.ps(c, "tp%d" % i, [128, 8, 128], BF16) for i in range(2)]
        kb.dma("sp", gbc, gbc[:, :], gsrc, gsrc.t.partition_broadcast(128))
        ntp = 0
        for ti in range(8):
            x_b, n_b, s_b = xt[ti % 2], xn[ti % 2], ss[ti % 2]
            kb.dma("sp", x_b, x_b[:, :], xsrc, xsrc.t[t0 + ti * 128: t0 + (ti + 1) * 128, :])
            kb.op("act", lambda e: e.activation(out=junk[:, :], in_=x_b[:, :], func=AF.Square,
                                                accum_out=s_b[:, 0:1]), reads=[x_b], writes=[junk, s_b])
            kb.op("dve", lambda e: e.tensor_scalar(out=s_b[:, 1:2], in0=s_b[:, 0:1], scalar1=1.0 / D_MODEL,
                                                   scalar2=EPS, op0=ALU.mult, op1=ALU.add),
                  reads=[s_b], writes=[s_b])
            kb.op("act", lambda e: e.activation(out=s_b[:, 2:3], in_=s_b[:, 1:2], func=AF.Sqrt),
                  reads=[s_b], writes=[s_b])
            kb.op("dve", lambda e: e.reciprocal(out=s_b[:, 3:4], in_=s_b[:, 2:3]), reads=[s_b], writes=[s_b])
            kb.op("dve", lambda e: e.scalar_tensor_tensor(out=n_b[:, :], in0=x_b[:, :], scalar=s_b[:, 3:4],
                                                          in1=gbc[:, :], op0=ALU.mult, op1=ALU.mult),
                  reads=[x_b, s_b, gbc], writes=[n_b])
            for k0 in range(0, 32, 8):
                p_b = tp[ntp % 2]
                ntp += 1
                for kk in range(8):
                    k = k0 + kk
                    kb.op("pe", lambda e: e.transpose(out=p_b[:, kk, :], in_=n_b[:, k * 128:(k + 1) * 128],
                                                      identity=G["ident"][:, :]),
                          reads=[n_b, G["ident"]], writes=[p_b])
                if (k0 // 8) % 2 == 0:
                    kb.op("act", lambda e: e.activation(out=hT[:, k0:k0 + 8, ti * 128:(ti + 1) * 128],
                                                        in_=p_b[:, :, :], func=AF.Copy),
                          reads=[p_b], writes=[hT])
                else:
                    kb.op("dve", lambda e: e.tensor_copy(out=hT[:, k0:k0 + 8, ti * 128:(ti + 1) * 128],
                                                         in_=p_b[:, :, :]), reads=[p_b], writes=[hT])
    kb.barrier()


def stream_proj(kb, wsrc, KT, jobs, wbufs, cw=256):
    chunks = []
    cur = None
    for (c0, c1, fn) in jobs:
        if cur is not None and cur[1] == c0 and (c1 - cur[0]) <= cw:
            cur[1] = c1
            cur[2].append((c0, c1, fn))
        else:
            cur = [c0, c1, [(c0, c1, fn)]]
            chunks.append(cur)
    wv = wsrc.t.rearrange("(k p) n -> p k n", p=128)
    for ci, (a, b, fl) in enumerate(chunks):
        wb = wbufs[ci % len(wbufs)]
        kb.dma("pool", wb, wb[:, 0:KT, 0:b - a], wsrc, wv[:, :, a:b])
        for (c0, c1, fn) in fl:
            fn(wb, c0 - a)


DILS = (1, 4, 16)
L0_QKV0 = 2048
L0_ZB0 = 11264
ATT_SCALE = 128 ** -0.5


def rope_evac(kb, G, src, s_out_view_fn, tok0, xr_b, psw, rt_b, D, prow=32):
    kb.op("act", lambda e: e.activation(out=xr_b[:, :], in_=src[0], func=AF.Copy),
          reads=[src[1]], writes=[xr_b])
    kb.op("pe", lambda e: e.matmul(psw[:, :], lhsT=G["perm32"][:, :], rhs=xr_b[:, :], start=True, stop=True),
          reads=[xr_b, G["perm32"]], writes=[psw])
    kb.op("dve", lambda e: e.tensor_tensor(out=rt_b[:, 0, :], in0=xr_b[:, :],
                                           in1=G["ropeC"][:, tok0:tok0 + 512], op=ALU.mult),
          reads=[xr_b, G["ropeC"]], writes=[rt_b])
    kb.op("dve", lambda e: e.tensor_tensor(out=rt_b[:, 1, :], in0=psw[:, :],
                                           in1=G["ropeS"][:, tok0:tok0 + 512], op=ALU.mult),
          reads=[psw, G["ropeS"], rt_b], writes=[rt_b])
    ov, ob = s_out_view_fn(0, 32)
    kb.op("dve", lambda e: e.tensor_tensor(
        out=ov, in0=rt_b.t[:, 0, :].rearrange("p (m r) -> p r m", r=D),
        in1=rt_b.t[:, 1, :].rearrange("p (m r) -> p r m", r=D), op=ALU.add),
        reads=[rt_b], writes=[ob])


def layer0_inproj(kb, G, half):
    t0 = half * HALF
    with ExitStack() as c:
        hT = kb.sb(c, "hT", [128, 32, HALF], BF16)
        phase_norm_T(kb, G, G["xin"], t0, G["even_norm"], hT)
        wbufs = [kb.sb(c, "wb%d" % i, [128, 32, 256], BF16) for i in range(3)]
        pacc = [kb.ps(c, "pacc%d" % i, [128, 512], F32) for i in range(4)]
        psw = [kb.ps(c, "psw%d" % i, [32, 512], F32) for i in range(4)]
        stg = [kb.sb(c, "stg%d" % i, [128, HALF], BF16) for i in range(3)]
        vst = [kb.sb(c, "vst%d" % i, [128, 256], BF16) for i in range(3)]
        xr = [kb.sb(c, "xr%d" % i, [32, 512], F32) for i in range(4)]
        rt = [kb.sb(c, "rt%d" % i, [32, 2, 512], F32) for i in range(4)]
        st = {"np": 0, "ns": 0, "nr": 0, "nv": 0}

        def mm_cm(wb, off, ps, tc):
            for k in range(32):
                kb.op("pe", lambda e: e.matmul(ps[:, :], lhsT=wb[:, k, off:off + 128],
                                               rhs=hT[:, k, tc * 512:(tc + 1) * 512],
                                               start=(k == 0), stop=(k == 31)),
                      reads=[wb, hT], writes=[ps])

        def cm_simple(c0, func, dst, ct):
            def fn(wb, off):
                s = stg[st["ns"] % 3]
                st["ns"] += 1
                for tc in range(2):
                    ps = pacc[st["np"] % 4]
                    st["np"] += 1
                    mm_cm(wb, off, ps, tc)
                    kb.op("act", lambda e: e.activation(out=s[:, tc * 512:(tc + 1) * 512], in_=ps[:, :],
                                                        func=func), reads=[ps], writes=[s])
                kb.dma("sp", dst, dst.t[ct, :, t0:t0 + HALF], s, s[:, :])
            return (c0, c0 + 128, fn)

        def cm_rope(c0, dst, g, hd):
            D = DILS[g]
            nl = HALF // D

            def fn(wb, off):
                s = stg[st["ns"] % 3]
                st["ns"] += 1
                sv = s.t[:, :].rearrange("p (r m) -> p r m", r=D)
                for tc in range(2):
                    ps = pacc[st["np"] % 4]
                    st["np"] += 1
                    mm_cm(wb, off, ps, tc)
                    m0 = tc * 512 // D
                    for (lo, hi) in ((32, 64), (64, 128)):
                        kb.op("act", lambda e: e.activation(
                            out=sv[lo:hi, :, m0:m0 + 512 // D],
                            in_=ps.t[lo:hi, :].rearrange("p (m r) -> p r m", r=D), func=AF.Copy),
                            reads=[ps], writes=[s])
                    i = st["nr"] % 4
                    st["nr"] += 1
                    rope_evac(kb, G, (ps[0:32, :], ps), lambda lo, hi: (sv[lo:hi, :, m0:m0 + 512 // D], s),
                              t0 + tc * 512, xr[i], psw[i], rt[i], D)
                dv = dst.t[g, hd, :, :].rearrange("p (r m) -> p r m", r=D)
                kb.dma("sp", dst, dv[:, :, half * nl:(half + 1) * nl], s, sv)
            return (c0, c0 + 128, fn)

        def tm_v(c0, g, cc):
            D = DILS[g]
            nl = HALF // D

            def fn(wb, off):
                units = []
                for vt in range(8):
                    if D == 1:
                        units.append((lambda k, vt=vt: hT[:, k, vt * 128:(vt + 1) * 128], 128, t0 + vt * 128))
                    elif D == 4:
                        r, ml0 = vt // 2, 128 * (vt % 2)
                        units.append((lambda k, r=r, ml0=ml0: hT.t[:, k, :].rearrange(
                            "p (m r) -> p r m", r=4)[:, r, ml0:ml0 + 128], 128, r * 512 + half * 256 + ml0))
                    else:
                        for a in range(2):
                            r = 2 * vt + a
                            units.append((lambda k, r=r: hT.t[:, k, :].rearrange(
                                "p (m r) -> p r m", r=16)[:, r, :], 64, r * 128 + half * 64))
                for ui, (ltf, nr, row) in enumerate(units):
                    ps = pacc[st["np"] % 4]
                    st["np"] += 1
                    for k in range(32):
                        kb.op("pe", lambda e: e.matmul(ps[0:nr, 0:256], lhsT=ltf(k), rhs=wb[:, k, off:off + 256],
                                                       start=(k == 0), stop=(k == 31)),
                              reads=[wb, hT], writes=[ps])
                    vs = vst[st["nv"] % 3]
                    st["nv"] += 1
                    if ui % 2 == 0:
                        kb.op("act", lambda e: e.activation(out=vs[0:nr, :], in_=ps[0:nr, 0:256], func=AF.Copy),
                              reads=[ps], writes=[vs])
                    else:
                        kb.op("dve", lambda e: e.tensor_copy(out=vs[0:nr, :], in_=ps[0:nr, 0:256]),
                              reads=[ps], writes=[vs])
                    dv = G["v"].t[g, :, cc * 256:(cc + 1) * 256]
                    kb.dma("sp", G["v"], dv[row:row + nr, :], vs, vs[0:nr, :])
            return (c0, c0 + 256, fn)

        jobs = []
        for ct in range(8):
            jobs.append(cm_simple(ct * 128, AF.Copy, G["uT"], ct))
        for ct in range(8):
            jobs.append(cm_simple(1024 + ct * 128, AF.Silu, G["zaT"], ct))
        for g in range(3):
            base = L0_QKV0 + g * 3072
            for hd in range(8):
                jobs.append(cm_rope(base + hd * 128, G["qT"], g, hd))
            for hd in range(8):
                jobs.append(cm_rope(base + 1024 + hd * 128, G["kT"], g, hd))
            for cc in range(4):
                jobs.append(tm_v(base + 2048 + cc * 256, g, cc))
        for ct in range(8):
            jobs.append(cm_simple(L0_ZB0 + ct * 128, AF.Silu, G["zbT"], ct))
        stream_proj(kb, G["even_w_in"], 32, jobs, wbufs)
    kb.barrier()


IN_SPECS = [
    ("xin", [SEQ, D_MODEL]),
    ("even_norm", [D_MODEL]), ("even_w_in", [D_MODEL, 12288]),
    ("s5_lambda_re", [64, 64]), ("s5_lambda_im", [64, 64]), ("s5_log_step", [64]),
    ("s5_b_re", [64, 64, 16]), ("s5_b_im", [64, 64, 16]), ("s5_c_re", [64, 16, 64]), ("s5_c_im", [64, 16, 64]),
    ("s5_d", [1024]), ("s5_glu_w", [1024, 1024]), ("s5_glu_b", [1024]), ("even_w_out", [2048, D_MODEL]),
    ("odd_norm", [D_MODEL]), ("odd_w_in", [D_MODEL, 6336]), ("mla_q_norm", [1536]), ("mla_kv_norm", [512]),
    ("idx_k_norm", [128]), ("mla_w_uq", [1536, 4096]), ("mla_w_uk", [512, 32, 96]), ("mla_w_uv", [512, 32, 128]),
    ("idx_w_q", [1536, 4096]), ("odd_w_out", [4096, D_MODEL]), ("final_norm", [D_MODEL]),
    ("c_ident", [128, 128]), ("c_perm32", [32, 32]), ("c_ropeC", [32, SEQ]), ("c_ropeS", [32, SEQ]),
    ("c_maskT", [128, 2, 128]), ("c_rowmask4", [128, 4]), ("c_negmask", [128, 128]),
]


def host_consts():
    half = 16
    inv = (500000.0 ** (-np.arange(0, 32, 2, dtype=np.float32) / np.float32(32))).astype(np.float32)
    ang = np.arange(SEQ, dtype=np.float32)[None, :] * inv[:, None]
    cos = np.cos(ang).astype(np.float32)
    sin = np.sin(ang).astype(np.float32)
    ropeC = np.concatenate([cos, cos], 0)
    ropeS = np.concatenate([-sin, sin], 0)
    perm = np.zeros((32, 32), np.float32)
    for m in range(32):
        perm[(m + 16) % 32, m] = 1.0
    k = np.arange(128)[:, None]
    q = np.arange(128)[None, :]
    maskT = np.stack([(k >= q), (k <= q)], 1).astype(np.float32)
    rowmask4 = (np.arange(128)[:, None] // 32 == np.arange(4)[None, :]).astype(np.float32)
    negmask = np.where(q.T >= k.T, 0.0, -1.0e30).astype(np.float32)
    return {"c_ident": np.eye(128, dtype=np.float32), "c_perm32": perm, "c_ropeC": ropeC, "c_ropeS": ropeS,
            "c_maskT": maskT, "c_rowmask4": rowmask4, "c_negmask": negmask}


def build_program(stop_after=None, debug_out=(), dbg=None, only=None):
    nc = bass.Bass("TRN2", target_bir_lowering=False)
    with ExitStack() as stack:
        kb = KB(nc, stack)
        G = {"_dbg": dbg, "_eng": "dve"}
        for name, shape in IN_SPECS:
            b = Buf(name, nc.dram_tensor(name, list(shape), F32, kind="ExternalInput").ap())
            b.track = False
            G[name] = b

        def scratch(name, shape, dt):
            kind = "ExternalOutput" if name in debug_out else "Internal"
            G[name] = kb.dram(name, shape, dt, kind=kind)

        scratch("uT", [8, 128, SEQ], BF16)
        scratch("zaT", [8, 128, SEQ], BF16)
        scratch("qT", [3, 8, 128, SEQ], BF16)
        scratch("kT", [3, 8, 128, SEQ], BF16)
        scratch("v", [3, SEQ, 1024], BF16)
        scratch("zbT", [8, 128, SEQ], BF16)
        scratch("mixT", [16, 128, SEQ], BF16)
        scratch("x1", [SEQ, D_MODEL], F32)
        scratch("cqT", [12, 128, SEQ], BF16)
        scratch("ckvT", [4, 128, SEQ], BF16)
        scratch("krT", [32, SEQ], BF16)
        scratch("kiT", [128, SEQ], BF16)
        scratch("wi", [SEQ, 32], F32)
        scratch("sgT", [32, 128, SEQ], BF16)
        scratch("q1T", [32, 128, SEQ], BF16)
        scratch("qiT", [32, 128, SEQ], BF16)
        scratch("selT", [16, 128, SEQ], BF16)
        scratch("ogT", [32, 128, SEQ], BF16)
        scratch("x2", [SEQ, D_MODEL], F32)
        G["out"] = kb.dram("out", [SEQ, D_MODEL], F32, kind="ExternalOutput")

        G["ident"] = kb.sb(stack, "ident", [128, 128], BF16)
        G["identf"] = kb.sb(stack, "identf", [128, 128], F32)
        G["perm32"] = kb.sb(stack, "perm32", [32, 32], F32)
        G["ropeC"] = kb.sb(stack, "ropeC", [32, SEQ], F32)
        G["ropeS"] = kb.sb(stack, "ropeS", [32, SEQ], F32)
        G["maskT"] = kb.sb(stack, "maskT", [128, 2, 128], BF16)
        G["rowmask4"] = kb.sb(stack, "rowmask4", [128, 4], F32)
        G["onesb"] = kb.sb(stack, "onesb", [128, 128], BF16)
        G["onesf"] = kb.sb(stack, "onesf", [128, 128], F32)
        kb.dma("pool", G["ident"], G["ident"][:, :], G["c_ident"], G["c_ident"].t[:, :])
        kb.dma("sp", G["identf"], G["identf"][:, :], G["c_ident"], G["c_ident"].t[:, :])
        kb.dma("sp", G["perm32"], G["perm32"][:, :], G["c_perm32"], G["c_perm32"].t[:, :])
        kb.dma("sp", G["ropeC"], G["ropeC"][:, :], G["c_ropeC"], G["c_ropeC"].t[:, :])
        kb.dma("sp", G["ropeS"], G["ropeS"][:, :], G["c_ropeS"], G["c_ropeS"].t[:, :])
        kb.dma("pool", G["maskT"], G["maskT"][:, :, :], G["c_maskT"], G["c_maskT"].t[:, :, :])
        kb.dma("sp", G["rowmask4"], G["rowmask4"][:, :], G["c_rowmask4"], G["c_rowmask4"].t[:, :])
        kb.op("pool", lambda e: e.memset(G["onesb"][:, :], 1.0), writes=[G["onesb"]])
        kb.op("pool", lambda e: e.memset(G["onesf"][:, :], 1.0), writes=[G["onesf"]])

        phases = [
            ("l0_in0", lambda: layer0_inproj(kb, G, 0)),
            ("l0_in1", lambda: layer0_inproj(kb, G, 1)),
            ("l0_s5", lambda: layer0_s5(kb, G)),
            ("l0_attn", lambda: layer0_attn(kb, G)),
            ("l0_out0", lambda: outproj(kb, G, 0, "mixT", 16, "even_w_out", "xin", "x1")),
            ("l0_out1", lambda: outproj(kb, G, 1, "mixT", 16, "even_w_out", "xin", "x1")),
            ("l1_in0", lambda: layer1_inproj(kb, G, 0)),
            ("l1_in1", lambda: layer1_inproj(kb, G, 1)),
            ("l1_q0", lambda: layer1_qproj(kb, G, 0)),
            ("l1_q1", lambda: layer1_qproj(kb, G, 1)),
            ("l1_idx", lambda: layer1_index(kb, G)),
            ("l1_attn", lambda: layer1_attn(kb, G)),
            ("l1_out0", lambda: outproj(kb, G, 0, "ogT", 32, "odd_w_out", "x1", "x2")),
            ("l1_out1", lambda: outproj(kb, G, 1, "ogT", 32, "odd_w_out", "x1", "x2")),
            ("final", lambda: final_norm(kb, G)),
        ]
        for name, fn in phases:
            if only is None or name in only:
                fn()
            if stop_after == name:
                break
        kb.barrier()
    return nc, kb


def make_in_maps(inputs, cores):
    consts = host_consts()
    shared = {}
    for name, shape in IN_SPECS:
        if name == "xin" or name.startswith("c_"):
            continue
        a = np.asarray(inputs[name], dtype=np.float32)
        if name != "final_norm":
            a = a[0]
        shared[name] = np.ascontiguousarray(a).reshape(shape)
    x = np.asarray(inputs["x"], dtype=np.float32)
    maps = []
    for c in cores:
        m = dict(shared)
        m.update(consts)
        m["xin"] = np.ascontiguousarray(x[c % BATCH])
        maps.append(m)
    return maps


def kernel(**inputs):
    nc, kb = build_program()
    cores = list(range(8))
    res = run_bass_kernel_spmd(nc, make_in_maps(inputs, cores), core_ids=cores)
    out = np.stack([np.asarray(res.results[b]["out"], dtype=np.float32) for b in range(BATCH)], 0)
    return out


def TT(kb, eng, out, a, b, op):
    kb.op(eng, lambda e: e.tensor_tensor(out=out[0], in0=a[0], in1=b[0], op=op),
          reads=[a[1], b[1]], writes=[out[1]])


def TS(kb, eng, out, a, s1, op0, s2=None, op1=None):
    rd = [a[1]]
    s1v, s2v = s1, s2
    if isinstance(s1, tuple):
        rd.append(s1[1])
        s1v = s1[0]
    if isinstance(s2, tuple):
        rd.append(s2[1])
        s2v = s2[0]
    if op1 is None:
        kb.op(eng, lambda e: e.tensor_scalar(out=out[0], in0=a[0], scalar1=s1v, scalar2=None, op0=op0),
              reads=rd, writes=[out[1]])
    else:
        kb.op(eng, lambda e: e.tensor_scalar(out=out[0], in0=a[0], scalar1=s1v, scalar2=s2v, op0=op0, op1=op1),
              reads=rd, writes=[out[1]])


def ACTF(kb, out, a, func, scale=None, bias=None):
    rd = [a[1]]
    kw = {}
    if scale is not None:
        if isinstance(scale, tuple):
            rd.append(scale[1])
            kw["scale"] = scale[0]
        else:
            kw["scale"] = scale
    if bias is not None:
        rd.append(bias[1])
        kw["bias"] = bias[0]
    kb.op("act", lambda e: e.activation(out=out[0], in_=a[0], func=func, **kw), reads=rd, writes=[out[1]])


def W(b, ap=None):
    if ap is None:
        nd = len(b.t.shape)
        ap = b.t[tuple(slice(None) for _ in range(nd))]
    return (ap, b)


def layer0_s5(kb, G):
    TWO_PI = 2.0 * math.pi
    with ExitStack() as c:
        def T(name, shape=(128, 32), dt=F32):
            return kb.sb(c, name, list(shape), dt)
        EBre = T("EBre", [128, 32, 128], BF16)
        EBim = T("EBim", [128, 32, 128], BF16)
        ECre = T("ECre", [128, 32, 128], BF16)
        ECim = T("ECim", [128, 32, 128], BF16)
        Ec = T("Ecos", [128, 32, 128])
        Es = T("Esin", [128, 32, 128])
        mag = T("mag")
        dsk = T("dsk", [128, 8])
        glb = T("glb", [128, 8])
        gw = T("gw", [128, 8, 1024], BF16)
        carry = [T("carry%d" % i, [128, 4, 2]) for i in range(8)]
        kb.dma("pool", gw, gw[:, :, :], G["s5_glu_w"], G["s5_glu_w"].t.rearrange("(k p) n -> p k n", p=128))
        for cb_ in carry:
            kb.op("pool", lambda e: e.memset(cb_[:, :, :], 0.0), writes=[cb_])

        with ExitStack() as c2:
            def T2(name, shape=(128, 32), dt=F32):
                return kb.sb(c2, name, list(shape), dt)
            lamR, lamI, dtl = T2("lamR"), T2("lamI"), T2("dtl")
            PT = T2("PT", [32, 3, 128])
            LS = T2("LS", [32, 2])
            PD = T2("PD", [8, 2, 128])
            kb.dma("sp", PT, PT[:, 0, :], G["s5_lambda_re"],
                   G["s5_lambda_re"].t.rearrange("(pr g2) p -> pr (g2 p)", g2=2))
            kb.dma("sp", PT, PT[:, 1, :], G["s5_lambda_im"],
                   G["s5_lambda_im"].t.rearrange("(pr g2) p -> pr (g2 p)", g2=2))
            kb.dma("sp", LS, LS[:, :], G["s5_log_step"], G["s5_log_step"].t.rearrange("(pr g2) -> pr g2", g2=2))
            kb.dma("sp", PD, PD[:, 0, :], G["s5_d"], G["s5_d"].t.rearrange("(ct c) -> ct c", c=128))
            kb.dma("sp", PD, PD[:, 1, :], G["s5_glu_b"], G["s5_glu_b"].t.rearrange("(ct c) -> ct c", c=128))
            kb.op("dve", lambda e: e.tensor_copy(
                out=PT.t[:, 2, :].rearrange("pr (g2 p) -> pr g2 p", g2=2),
                in_=LS.t[:, :].unsqueeze(2).broadcast_to([32, 2, 64])), reads=[LS, PT], writes=[PT])
            ppt = kb.ps(c2, "ppt", [128, 3, 32], F32)
            ppd = kb.ps(c2, "ppd", [128, 2, 8], F32)
            for j in range(3):
                kb.op("pe", lambda e: e.transpose(out=ppt[:, j, :], in_=PT[:, j, :], identity=G["identf"][0:32, 0:32]),
                      reads=[PT, G["identf"]], writes=[ppt])
            for j in range(2):
                kb.op("pe", lambda e: e.transpose(out=ppd[:, j, :], in_=PD[:, j, :], identity=G["identf"][0:8, 0:8]),
                      reads=[PD, G["identf"]], writes=[ppd])
            for j, dst in enumerate((lamR, lamI, dtl)):
                kb.op("dve", lambda e: e.tensor_copy(out=dst[:, :], in_=ppt[:, j, :]), reads=[ppt], writes=[dst])
            for j, dst in enumerate((dsk, glb)):
                kb.op("dve", lambda e: e.tensor_copy(out=dst[:, :], in_=ppd[:, j, :]), reads=[ppd], writes=[dst])
            lr, dt_, xx, pp, ang, ff, kf, th, ath = (T2(n) for n in
                                                     ("lr", "dt", "xx", "pp", "ang", "ff", "kf", "th", "ath"))
            ki = T2("ki", dt=mybir.dt.int32)
            cth, sth, ar, ai, den, rden, nr, t1, t2, cr, ci = (T2(n) for n in (
                "cth", "sth", "ar", "ai", "den", "rden", "nr", "t1", "t2", "cr", "ci"))
            TS(kb, "dve", W(lr), W(lamR), -1e-4, ALU.min)
            TS(kb, "dve", W(xx), W(dtl), 0.125, ALU.mult)
            TS(kb, "dve", W(pp), W(xx), 1.0 / 12.0, ALU.mult, 1.0, ALU.add)
            for kk in range(11, 0, -1):
                TT(kb, "dve", W(pp), W(pp), W(xx), ALU.mult)
                TS(kb, "dve", W(pp), W(pp), 1.0 / kk, ALU.mult, 1.0, ALU.add)
            TT(kb, "dve", W(dt_), W(pp), W(pp), ALU.mult)
            TT(kb, "dve", W(dt_), W(dt_), W(dt_), ALU.mult)
            TT(kb, "dve", W(dt_), W(dt_), W(dt_), ALU.mult)
            TT(kb, "dve", W(xx), W(lr), W(dt_), ALU.mult)
            TS(kb, "dve", W(pp), W(xx), 1.0 / 7.0, ALU.mult, 1.0, ALU.add)
            for kk in (6, 5, 4, 3, 2, 1):
                TT(kb, "dve", W(pp), W(pp), W(xx), ALU.mult)
                TS(kb, "dve", W(pp), W(pp), 1.0 / kk, ALU.mult, 1.0, ALU.add)
            kb.op("dve", lambda e: e.tensor_copy(out=mag[:, :], in_=pp[:, :]), reads=[pp], writes=[mag])
            TT(kb, "dve", W(ang), W(lamI), W(dt_), ALU.mult)
            TS(kb, "dve", W(ff), W(ang), 1.0 / TWO_PI, ALU.mult)
            kb.op("dve", lambda e: e.tensor_copy(out=ki[:, :], in_=ff[:, :]), reads=[ff], writes=[ki])
            kb.op("dve", lambda e: e.tensor_copy(out=kf[:, :], in_=ki[:, :]), reads=[ki], writes=[kf])
            TT(kb, "dve", W(ff), W(ff), W(kf), ALU.subtract)
            TS(kb, "dve", W(th), W(ff), TWO_PI, ALU.mult, math.pi, ALU.min)
            TS(kb, "dve", W(th), W(th), -math.pi, ALU.max)
            ACTF(kb, W(sth), W(th), AF.Sin)
            ACTF(kb, W(ath), W(th), AF.Abs)
            TS(kb, "dve", W(ath), W(ath), -1.0, ALU.mult, math.pi / 2.0, ALU.add)
            ACTF(kb, W(cth), W(ath), AF.Sin)
            TT(kb, "dve", W(ar), W(mag), W(cth), ALU.mult)
            TT(kb, "dve", W(ai), W(mag), W(sth), ALU.mult)
            TT(kb, "dve", W(den), W(lr), W(lr), ALU.mult)
            TT(kb, "dve", W(t1), W(lamI), W(lamI), ALU.mult)
            TT(kb, "dve", W(den), W(den), W(t1), ALU.add)
            kb.op("dve", lambda e: e.reciprocal(out=rden[:, :], in_=den[:, :]), reads=[den], writes=[rden])
            TS(kb, "dve", W(nr), W(ar), -1.0, ALU.add)
            TT(kb, "dve", W(t1), W(nr), W(lr), ALU.mult)
            TT(kb, "dve", W(t2), W(ai), W(lamI), ALU.mult)
            TT(kb, "dve", W(t1), W(t1), W(t2), ALU.add)
            TT(kb, "dve", W(cr), W(t1), W(rden), ALU.mult)
            TT(kb, "dve", W(t1), W(ai), W(lr), ALU.mult)
            TT(kb, "dve", W(t2), W(nr), W(lamI), ALU.mult)
            TT(kb, "dve", W(t1), W(t1), W(t2), ALU.subtract)
            TT(kb, "dve", W(ci), W(t1), W(rden), ALU.mult)
            bre, bim = T2("bre", [128, 32, 16]), T2("bim", [128, 32, 16])
            for (dst, nm) in ((bre, "s5_b_re"), (bim, "s5_b_im")):
                kb.dma("sp", dst, dst[:, :, :], G[nm], G[nm].t.rearrange("(pr g2) p j -> (g2 p) pr j", g2=2))
            u1, u2 = T2("u1", [128, 32, 16]), T2("u2", [128, 32, 16])
            X2r, X2i = T2("X2r", [128, 32, 32], BF16), T2("X2i", [128, 32, 32], BF16)
            kb.op("pool", lambda e: e.memset(X2r[:, :, :], 0.0), writes=[X2r])
            kb.op("pool", lambda e: e.memset(X2i[:, :, :], 0.0), writes=[X2i])
            crb = (cr.t[:, :].unsqueeze(2).broadcast_to([128, 32, 16]), cr)
            cib = (ci.t[:, :].unsqueeze(2).broadcast_to([128, 32, 16]), ci)
            for (X2, pa, pb, opc) in ((X2r, bre, bim, ALU.subtract), (X2i, bim, bre, ALU.add)):
                TT(kb, "dve", W(u1), W(pa), crb, ALU.mult)
                TT(kb, "dve", W(u2), W(pb), cib, ALU.mult)
                for g2 in range(2):
                    lo, hi = 64 * g2, 64 * g2 + 64
                    TT(kb, "dve", (X2[lo:hi, :, 16 * g2:16 * g2 + 16], X2), (u1[lo:hi, :, :], u1),
                       (u2[lo:hi, :, :], u2), opc)
            ptp = [kb.ps(c2, "s5tp%d" % i, [128, 128], BF16) for i in range(2)]
            ntp = 0
            for (X2, EB) in ((X2r, EBre), (X2i, EBim)):
                for ct in range(8):
                    p_b = ptp[ntp % 2]
                    ntp += 1
                    kb.op("pe", lambda e: e.transpose(out=p_b[:, :], in_=X2.t.rearrange("q pr x -> q (pr x)")[:, ct * 128:(ct + 1) * 128],
                                                      identity=G["ident"][:, :]),
                          reads=[X2, G["ident"]], writes=[p_b])
                    for a in range(4):
                        TS(kb, "dve", (EB[:, 4 * ct + a, :], EB), W(p_b),
                           (G["rowmask4"][:, a:a + 1], G["rowmask4"]), ALU.mult)
            Cn = T2("Cn", [128, 8, 2, 64])
            Cb = T2("Cb", [128, 8, 128], BF16)
            for (nm, EC, sc) in (("s5_c_re", ECre, 1.0), ("s5_c_im", ECim, -1.0)):
                srcv = G[nm].t.rearrange("g j p -> (g j) p").rearrange("(ct c) p -> c ct p", c=128)
                for dup in range(2):
                    kb.dma("sp", Cn, Cn[:, :, dup, :], G[nm], srcv)
                kb.op("act", lambda e: e.activation(out=Cb[:, :, :], in_=Cn.t.rearrange("c ct d p -> c ct (d p)"),
                                                    func=AF.Copy, scale=sc), reads=[Cn], writes=[Cb])
                kb.op("pool", lambda e: e.memset(EC[:, :, :], 0.0), writes=[EC])
                for ct in range(8):
                    p_b = ptp[ntp % 2]
                    ntp += 1
                    kb.op("pe", lambda e: e.transpose(out=p_b[:, :], in_=Cb[:, ct, :], identity=G["ident"][:, :]),
                          reads=[Cb, G["ident"]], writes=[p_b])
                    for a in range(4):
                        for g2 in range(2):
                            lo, hi = 64 * g2, 64 * g2 + 64
                            c0 = 32 * a + 16 * g2
                            kb.op("act" if g2 == 0 else "dve",
                                  (lambda e: e.activation(out=EC[lo:hi, 4 * ct + a, c0:c0 + 16],
                                                          in_=p_b[lo:hi, c0:c0 + 16], func=AF.Copy)) if g2 == 0 else
                                  (lambda e: e.tensor_copy(out=EC[lo:hi, 4 * ct + a, c0:c0 + 16],
                                                           in_=p_b[lo:hi, c0:c0 + 16])),
                                  reads=[p_b], writes=[EC])
            wc, ws, wt = T2("wc"), T2("ws"), T2("wt")
            m1, m2 = T2("m1", [128, 32, 64]), T2("m2", [128, 32, 64])
            kb.op("dve", lambda e: e.tensor_copy(out=wc[:, :], in_=cth[:, :]), reads=[cth], writes=[wc])
            kb.op("dve", lambda e: e.tensor_copy(out=ws[:, :], in_=sth[:, :]), reads=[sth], writes=[ws])
            kb.op("dve", lambda e: e.tensor_copy(out=Ec[:, :, 0], in_=cth[:, :]), reads=[cth], writes=[Ec])
            kb.op("dve", lambda e: e.tensor_copy(out=Es[:, :, 0], in_=sth[:, :]), reads=[sth], writes=[Es])
            n = 1
            while n < 128:
                wcb = (wc.t[:, :].unsqueeze(2).broadcast_to([128, 32, n]), wc)
                wsb = (ws.t[:, :].unsqueeze(2).broadcast_to([128, 32, n]), ws)
                TT(kb, "dve", (m1[:, :, 0:n], m1), (Ec[:, :, 0:n], Ec), wcb, ALU.mult)
                TT(kb, "dve", (m2[:, :, 0:n], m2), (Es[:, :, 0:n], Es), wsb, ALU.mult)
                TT(kb, "dve", (Ec[:, :, n:2 * n], Ec), (m1[:, :, 0:n], m1), (m2[:, :, 0:n], m2), ALU.subtract)
                TT(kb, "dve", (m1[:, :, 0:n], m1), (Ec[:, :, 0:n], Ec), wsb, ALU.mult)
                TT(kb, "dve", (m2[:, :, 0:n], m2), (Es[:, :, 0:n], Es), wcb, ALU.mult)
                TT(kb, "dve", (Es[:, :, n:2 * n], Es), (m1[:, :, 0:n], m1), (m2[:, :, 0:n], m2), ALU.add)
                TT(kb, "dve", W(wt), W(wc), W(ws), ALU.mult)
                TT(kb, "dve", W(wc), W(wc), W(wc), ALU.mult)
                TT(kb, "dve", W(ws), W(ws), W(ws), ALU.mult)
                TT(kb, "dve", W(wc), W(wc), W(ws), ALU.subtract)
                TS(kb, "dve", W(ws), W(wt), 2.0, ALU.mult)
                n *= 2
        kb.barrier()
        if G.get("_dbg") == "s5prep":
            for nm, tl in (("EBre", EBre), ("EBim", EBim), ("ECre", ECre), ("ECim", ECim), ("Ecos", Ec), ("Esin", Es)):
                d = kb.dram("dbg_" + nm, list(tl.t.shape), tl.t.dtype, kind="ExternalOutput")
                kb.dma("sp", d, d.t[:, :, :], tl, tl[:, :, :])
            for nm, tl in (("mag", mag), ("dsk", dsk), ("glb", glb)):
                d = kb.dram("dbg_" + nm, list(tl.t.shape), tl.t.dtype, kind="ExternalOutput")
                kb.dma("sp", d, d.t[:, :], tl, tl[:, :])
            kb.barrier()
            return

        uTc = [T("uTc%d" % i, [128, 8, 512], BF16) for i in range(2)]
        zac = [T("zac%d" % i, [128, 8, 512], BF16) for i in range(2)]
        Yt = [T("Yt%d" % i, [128, 8, 512], BF16) for i in range(2)]
        psR = [kb.ps(c, "psR%d" % i, [128, 4, 128]) for i in range(2)]
        psI = [kb.ps(c, "psI%d" % i, [128, 4, 128]) for i in range(2)]
        psY = [kb.ps(c, "psY%d" % i, [128, 128]) for i in range(2)]
        psG = [kb.ps(c, "psG%d" % i, [128, 512]) for i in range(2)]
        NB = 2
        A = [[T("A%d_%d" % (j, i), [128, 4, 128]) for j in range(4)] for i in range(NB)]
        Rin = [[T("Rin%d_%d" % (j, i), [128, 4, 128]) for j in range(2)] for i in range(NB)]
        Rs = [[T("Rs%d_%d" % (j, i), [128, 4, 128]) for j in range(2)] for i in range(NB)]
        Bm = A
        S = [[T("S%d_%d" % (j, i), [128, 4, 128]) for j in range(2)] for i in range(NB)]
        Sb = [[T("Sb%d_%d" % (j, i), [128, 4, 128], BF16) for j in range(2)] for i in range(NB)]
        y1 = [T("y1_%d" % i, [128, 128]) for i in range(2)]
        sg = [T("sg%d" % i, [128, 512]) for i in range(2)]
        a1 = [T("a1_%d" % i, [128, 512]) for i in range(2)]
        ost = [T("ost%d" % i, [128, 512], BF16) for i in range(3)]
        it = 0
        ny = 0
        ng = 0
        for blk in range(4):
            ub, zb_, yb = uTc[blk % 2], zac[blk % 2], Yt[blk % 2]
            tsl = slice(blk * 512, (blk + 1) * 512)
            kb.dma("sp", ub, ub[:, :, :], G["uT"], G["uT"].t[:, :, tsl].rearrange("ct p t -> p ct t"))
            kb.dma("sp", zb_, zb_[:, :, :], G["zaT"], G["zaT"].t[:, :, tsl].rearrange("ct p t -> p ct t"))
            its = [(ch, pg) for ch in range(4) for pg in range(8)]

            def s1(idx):
                ch, pg = its[idx]
                csl = slice(ch * 128, (ch + 1) * 128)
                i = idx % NB
                pR, pI = psR[i], psI[i]
                Ec4, Es4 = (Ec[:, 4 * pg:4 * pg + 4, :], Ec), (Es[:, 4 * pg:4 * pg + 4, :], Es)
                for a in range(4):
                    pr = 4 * pg + a
                    kb.op("pe", lambda e: e.matmul(pR[:, a, :], lhsT=EBre[:, pr, :], rhs=ub[:, pg, csl],
                                                   start=True, stop=True), reads=[EBre, ub], writes=[pR])
                for a in range(4):
                    pr = 4 * pg + a
                    kb.op("pe", lambda e: e.matmul(pI[:, a, :], lhsT=EBim[:, pr, :], rhs=ub[:, pg, csl],
                                                   start=True, stop=True), reads=[EBim, ub], writes=[pI])
                A0, A1, A2, A3 = A[i]
                TT(kb, "dve", W(A0), W(pR), Ec4, ALU.mult)
                TT(kb, "dve", W(A1), W(pI), Es4, ALU.mult)
                TT(kb, "dve", W(A2), W(pI), Ec4, ALU.mult)
                TT(kb, "dve", W(A3), W(pR), Es4, ALU.mult)
                TT(kb, "pool", W(Rin[i][0]), W(A0), W(A1), ALU.add)
                TT(kb, "pool", W(Rin[i][1]), W(A2), W(A3), ALU.subtract)

            def s2(idx):
                ch, pg = its[idx]
                i = idx % NB
                Ec4, Es4 = (Ec[:, 4 * pg:4 * pg + 4, :], Ec), (Es[:, 4 * pg:4 * pg + 4, :], Es)
                for a in range(4):
                    pr = 4 * pg + a
                    for j in range(2):
                        kb.op("dve", lambda e: e.tensor_tensor_scan(
                            out=Rs[i][j][:, a, :], data0=mag.t[:, pr:pr + 1].to_broadcast([128, 128]),
                            data1=Rin[i][j][:, a, :], initial=carry[pg][:, a, j:j + 1],
                            op0=ALU.mult, op1=ALU.add), reads=[mag, Rin[i][j], carry[pg]], writes=[Rs[i][j]])
                B0, B1, B2, B3 = Bm[i]
                TT(kb, "pool", W(B0), W(Rs[i][0]), Ec4, ALU.mult)
                TT(kb, "pool", W(B1), W(Rs[i][1]), Es4, ALU.mult)
                TT(kb, "pool", W(B2), W(Rs[i][0]), Es4, ALU.mult)
                TT(kb, "pool", W(B3), W(Rs[i][1]), Ec4, ALU.mult)

            def s3(idx):
                ch, pg = its[idx]
                i = idx % NB
                B0, B1, B2, B3 = Bm[i]
                TT(kb, "dve", W(S[i][0]), W(B0), W(B1), ALU.subtract)
                TT(kb, "dve", W(S[i][1]), W(B2), W(B3), ALU.add)
                for j in range(2):
                    kb.op("act", lambda e: e.activation(out=carry[pg][:, :, j], in_=S[i][j][:, :, 127],
                                                        func=AF.Copy), reads=[S[i][j]], writes=[carry[pg]])
                    kb.op("act", lambda e: e.activation(out=Sb[i][j][:, :, :], in_=S[i][j][:, :, :],
                                                        func=AF.Copy), reads=[S[i][j]], writes=[Sb[i][j]])
                pY = psY[idx % 2]
                for a in range(4):
                    pr = 4 * pg + a
                    kb.op("pe", lambda e: e.matmul(pY[:, :], lhsT=ECre[:, pr, :], rhs=Sb[i][0][:, a, :],
                                                   start=(a == 0), stop=False), reads=[ECre, Sb[i][0]], writes=[pY])
                    kb.op("pe", lambda e: e.matmul(pY[:, :], lhsT=ECim[:, pr, :], rhs=Sb[i][1][:, a, :],
                                                   start=False, stop=(a == 3)), reads=[ECim, Sb[i][1]], writes=[pY])

            def s4(idx):
                ch, pg = its[idx]
                csl = slice(ch * 128, (ch + 1) * 128)
                pY = psY[idx % 2]
                yy = y1[idx % 2]
                kb.op("dve", lambda e: e.scalar_tensor_tensor(out=yy[:, :], in0=ub[:, pg, csl],
                                                              scalar=dsk[:, pg:pg + 1], in1=pY[:, :],
                                                              op0=ALU.mult, op1=ALU.add),
                      reads=[ub, dsk, pY], writes=[yy])
                kb.op("act", lambda e: e.activation(out=yb[:, pg, csl], in_=yy[:, :], func=AF.Gelu_apprx_tanh),
                      reads=[yy], writes=[yb])

            nit = len(its)
            for step in range(nit + 3):
                if 0 <= step - 3 < nit:
                    s4(step - 3)
                if 0 <= step - 2 < nit:
                    s3(step - 2)
                if 0 <= step - 1 < nit:
                    s2(step - 1)
                if step < nit:
                    s1(step)
            if G.get("_dbg") in ("s5a", "s5b", "s5c", "s5d"):
                continue
            for co in range(8):
                pG = psG[ng % 2]
                sgb, a1b = sg[ng % 2], a1[ng % 2]
                ob = ost[ng % 3]
                ng += 1
                for ct in range(8):
                    kb.op("pe", lambda e: e.matmul(pG[:, :], lhsT=gw[:, ct, co * 128:(co + 1) * 128], rhs=yb[:, ct, :],
                                                   start=(ct == 0), stop=(ct == 7)), reads=[gw, yb], writes=[pG])
                kb.op("act", lambda e: e.activation(out=sgb[:, :], in_=pG[:, :], func=AF.Sigmoid,
                                                    bias=glb[:, co:co + 1]), reads=[pG, glb], writes=[sgb])
                if G.get("_dbg") == "s5e":
                    continue
                TT(kb, "pool", W(a1b), W(sgb), (yb[:, co, :], yb), ALU.mult)
                TT(kb, G.get("_eng", "pool"), W(ob), W(a1b), (zb_[:, co, :], zb_), ALU.mult)
                if G.get("_dbg") == "s5f":
                    continue
                kb.dma("sp", G["mixT"], G["mixT"].t[co, :, tsl], ob, ob[:, :])
    kb.barrier()


def layer0_attn(kb, G):
    with ExitStack() as c:
        QT = [[kb.sb(c, "QT%d_%d" % (g, i), [128, SEQ], BF16) for g in range(3)] for i in range(2)]
        KT = [[kb.sb(c, "KT%d_%d" % (g, i), [128, SEQ], BF16) for g in range(3)] for i in range(2)]
        VV = [[kb.sb(c, "VV%d_%d" % (g, i), [128, 16, 128], BF16) for g in range(3)] for i in range(2)]
        zbh = [kb.sb(c, "zbh%d" % i, [128, SEQ], BF16) for i in range(2)]
        Oacc = [kb.sb(c, "Oacc%d" % i, [128, SEQ], F32) for i in range(2)]
        Dacc = [kb.sb(c, "Dacc%d" % i, [1, SEQ], F32) for i in range(2)]
        rD = [kb.sb(c, "rD%d" % i, [1, SEQ], F32) for i in range(2)]
        ao = [kb.sb(c, "ao%d" % i, [128, SEQ], BF16) for i in range(2)]
        pT = [kb.sb(c, "pT%d" % i, [128, 2, 128], BF16) for i in range(3)]
        ps_s = [kb.ps(c, "pss%d" % i, [128, 2, 128]) for i in range(2)]
        ps_o = [kb.ps(c, "pso%d" % i, [128, 4, 128]) for i in range(2)]
        ps_d = [kb.ps(c, "psd%d" % i, [1, 4, 128]) for i in range(2)]
        ps_b = [kb.ps(c, "psb%d" % i, [128, 512]) for i in range(2)]
        nu = 0
        nb_ = 0
        nbb = 0
        for hd in range(8):
            s = hd % 2
            for g in range(3):
                kb.dma("sp", QT[s][g], QT[s][g][:, :], G["qT"], G["qT"].t[g, hd, :, :])
                kb.dma("sp", KT[s][g], KT[s][g][:, :], G["kT"], G["kT"].t[g, hd, :, :])
                kb.dma("sp", VV[s][g], VV[s][g][:, :, :], G["v"],
                       G["v"].t[g, :, hd * 128:(hd + 1) * 128].rearrange("(t p) d -> p t d", p=128))
            kb.dma("sp", zbh[s], zbh[s][:, :], G["zbT"], G["zbT"].t[hd, :, :])
            O, Dn = Oacc[s], Dacc[s]
            for g in range(3):
                Q, Kt, V = QT[s][g], KT[s][g], VV[s][g]
                for bt in range(4):
                    po, pd = ps_o[nb_ % 2], ps_d[nb_ % 2]
                    nb_ += 1
                    for j in range(4):
                        if g == 0:
                            cur = 4 * bt + j
                            prev = cur - 1 if cur > 0 else None
                        elif g == 1:
                            cur = 4 * bt + j
                            prev = cur - 1 if j > 0 else None
                        else:
                            cur = 4 * bt + j
                            prev = None
                        qs = slice(cur * 128, (cur + 1) * 128)
                        ss = ps_s[nu % 2]
                        p = pT[nu % 3]
                        nu += 1
                        kb.op("pe", lambda e: e.matmul(ss[:, 1, :], lhsT=Kt[:, qs], rhs=Q[:, qs], start=True, stop=True),
                              reads=[Kt, Q], writes=[ss])
                        lo = 1
                        if prev is not None:
                            lo = 0
                            kb.op("pe", lambda e: e.matmul(ss[:, 0, :], lhsT=Kt[:, prev * 128:(prev + 1) * 128],
                                                           rhs=Q[:, qs], start=True, stop=True),
                                  reads=[Kt, Q], writes=[ss])
                        kb.op("act", lambda e: e.activation(out=p[:, lo:2, :], in_=ss[:, lo:2, :], func=AF.Exp,
                                                            scale=ATT_SCALE), reads=[ss], writes=[p])
                        kb.op("dve", lambda e: e.tensor_tensor(out=p[:, lo:2, :], in0=p[:, lo:2, :],
                                                               in1=G["maskT"][:, lo:2, :], op=ALU.mult),
                              reads=[p, G["maskT"]], writes=[p])
                        kb.op("pe", lambda e: e.matmul(po[:, j, :], lhsT=V[:, cur, :], rhs=p[:, 1, :],
                                                       start=True, stop=(prev is None)), reads=[V, p], writes=[po])
                        if prev is not None:
                            kb.op("pe", lambda e: e.matmul(po[:, j, :], lhsT=V[:, prev, :], rhs=p[:, 0, :],
                                                           start=False, stop=True), reads=[V, p], writes=[po])
                        kb.op("pe", lambda e: e.matmul(pd[0:1, j, :], lhsT=G["onesb"][:, 0:1], rhs=p[:, 1, :],
                                                       start=True, stop=(prev is None)), reads=[G["onesb"], p], writes=[pd])
                        if prev is not None:
                            kb.op("pe", lambda e: e.matmul(pd[0:1, j, :], lhsT=G["onesb"][:, 0:1], rhs=p[:, 0, :],
                                                           start=False, stop=True), reads=[G["onesb"], p], writes=[pd])
                    pof = po.t.rearrange("p a q -> p (a q)")
                    pdf = pd.t.rearrange("p a q -> p (a q)")
                    if g == 0:
                        cs = slice(bt * 512, (bt + 1) * 512)
                        kb.op("act", lambda e: e.activation(out=O[:, cs], in_=pof, func=AF.Copy), reads=[po], writes=[O])
                        kb.op("act", lambda e: e.activation(out=Dn[0:1, cs], in_=pdf, func=AF.Copy),
                              reads=[pd], writes=[Dn])
                    elif g == 1:
                        ov = O.t.rearrange("p (m r) -> p r m", r=4)[:, bt, :]
                        dv = Dn.t.rearrange("p (m r) -> p r m", r=4)[:, bt, :]
                        kb.op("dve", lambda e: e.tensor_tensor(out=ov, in0=pof, in1=ov, op=ALU.add),
                              reads=[po, O], writes=[O])
                        kb.op("dve", lambda e: e.tensor_tensor(out=dv, in0=pdf, in1=dv, op=ALU.add),
                              reads=[pd, Dn], writes=[Dn])
                    else:
                        ov = O.t.rearrange("p (m r) -> p r m", r=16)[:, 4 * bt:4 * bt + 4, :]
                        dv = Dn.t.rearrange("p (m r) -> p r m", r=16)[:, 4 * bt:4 * bt + 4, :]
                        kb.op("dve", lambda e: e.tensor_tensor(out=ov, in0=po[:, :, :], in1=ov, op=ALU.add),
                              reads=[po, O], writes=[O])
                        kb.op("dve", lambda e: e.tensor_tensor(out=dv, in0=pd[0:1, :, :], in1=dv, op=ALU.add),
                              reads=[pd, Dn], writes=[Dn])
            r_ = rD[s]
            kb.op("dve", lambda e: e.reciprocal(out=r_[0:1, :], in_=Dn[0:1, :]), reads=[Dn], writes=[r_])
            a_ = ao[s]
            for tcx in range(4):
                cs = slice(tcx * 512, (tcx + 1) * 512)
                pb = ps_b[nbb % 2]
                nbb += 1
                kb.op("pe", lambda e: e.matmul(pb[:, :], lhsT=G["onesf"][0:1, 0:128], rhs=r_[0:1, cs],
                                               start=True, stop=True), reads=[G["onesf"], r_], writes=[pb])
                kb.op("dve", lambda e: e.tensor_tensor(out=O[:, cs], in0=O[:, cs], in1=pb[:, :], op=ALU.mult),
                      reads=[O, pb], writes=[O])
                kb.op("dve", lambda e: e.tensor_tensor(out=a_[:, cs], in0=O[:, cs], in1=zbh[s][:, cs], op=ALU.mult),
                      reads=[O, zbh[s]], writes=[a_])
            kb.dma("sp", G["mixT"], G["mixT"].t[8 + hd, :, :], a_, a_[:, :])
    kb.barrier()


def outproj(kb, G, half, src, KT, wname, resid, dst):
    t0 = half * HALF
    with ExitStack() as c:
        mT = kb.sb(c, "mT", [128, KT, HALF], BF16)
        kb.dma("sp", mT, mT[:, :, :], G[src], G[src].t[:, :, t0:t0 + HALF].rearrange("k p t -> p k t"))
        wbufs = [kb.sb(c, "wo%d" % i, [128, KT, 512], BF16) for i in range(2)]
        xr = [kb.sb(c, "xo%d" % i, [128, 512], F32) for i in range(3)]
        pacc = [kb.ps(c, "po%d" % i, [128, 512]) for i in range(4)]
        wv = G[wname].t.rearrange("(k p) n -> p k n", p=128)
        n = 0
        for cc in range(8):
            wb = wbufs[cc % 2]
            cs = slice(cc * 512, (cc + 1) * 512)
            kb.dma("pool", wb, wb[:, :, :], G[wname], wv[:, :, cs])
            for ti in range(8):
                x_b = xr[n % 3]
                ps = pacc[n % 4]
                n += 1
                rows = slice(t0 + ti * 128, t0 + (ti + 1) * 128)
                kb.dma("sp", x_b, x_b[:, :], G[resid], G[resid].t[rows, cs])
                for k in range(KT):
                    kb.op("pe", lambda e: e.matmul(ps[:, :], lhsT=mT[:, k, ti * 128:(ti + 1) * 128], rhs=wb[:, k, :],
                                                   start=(k == 0), stop=(k == KT - 1)), reads=[mT, wb], writes=[ps])
                kb.op("dve", lambda e: e.tensor_tensor(out=x_b[:, :], in0=ps[:, :], in1=x_b[:, :], op=ALU.add),
                      reads=[ps, x_b], writes=[x_b])
                kb.dma("sp", G[dst], G[dst].t[rows, cs], x_b, x_b[:, :])
    kb.barrier()


Q_RANK, KV_RANK = 1536, 512
L1_KR0, L1_KI0, L1_WI0, L1_GATE0 = 2048, 2080, 2208, 2240
NEG = -1.0e30


def load_cols_T(kb, c, srcs, ncol):
    outs = [kb.sb(c, "lc_o%d" % j, [128, ncol], F32) for j in range(len(srcs))]
    with ExitStack() as c2:
        st = kb.sb(c2, "lc_st", [ncol, len(srcs), 128], F32)
        pp = kb.ps(c2, "lc_ps", [128, len(srcs), ncol], F32)
        for j, sb_ in enumerate(srcs):
            kb.dma("sp", st, st[:, j, :], sb_, sb_.t.rearrange("(ct c) -> ct c", c=128))
        for j in range(len(srcs)):
            kb.op("pe", lambda e: e.transpose(out=pp[:, j, :], in_=st[:, j, :], identity=G_IDENTF[0][0:ncol, 0:ncol]),
                  reads=[st, G_IDENTF[0]], writes=[pp])
            o = outs[j]
            kb.op("dve", lambda e: e.tensor_copy(out=o[:, :], in_=pp[:, j, :]), reads=[pp], writes=[o])
        kb.barrier()
    return outs


G_IDENTF = [None]


def layer1_inproj(kb, G, half):
    t0 = half * HALF
    G_IDENTF[0] = G["identf"]
    with ExitStack() as c:
        hT = kb.sb(c, "hT1", [128, 32, HALF], BF16)
        phase_norm_T(kb, G, G["x1"], t0, G["odd_norm"], hT)
        (gq,) = load_cols_T(kb, c, [G["mla_q_norm"]], 12)
        (gkv,) = load_cols_T(kb, c, [G["mla_kv_norm"]], 4)
        (gki,) = load_cols_T(kb, c, [G["idx_k_norm"]], 1)
        wbufs = [kb.sb(c, "wb%d" % i, [128, 32, 256], BF16) for i in range(3)]
        pacc = [kb.ps(c, "pacc%d" % i, [128, 512], F32) for i in range(3)]
        pssq = [kb.ps(c, "pssq%d" % i, [128, 512], F32) for i in range(2)]
        psw = kb.ps(c, "psw", [32, 512], F32)
        ptr = kb.ps(c, "ptr", [128, 4, 32], F32)
        raw = kb.sb(c, "raw", [128, 12, HALF], F32)
        sq = [kb.sb(c, "sq%d" % i, [128, 512], BF16) for i in range(2)]
        rstd = kb.sb(c, "rstd", [128, HALF], F32)
        stg = [kb.sb(c, "stg%d" % i, [128, HALF], BF16) for i in range(3)]
        xr = kb.sb(c, "xr", [32, 512], F32)
        rt = kb.sb(c, "rt", [32, 2, 512], F32)
        kin = kb.sb(c, "kin", [128, HALF], F32)
        wis = kb.sb(c, "wis", [32, HALF], F32)
        wit = kb.sb(c, "wit", [128, 8, 32], F32)
        st = {"np": 0, "ns": 0, "nq": 0}

        def mm(wb, off, M, ps, tc):
            for k in range(32):
                kb.op("pe", lambda e: e.matmul(ps[0:M, :], lhsT=wb[:, k, off:off + M],
                                               rhs=hT[:, k, tc * 512:(tc + 1) * 512],
                                               start=(k == 0), stop=(k == 31)), reads=[wb, hT], writes=[ps])

        def finish_norm(ntile, rank, gcol, dst):
            for tc in range(2):
                cs = slice(tc * 512, (tc + 1) * 512)
                kb.op("dve", lambda e: e.tensor_scalar(out=rstd[:, cs], in0=pssq[tc][:, :], scalar1=1.0 / rank,
                                                       scalar2=EPS, op0=ALU.mult, op1=ALU.add),
                      reads=[pssq[tc]], writes=[rstd])
            kb.op("act", lambda e: e.activation(out=rstd[:, :], in_=rstd[:, :], func=AF.Sqrt), reads=[rstd], writes=[rstd])
            kb.op("dve", lambda e: e.reciprocal(out=rstd[:, :], in_=rstd[:, :]), reads=[rstd], writes=[rstd])
            for ct in range(ntile):
                s = stg[st["ns"] % 3]
                st["ns"] += 1
                kb.op("dve", lambda e: e.scalar_tensor_tensor(out=s[:, :], in0=raw[:, ct, :], scalar=gcol[:, ct:ct + 1],
                                                              in1=rstd[:, :], op0=ALU.mult, op1=ALU.mult),
                      reads=[raw, gcol, rstd], writes=[s])
                kb.dma("sp", dst, dst.t[ct, :, t0:t0 + HALF], s, s[:, :])

        def norm_tile(c0, ct, ntile, rank, gcol, dst):
            def fn(wb, off):
                for tc in range(2):
                    ps = pacc[st["np"] % 3]
                    st["np"] += 1
                    mm(wb, off, 128, ps, tc)
                    cs = slice(tc * 512, (tc + 1) * 512)
                    kb.op("act", lambda e: e.activation(out=raw[:, ct, cs], in_=ps[:, :], func=AF.Copy),
                          reads=[ps], writes=[raw])
                    sq_b = sq[st["nq"] % 2]
                    st["nq"] += 1
                    kb.op("act", lambda e: e.activation(out=sq_b[:, :], in_=ps[:, :], func=AF.Square),
                          reads=[ps], writes=[sq_b])
                    kb.op("pe", lambda e: e.matmul(pssq[tc][:, :], lhsT=G["onesb"][:, :], rhs=sq_b[:, :],
                                                   start=(ct == 0), stop=(ct == ntile - 1)),
                          reads=[G["onesb"], sq_b], writes=[pssq[tc]])
                if ct == ntile - 1:
                    finish_norm(ntile, rank, gcol, dst)
            return (c0, c0 + 128, fn)

        def kr_job():
            def fn(wb, off):
                s = stg[st["ns"] % 3]
                st["ns"] += 1
                for tc in range(2):
                    ps = pacc[st["np"] % 3]
                    st["np"] += 1
                    mm(wb, off, 32, ps, tc)
                    rope_evac(kb, G, (ps[0:32, :], ps),
                              lambda lo, hi: (s.t[lo:hi, tc * 512:(tc + 1) * 512].rearrange("p (r m) -> p r m", r=1), s),
                              t0 + tc * 512, xr, psw, rt, 1)
                kb.dma("sp", G["krT"], G["krT"].t[:, t0:t0 + HALF], s, s[0:32, :])
            return (L1_KR0, L1_KR0 + 32, fn)

        def ki_job():
            def fn(wb, off):
                s = stg[st["ns"] % 3]
                st["ns"] += 1
                for tc in range(2):
                    ps = pacc[st["np"] % 3]
                    st["np"] += 1
                    mm(wb, off, 128, ps, tc)
                    cs = slice(tc * 512, (tc + 1) * 512)
                    kb.op("act", lambda e: e.activation(out=kin[:, cs], in_=ps[:, :], func=AF.Copy),
                          reads=[ps], writes=[kin])
                    sq_b = sq[st["nq"] % 2]
                    st["nq"] += 1
                    kb.op("act", lambda e: e.activation(out=sq_b[:, :], in_=ps[:, :], func=AF.Square),
                          reads=[ps], writes=[sq_b])
                    kb.op("pe", lambda e: e.matmul(pssq[tc][:, :], lhsT=G["onesb"][:, :], rhs=sq_b[:, :],
                                                   start=True, stop=True), reads=[G["onesb"], sq_b], writes=[pssq[tc]])
                    kb.op("dve", lambda e: e.tensor_scalar(out=rstd[:, cs], in0=pssq[tc][:, :], scalar1=1.0 / 128,
                                                           scalar2=EPS, op0=ALU.mult, op1=ALU.add),
                          reads=[pssq[tc]], writes=[rstd])
                    kb.op("act", lambda e: e.activation(out=rstd[:, cs], in_=rstd[:, cs], func=AF.Sqrt),
                          reads=[rstd], writes=[rstd])
                    kb.op("dve", lambda e: e.reciprocal(out=rstd[:, cs], in_=rstd[:, cs]), reads=[rstd], writes=[rstd])
                    kb.op("dve", lambda e: e.scalar_tensor_tensor(out=kin[:, cs], in0=kin[:, cs], scalar=gki[:, 0:1],
                                                                  in1=rstd[:, cs], op0=ALU.mult, op1=ALU.mult),
                          reads=[kin, gki, rstd], writes=[kin])
                    for (lo, hi) in ((32, 64), (64, 128)):
                        kb.op("act", lambda e: e.activation(out=s[lo:hi, cs], in_=kin[lo:hi, cs], func=AF.Copy),
                              reads=[kin], writes=[s])
                    rope_evac(kb, G, (kin[0:32, cs], kin),
                              lambda lo, hi: (s.t[lo:hi, cs].rearrange("p (r m) -> p r m", r=1), s),
                              t0 + tc * 512, xr, psw, rt, 1)
                kb.dma("sp", G["kiT"], G["kiT"].t[:, t0:t0 + HALF], s, s[:, :])
            return (L1_KI0, L1_KI0 + 128, fn)

        def wi_job():
            def fn(wb, off):
                for tc in range(2):
                    ps = pacc[st["np"] % 3]
                    st["np"] += 1
                    mm(wb, off, 32, ps, tc)
                    kb.op("act", lambda e: e.activation(out=wis[:, tc * 512:(tc + 1) * 512], in_=ps[0:32, :],
                                                        func=AF.Copy, scale=float((128 * 32) ** -0.5)),
                          reads=[ps], writes=[wis])
                for tg in range(2):
                    for j in range(4):
                        ti = 4 * tg + j
                        kb.op("pe", lambda e: e.transpose(out=ptr[:, j, :], in_=wis[:, ti * 128:(ti + 1) * 128],
                                                          identity=G["identf"][0:32, 0:32]),
                              reads=[wis, G["identf"]], writes=[ptr])
                    kb.op("dve", lambda e: e.tensor_copy(out=wit[:, 4 * tg:4 * tg + 4, :], in_=ptr[:, :, :]),
                          reads=[ptr], writes=[wit])
                kb.dma("sp", G["wi"], G["wi"].t[t0:t0 + HALF, :].rearrange("(t p) h -> p t h", p=128), wit, wit[:, :, :])
            return (L1_WI0, L1_WI0 + 32, fn)

        def gate_job(ct):
            c0 = L1_GATE0 + ct * 128

            def fn(wb, off):
                s = stg[st["ns"] % 3]
                st["ns"] += 1
                for tc in range(2):
                    ps = pacc[st["np"] % 3]
                    st["np"] += 1
                    mm(wb, off, 128, ps, tc)
                    kb.op("act", lambda e: e.activation(out=s[:, tc * 512:(tc + 1) * 512], in_=ps[:, :], func=AF.Silu),
                          reads=[ps], writes=[s])
                kb.dma("sp", G["sgT"], G["sgT"].t[ct, :, t0:t0 + HALF], s, s[:, :])
            return (c0, c0 + 128, fn)

        jobs = [norm_tile(ct * 128, ct, 12, Q_RANK, gq, G["cqT"]) for ct in range(12)]
        jobs += [norm_tile(Q_RANK + ct * 128, ct, 4, KV_RANK, gkv, G["ckvT"]) for ct in range(4)]
        jobs += [kr_job(), ki_job(), wi_job()]
        jobs += [gate_job(ct) for ct in range(32)]
        stream_proj(kb, G["odd_w_in"], 32, jobs, wbufs)
    kb.barrier()


def layer1_qproj(kb, G, half):
    t0 = half * HALF
    with ExitStack() as c:
        cq = kb.sb(c, "cq", [128, 12, HALF], BF16)
        kb.dma("sp", cq, cq[:, :, :], G["cqT"], G["cqT"].t[:, :, t0:t0 + HALF].rearrange("k p t -> p k t"))
        wbufs = [kb.sb(c, "wq%d" % i, [128, 12, 256], BF16) for i in range(3)]
        pacc = [kb.ps(c, "pacc%d" % i, [128, 512], F32) for i in range(4)]
        psw = [kb.ps(c, "psw%d" % i, [32, 512], F32) for i in range(4)]
        stg = [kb.sb(c, "stg%d" % i, [128, HALF], BF16) for i in range(4)]
        xr = [kb.sb(c, "xr%d" % i, [32, 512], F32) for i in range(4)]
        rt = [kb.sb(c, "rt%d" % i, [32, 2, 512], F32) for i in range(4)]
        st = {"np": 0, "ns": 0, "nr": 0}

        def job(h, dst):
            def fn(wb, off):
                s = stg[st["ns"] % 4]
                st["ns"] += 1
                for tc in range(2):
                    ps = pacc[st["np"] % 4]
                    st["np"] += 1
                    cs = slice(tc * 512, (tc + 1) * 512)
                    for k in range(12):
                        kb.op("pe", lambda e: e.matmul(ps[:, :], lhsT=wb[:, k, off:off + 128], rhs=cq[:, k, cs],
                                                       start=(k == 0), stop=(k == 11)), reads=[wb, cq], writes=[ps])
                    for (lo, hi) in ((32, 64), (64, 128)):
                        kb.op("act", lambda e: e.activation(out=s[lo:hi, cs], in_=ps[lo:hi, :], func=AF.Copy),
                              reads=[ps], writes=[s])
                    i = st["nr"] % 4
                    st["nr"] += 1
                    rope_evac(kb, G, (ps[0:32, :], ps),
                              lambda lo, hi: (s.t[lo:hi, cs].rearrange("p (r m) -> p r m", r=1), s),
                              t0 + tc * 512, xr[i], psw[i], rt[i], 1)
                kb.dma("sp", dst, dst.t[h, :, t0:t0 + HALF], s, s[:, :])
            return (h * 128, (h + 1) * 128, fn)

        stream_proj(kb, G["mla_w_uq"], 12, [job(h, G["q1T"]) for h in range(32)], wbufs)
        stream_proj(kb, G["idx_w_q"], 12, [job(h, G["qiT"]) for h in range(32)], wbufs)
    kb.barrier()


def layer1_index(kb, G):
    POOL_HEADS = (5, 11, 17, 23, 29)
    with ExitStack() as c:
        kidx = kb.sb(c, "kidx", [128, SEQ], BF16)
        kb.dma("sp", kidx, kidx[:, :], G["kiT"], G["kiT"].t[:, :])
        negm = kb.sb(c, "negm", [128, 128], F32)
        kb.dma("sp", negm, negm[:, :], G["c_negmask"], G["c_negmask"].t[:, :])
        qi = [kb.sb(c, "qi%d" % i, [128, 32, 128], BF16) for i in range(2)]
        wq = [kb.sb(c, "wq%d" % i, [128, 32], F32) for i in range(2)]
        acc = [kb.sb(c, "acc%d" % i, [128, SEQ], F32) for i in range(2)]
        acc2 = [kb.sb(c, "accp%d" % i, [128, SEQ], F32) for i in range(2)]
        accP = [[Buf("accP%d_%d" % (i, k), acc[i].t) for k in range(4)] for i in range(2)]
        acc2P = [[Buf("acc2P%d_%d" % (i, k), acc2[i].t) for k in range(4)] for i in range(2)]
        rl = [kb.sb(c, "rl%d" % i, [128, 512], F32) for i in range(6)]
        tmpp = [kb.sb(c, "tmpp%d" % i, [128, 512], F32) for i in range(2)]
        wk = [kb.sb(c, "wk%d" % i, [128, SEQ], F32) for i in range(2)]
        m8 = [kb.sb(c, "m8_%d" % i, [128, 8], F32) for i in range(2)]
        msk = [kb.sb(c, "msk%d" % i, [128, SEQ], BF16) for i in range(2)]
        mTs = [kb.sb(c, "mTs%d" % i, [128, 16, 128], BF16) for i in range(2)]
        ps = [kb.ps(c, "pix%d" % i, [128, 512], F32) for i in range(6)]
        ptp = [kb.ps(c, "pmt%d" % i, [128, 8, 128], BF16) for i in range(2)]
        st = {"nps": 0, "nrl": 0, "ntp": 0, "ntm": 0}

        def accumulate(i):
            n = 128 * (i + 1)
            s = i % 2
            q_b, w_b, a_b, a2_b = qi[s], wq[s], acc[s], acc2[s]
            qs = slice(i * 128, (i + 1) * 128)
            kb.dma("sp", q_b, q_b[:, :, :], G["qiT"], G["qiT"].t[:, :, qs].rearrange("h d q -> d h q"))
            kb.dma("sp", w_b, w_b[:, :], G["wi"], G["wi"].t[qs, :])
            nch = (n + 511) // 512
            first = {"dve": True, "pool": True}
            for h in range(32):
                onpool = h in POOL_HEADS
                for kc in range(nch):
                    N = min(512, n - 512 * kc)
                    ks = slice(512 * kc, 512 * kc + N)
                    p_b = ps[st["nps"] % 6]
                    st["nps"] += 1
                    r_b = rl[st["nrl"] % 6]
                    st["nrl"] += 1
                    kb.op("pe", lambda e: e.matmul(p_b[:, 0:N], lhsT=q_b[:, h, :], rhs=kidx[:, ks], start=True, stop=True),
                          reads=[q_b, kidx], writes=[p_b])
                    kb.op("act", lambda e: e.activation(out=r_b[:, 0:N], in_=p_b[:, 0:N], func=AF.Relu),
                          reads=[p_b], writes=[r_b])
                    if not onpool:
                        ap_ = accP[s][kc]
                        if first["dve"]:
                            kb.op("dve", lambda e: e.tensor_scalar(out=a_b[:, ks], in0=r_b[:, 0:N], scalar1=w_b[:, h:h + 1],
                                                                   scalar2=None, op0=ALU.mult),
                                  reads=[r_b, w_b], writes=[ap_])
                        else:
                            kb.op("dve", lambda e: e.scalar_tensor_tensor(out=a_b[:, ks], in0=r_b[:, 0:N],
                                                                          scalar=w_b[:, h:h + 1], in1=a_b[:, ks],
                                                                          op0=ALU.mult, op1=ALU.add),
                                  reads=[r_b, w_b, ap_], writes=[ap_])
                    else:
                        ap_ = acc2P[s][kc]
                        if first["pool"]:
                            kb.op("pool", lambda e: e.tensor_scalar(out=a2_b[:, ks], in0=r_b[:, 0:N], scalar1=w_b[:, h:h + 1],
                                                                    scalar2=None, op0=ALU.mult),
                                  reads=[r_b, w_b], writes=[ap_])
                        else:
                            t_b = tmpp[st["ntm"] % 2]
                            st["ntm"] += 1
                            kb.op("pool", lambda e: e.tensor_scalar(out=t_b[:, 0:N], in0=r_b[:, 0:N],
                                                                    scalar1=w_b[:, h:h + 1], scalar2=None, op0=ALU.mult),
                                  reads=[r_b, w_b], writes=[t_b])
                            kb.op("pool", lambda e: e.tensor_tensor(out=a2_b[:, ks], in0=a2_b[:, ks], in1=t_b[:, 0:N],
                                                                    op=ALU.add), reads=[ap_, t_b], writes=[ap_])
                first["pool" if onpool else "dve"] = False
            for kc in range(nch):
                N = min(512, n - 512 * kc)
                ks = slice(512 * kc, 512 * kc + N)
                kb.op("dve", lambda e: e.tensor_tensor(out=a_b[:, ks], in0=a_b[:, ks], in1=a2_b[:, ks], op=ALU.add),
                      reads=[accP[s][kc], acc2P[s][kc]], writes=[accP[s][kc]])
            kd = i // 4
            kb.op("dve", lambda e: e.tensor_tensor(out=a_b[:, qs], in0=a_b[:, qs], in1=negm[:, :], op=ALU.add),
                  reads=[accP[s][kd], negm], writes=[accP[s][kd]])

        def topk_ops(i):
            n = 128 * (i + 1)
            s = i % 2
            nch = (n + 511) // 512
            a_b, w_, m_ = acc[s], wk[s], m8[s]
            ops = []
            ops.append(lambda: kb.op("dve", lambda e: e.tensor_copy(out=w_[:, 0:n], in_=a_b[:, 0:n]),
                                     reads=accP[s][0:nch], writes=[w_]))
            for r in range(32):
                ops.append(lambda: kb.op("dve", lambda e: e.max(out=m_[:, :], in_=w_[:, 0:n]), reads=[w_], writes=[m_]))
                if r < 31:
                    ops.append(lambda: kb.op("dve", lambda e: e.match_replace(out=w_[:, 0:n], in_to_replace=m_[:, :],
                                                                              in_values=w_[:, 0:n], imm_value=NEG),
                                             reads=[w_, m_], writes=[w_]))
            return ops

        def finish(i):
            n = 128 * (i + 1)
            s = i % 2
            nch = (n + 511) // 512
            qs = slice(i * 128, (i + 1) * 128)
            a_b, m_b, m_ = acc[s], msk[s], m8[s]
            if i >= 2:
                kb.op("dve", lambda e: e.tensor_scalar(out=m_b[:, 0:n], in0=a_b[:, 0:n], scalar1=m_[:, 7:8],
                                                       scalar2=None, op0=ALU.is_ge),
                      reads=accP[s][0:nch] + [m_], writes=[m_b])
            else:
                kb.op("dve", lambda e: e.tensor_scalar(out=m_b[:, 0:n], in0=a_b[:, 0:n], scalar1=-1.0e29,
                                                       scalar2=None, op0=ALU.is_ge), reads=accP[s][0:nch], writes=[m_b])
            mt_b = mTs[s]
            for j0 in range(0, i + 1, 8):
                nj = min(8, i + 1 - j0)
                t_b = ptp[st["ntp"] % 2]
                st["ntp"] += 1
                for jj in range(nj):
                    j = j0 + jj
                    kb.op("pe", lambda e: e.transpose(out=t_b[:, jj, :], in_=m_b[:, j * 128:(j + 1) * 128],
                                                      identity=G["ident"][:, :]), reads=[m_b, G["ident"]], writes=[t_b])
                kb.op("act", lambda e: e.activation(out=mt_b[:, j0:j0 + nj, :], in_=t_b[:, 0:nj, :], func=AF.Copy),
                      reads=[t_b], writes=[mt_b])
            kb.dma("sp", G["selT"], G["selT"].t[0:i + 1, :, qs].rearrange("j k q -> k j q"), mt_b, mt_b[:, 0:i + 1, :])

        for i0 in range(0, 16, 2):
            accumulate(i0)
            accumulate(i0 + 1)
            if i0 >= 2:
                oa, ob_ = topk_ops(i0), topk_ops(i0 + 1)
                for x, y in zip(oa, ob_):
                    x()
                    y()
            finish(i0)
            finish(i0 + 1)
    kb.barrier()


def layer1_attn(kb, G):
    with ExitStack() as c:
        ckv = kb.sb(c, "ckv", [128, 4, SEQ], BF16)
        kb.dma("sp", ckv, ckv[:, :, :], G["ckvT"], G["ckvT"].t[:, :, :].rearrange("k p t -> p k t"))
        krp = kb.sb(c, "krp", [128, SEQ], BF16)
        kb.op("dve", lambda e: e.memset(krp[:, :], 0.0), writes=[krp])
        kb.dma("sp", krp, krp[0:32, :], G["krT"], G["krT"].t[:, :])
        sel = kb.sb(c, "sel", [128, 16, SEQ], BF16)
        for j in range(16):
            kb.dma("sp", sel, sel[:, j, j * 128:SEQ], G["selT"], G["selT"].t[j, :, j * 128:SEQ])
        wuk = [kb.sb(c, "wuk%d" % i, [128, 4, 128], BF16) for i in range(2)]
        wuv = [kb.sb(c, "wuv%d" % i, [128, 4, 128], BF16) for i in range(2)]
        Qh = [kb.sb(c, "Qh%d" % i, [128, SEQ], BF16) for i in range(2)]
        sgh = [kb.sb(c, "sgh%d" % i, [128, SEQ], BF16) for i in range(2)]
        KhT = [kb.sb(c, "KhT%d" % i, [128, SEQ], BF16) for i in range(2)]
        Va = [kb.sb(c, "Va%d" % i, [128, 16, 130], BF16) for i in range(2)]
        ogh = [kb.sb(c, "ogh%d" % i, [128, SEQ], BF16) for i in range(2)]
        PT = [kb.sb(c, "PT%d" % i, [128, 16, 128], BF16) for i in range(2)]
        on = [kb.sb(c, "on%d" % i, [128, 128], BF16) for i in range(2)]
        rd = [kb.sb(c, "rd%d" % i, [128, 2], F32) for i in range(2)]
        pk = [kb.ps(c, "pk%d" % i, [128, 512], F32) for i in range(2)]
        pst = [kb.ps(c, "pst%d" % i, [128, 4, 128], F32) for i in range(2)]
        po = [kb.ps(c, "pov%d" % i, [128, 130], F32) for i in range(2)]
        ptr = [kb.ps(c, "ptr%d" % i, [128, 128], BF16) for i in range(2)]
        for i in range(2):
            kb.op("dve", lambda e: e.memset(wuk[i][:, :, :], 0.0), writes=[wuk[i]])
            kb.op("dve", lambda e: e.memset(Va[i][:, :, 128:130], 1.0), writes=[Va[i]])
        ukv = G["mla_w_uk"].t.rearrange("(k p) h d -> p k h d", p=128)
        uvv = G["mla_w_uv"].t.rearrange("(k p) h d -> p k h d", p=128)
        npk = 0
        nst = 0
        npo = 0
        nq = 0
        for h in range(32):
            s = h % 2
            kb.dma("pool", wuk[s], wuk[s][:, :, 32:128], G["mla_w_uk"], ukv[:, :, h, :])
            kb.dma("pool", wuv[s], wuv[s][:, :, :], G["mla_w_uv"], uvv[:, :, h, :])
            kb.dma("sp", Qh[s], Qh[s][:, :], G["q1T"], G["q1T"].t[h, :, :])
            kb.dma("sp", sgh[s], sgh[s][:, :], G["sgT"], G["sgT"].t[h, :, :])
            K_, V_ = KhT[s], Va[s]
            for kc in range(4):
                cs = slice(kc * 512, (kc + 1) * 512)
                p_b = pk[npk % 2]
                npk += 1
                for c4 in range(4):
                    kb.op("pe", lambda e: e.matmul(p_b[:, :], lhsT=wuk[s][:, c4, :], rhs=ckv[:, c4, cs],
                                                   start=(c4 == 0), stop=(c4 == 3)), reads=[wuk[s], ckv], writes=[p_b])
                kb.op("dve", lambda e: e.tensor_tensor(out=K_[:, cs], in0=p_b[:, :], in1=krp[:, cs], op=ALU.add),
                      reads=[p_b, krp], writes=[K_])
            for j0 in range(0, 16, 4):
                p_b = pk[npk % 2]
                npk += 1
                for jj in range(4):
                    j = j0 + jj
                    for c4 in range(4):
                        kb.op("pe", lambda e: e.matmul(p_b[:, jj * 128:(jj + 1) * 128], lhsT=ckv[:, c4, j * 128:(j + 1) * 128],
                                                       rhs=wuv[s][:, c4, :], start=(c4 == 0), stop=(c4 == 3)),
                              reads=[ckv, wuv[s]], writes=[p_b])
                kb.op("act", lambda e: e.activation(out=V_[:, j0:j0 + 4, 0:128],
                                                    in_=p_b.t[:, :].rearrange("p (a v) -> p a v", a=4), func=AF.Copy),
                      reads=[p_b], writes=[V_])
            og = ogh[s]
            for i in range(16):
                qs = slice(i * 128, (i + 1) * 128)
                P_ = PT[nq % 2]
                nq += 1
                for j0 in range(0, i + 1, 4):
                    nj = min(4, i + 1 - j0)
                    s_b = pst[nst % 2]
                    nst += 1
                    for jj in range(nj):
                        j = j0 + jj
                        kb.op("pe", lambda e: e.matmul(s_b[:, jj, :], lhsT=K_[:, j * 128:(j + 1) * 128], rhs=Qh[s][:, qs],
                                                       start=True, stop=True), reads=[K_, Qh[s]], writes=[s_b])
                    kb.op("act", lambda e: e.activation(out=P_[:, j0:j0 + nj, :], in_=s_b[:, 0:nj, :], func=AF.Exp,
                                                        scale=ATT_SCALE), reads=[s_b], writes=[P_])
                kb.op("dve", lambda e: e.tensor_tensor(out=P_[:, 0:i + 1, :], in0=P_[:, 0:i + 1, :],
                                                       in1=sel[:, 0:i + 1, qs], op=ALU.mult), reads=[P_, sel], writes=[P_])
                o_b = po[npo % 2]
                t_b = ptr[npo % 2]
                on_b = on[npo % 2]
                r_b = rd[npo % 2]
                npo += 1
                for j in range(i + 1):
                    kb.op("pe", lambda e: e.matmul(o_b[:, 0:129], lhsT=P_[:, j, :], rhs=V_[:, j, 0:129],
                                                   start=(j == 0), stop=(j == i)), reads=[P_, V_], writes=[o_b])
                kb.op("dve", lambda e: e.reciprocal(out=r_b[:, 0:1], in_=o_b[:, 128:129]), reads=[o_b], writes=[r_b])
                kb.op("act", lambda e: e.activation(out=on_b[:, :], in_=o_b[:, 0:128], func=AF.Copy, scale=r_b[:, 0:1]),
                      reads=[o_b, r_b], writes=[on_b])
                kb.op("pe", lambda e: e.transpose(out=t_b[:, :], in_=on_b[:, :], identity=G["ident"][:, :]),
                      reads=[on_b, G["ident"]], writes=[t_b])
                kb.op("dve", lambda e: e.tensor_tensor(out=og[:, qs], in0=t_b[:, :], in1=sgh[s][:, qs], op=ALU.mult),
                      reads=[t_b, sgh[s]], writes=[og])
            kb.dma("sp", G["ogT"], G["ogT"].t[h, :, :], og, og[:, :])
    kb.barrier()


def final_norm(kb, G):
    with ExitStack() as c:
        gbc = kb.sb(c, "fgbc", [128, D_MODEL], F32)
        xt = [kb.sb(c, "fxt%d" % i, [128, D_MODEL], F32) for i in range(2)]
        junk = kb.sb(c, "fjunk", [128, D_MODEL], BF16)
        yo = [kb.sb(c, "fyo%d" % i, [128, D_MODEL], F32) for i in range(2)]
        ss = [kb.sb(c, "fss%d" % i, [128, 4], F32) for i in range(2)]
        kb.dma("sp", gbc, gbc[:, :], G["final_norm"], G["final_norm"].t.partition_broadcast(128))
        for ti in range(16):
            x_b, y_b, s_b = xt[ti % 2], yo[ti % 2], ss[ti % 2]
            rows = slice(ti * 128, (ti + 1) * 128)
            kb.dma("sp", x_b, x_b[:, :], G["x2"], G["x2"].t[rows, :])
            kb.op("act", lambda e: e.activation(out=junk[:, :], in_=x_b[:, :], func=AF.Square, accum_out=s_b[:, 0:1]),
                  reads=[x_b], writes=[junk, s_b])
            kb.op("dve", lambda e: e.tensor_scalar(out=s_b[:, 1:2], in0=s_b[:, 0:1], scalar1=1.0 / D_MODEL, scalar2=EPS,
                                                   op0=ALU.mult, op1=ALU.add), reads=[s_b], writes=[s_b])
            kb.op("act", lambda e: e.activation(out=s_b[:, 2:3], in_=s_b[:, 1:2], func=AF.Sqrt), reads=[s_b], writes=[s_b])
            kb.op("dve", lambda e: e.reciprocal(out=s_b[:, 3:4], in_=s_b[:, 2:3]), reads=[s_b], writes=[s_b])
            kb.op("dve", lambda e: e.scalar_tensor_tensor(out=y_b[:, :], in0=x_b[:, :], scalar=s_b[:, 3:4], in1=gbc[:, :],
                                                          op0=ALU.mult, op1=ALU.mult), reads=[x_b, s_b, gbc], writes=[y_b])
            kb.dma("sp", G["out"], G["out"].t[rows, :], y_b, y_b[:, :])
    kb.barrier()
```

```python
import math
import numpy as np
import concourse.bass as bass
import concourse.mybir as mybir
from concourse.bass_utils import run_bass_kernel_spmd

F32 = mybir.dt.float32
BF16 = mybir.dt.bfloat16
AF = mybir.ActivationFunctionType
ALU = mybir.AluOpType
AX = mybir.AxisListType

D_MODEL = 4096
SEQ = 2048
BATCH = 4
HALF = 1024
EPS = 1e-6
NDS = 20
SAME_ENGINE_SYNC = True


class Buf:
    def __init__(self, name, t=None):
        self.name = name
        self.t = t
        self.w = {}
        self.r = {}
        self.track = True

    def __getitem__(self, k):
        return self.t[k]


class KB:
    def __init__(self, nc, stack):
        self.nc = nc
        self.eobj = {"pe": nc.tensor, "act": nc.scalar, "dve": nc.vector, "pool": nc.gpsimd, "sp": nc.sync}
        self.semsets = [{n: stack.enter_context(nc.semaphore("s%d_%s" % (i, n))) for n in self.eobj}
                        for i in range(3)]
        self.epoch = 0
        self.sem = self.semsets[0]
        self.cnt = {n: 0 for n in self.eobj}
        self.tot = {n: 0 for n in self.eobj}
        self.waited = {n: {} for n in self.eobj}
        self.dsem = {}
        self.dcnt = {}
        for qn in ("sp", "pool"):
            self.dsem[qn] = [stack.enter_context(nc.semaphore("d_%s%d" % (qn, i))) for i in range(NDS)]
            self.dcnt[qn] = 0
        self.nops = 0
        self.nuid = 0
        self.active = {}

    def sb(self, ctx, name, shape, dt):
        self.nuid += 1
        name = "%s_%d" % (name, self.nuid)
        t = ctx.enter_context(self.nc.sbuf_tensor(name, list(shape), dt))
        return Buf(name, t)

    def ps(self, ctx, name, shape, dt=F32):
        self.nuid += 1
        name = "%s_%d" % (name, self.nuid)
        t = ctx.enter_context(self.nc.psum_tensor(name, list(shape), dt))
        return Buf(name, t)

    def dram(self, name, shape, dt, kind="Internal"):
        t = self.nc.dram_tensor(name, list(shape), dt, kind=kind).ap()
        b = Buf(name, t)
        b.track = False
        return b

    def _deps(self, reads, writes):
        deps = {}
        ep = self.epoch

        def add(d):
            for k, (s, v, e) in d.items():
                if e < ep:
                    continue
                if k not in deps or deps[k][1] < v:
                    deps[k] = (s, v)
        for b in reads:
            if b.track:
                add(b.w)
        for b in writes:
            if b.track:
                add(b.w)
                add(b.r)
        return deps

    def _do_waits(self, eng, deps):
        wd = self.waited[eng]
        e = self.eobj[eng]
        for k, (s, v) in deps.items():
            if k == eng and (eng == "pe" or not SAME_ENGINE_SYNC):
                continue
            if wd.get(k, 0) >= v:
                continue
            wd[k] = v
            e.wait_ge(s, v)

    def op(self, eng, fn, reads=(), writes=()):
        self._do_waits(eng, self._deps(reads, writes))
        self.cnt[eng] += 1
        tok = (self.sem[eng], self.cnt[eng], self.epoch)
        fn(self.eobj[eng]).then_inc(tok[0], 1)
        self.nops += 1
        self.active[eng] = True
        for b in writes:
            b.w = {eng: tok}
            b.r = {}
        for b in reads:
            b.r[eng] = tok

    def dma(self, qn, out_b, out_ap, in_b, in_ap, **kw):
        self._do_waits(qn, self._deps([in_b], [out_b]))
        i = self.dcnt[qn]
        self.dcnt[qn] += 1
        s = self.dsem[qn][i % NDS]
        prev = 16 * (i // NDS)
        key = "d_%s%d" % (qn, i % NDS)
        e = self.eobj[qn]
        if prev > 0 and self.waited[qn].get(key, 0) < prev:
            self.waited[qn][key] = prev
            e.wait_ge(s, prev)
        e.dma_start(out=out_ap, in_=in_ap, **kw).then_inc(s, 16)
        self.nops += 1
        self.active[qn] = True
        tok = (s, prev + 16, self.epoch)
        if out_b.track:
            out_b.w = {key: tok}
            out_b.r = {}
        if in_b.track:
            in_b.r[key] = tok

    def _all_tokens(self):
        toks = {}
        for n in self.eobj:
            if self.cnt[n] > 0:
                toks[n] = (self.sem[n], self.cnt[n])
        for qn in self.dsem:
            for j in range(NDS):
                uses = (self.dcnt[qn] - j + NDS - 1) // NDS if self.dcnt[qn] > j else 0
                if uses > 0:
                    toks["d_%s%d" % (qn, j)] = (self.dsem[qn][j], 16 * uses)
        return toks

    def barrier(self):
        toks = self._all_tokens()
        rotate = all(self.active.get(n, False) for n in self.eobj)
        nxt2 = self.semsets[(self.epoch + 2) % 3]
        for n in self.eobj:
            wd = self.waited[n]
            e = self.eobj[n]
            for k, (s, v) in toks.items():
                if k == n or wd.get(k, 0) >= v:
                    continue
                wd[k] = v
                e.wait_ge(s, v)
            if rotate:
                e.sem_clear(nxt2[n])
        if not rotate:
            return
        self.epoch += 1
        self.active = {}
        self.sem = self.semsets[self.epoch % 3]
        for n in self.eobj:
            self.tot[n] += self.cnt[n]
            self.cnt[n] = 0
            for k in list(self.waited[n].keys()):
                if k in self.eobj:
                    del self.waited[n][k]


from contextlib import ExitStack


def phase_norm_T(kb, G, xsrc, t0, gsrc, hT):
    with ExitStack() as c:
        gbc = kb.sb(c, "gbc", [128, D_MODEL], F32)
        xt = [kb.sb(c, "xt%d" % i, [128, D_MODEL], F32) for i in range(2)]
        junk = kb.sb(c, "junk", [128, D_MODEL], BF16)
        xn = [kb.sb(c, "xn%d" % i, [128, D_MODEL], BF16) for i in range(2)]
        ss = [kb.sb(c, "ss%d" % i, [128, 4], F32) for i in range(2)]
        tp = [kb.ps(c, "tp%d" % i, [128, 8, 128], BF16) for i in range(2)]
        kb.dma("sp", gbc, gbc[:, :], gsrc, gsrc.t.partition_broadcast(128))
        ntp = 0
        for ti in range(8):
            x_b, n_b, s_b = xt[ti % 2], xn[ti % 2], ss[ti % 2]
            kb.dma("sp", x_b, x_b[:, :], xsrc, xsrc.t[t0 + ti * 128: t0 + (ti + 1) * 128, :])
            kb.op("act", lambda e: e.activation(out=junk[:, :], in_=x_b[:, :], func=AF.Square,
                                                accum_out=s_b[:, 0:1]), reads=[x_b], writes=[junk, s_b])
            kb.op("dve", lambda e: e.tensor_scalar(out=s_b[:, 1:2], in0=s_b[:, 0:1], scalar1=1.0 / D_MODEL,
                                                   scalar2=EPS, op0=ALU.mult, op1=ALU.add),
                  reads=[s_b], writes=[s_b])
            kb.op("act", lambda e: e.activation(out=s_b[:, 2:3], in_=s_b[:, 1:2], func=AF.Sqrt),
                  reads=[s_b], writes=[s_b])
            kb.op("dve", lambda e: e.reciprocal(out=s_b[:, 3:4], in_=s_b[:, 2:3]), reads=[s_b], writes=[s_b])
            kb.op("dve", lambda e: e.scalar_tensor_tensor(out=n_b[:, :], in0=x_b[:, :], scalar=s_b[:, 3:4],
                                                          in1=gbc[:, :], op0=ALU.mult, op1=ALU.mult),
                  reads=[x_b, s_b, gbc], writes=[n_b])
            for k0 in range(0, 32, 8):
                p_b = tp[ntp % 2]
                ntp += 1
                for kk in range(8):
                    k = k0 + kk
                    kb.op("pe", lambda e: e.transpose(out=p_b[:, kk, :], in_=n_b[:, k * 128:(k + 1) * 128],
                                                      identity=G["ident"][:, :]),
                          reads=[n_b, G["ident"]], writes=[p_b])
                if (k0 // 8) % 2 == 0:
                    kb.op("act", lambda e: e.activation(out=hT[:, k0:k0 + 8, ti * 128:(ti + 1) * 128],
                                                        in_=p_b[:, :, :], func=AF.Copy),
                          reads=[p_b], writes=[hT])
                else:
                    kb.op("dve", lambda e: e.tensor_copy(out=hT[:, k0:k0 + 8, ti * 128:(ti + 1) * 128],
                                                         in_=p_b[:, :, :]), reads=[p_b], writes=[hT])
    kb.barrier()


def stream_proj(kb, wsrc, KT, jobs, wbufs, cw=256):
    chunks = []
    cur = None
    for (c0, c1, fn) in jobs:
        if cur is not None and cur[1] == c0 and (c1 - cur[0]) <= cw:
            cur[1] = c1
            cur[2].append((c0, c1, fn))
        else:
            cur = [c0, c1, [(c0, c1, fn)]]
            chunks.append(cur)
    wv = wsrc.t.rearrange("(k p) n -> p k n", p=128)
    for ci, (a, b, fl) in enumerate(chunks):
        wb = wbufs[ci % len(wbufs)]
        kb.dma("pool", wb, wb[:, 0:KT, 0:b - a], wsrc, wv[:, :, a:b])
        for (c0, c1, fn) in fl:
            fn(wb, c0 - a)


DILS = (1, 4, 16)
L0_QKV0 = 2048
L0_ZB0 = 11264
ATT_SCALE = 128 ** -0.5


def rope_evac(kb, G, src, s_out_view_fn, tok0, xr_b, psw, rt_b, D, prow=32):
    kb.op("act", lambda e: e.activation(out=xr_b[:, :], in_=src[0], func=AF.Copy),
          reads=[src[1]], writes=[xr_b])
    kb.op("pe", lambda e: e.matmul(psw[:, :], lhsT=G["perm32"][:, :], rhs=xr_b[:, :], start=True, stop=True),
          reads=[xr_b, G["perm32"]], writes=[psw])
    kb.op("dve", lambda e: e.tensor_tensor(out=rt_b[:, 0, :], in0=xr_b[:, :],
                                           in1=G["ropeC"][:, tok0:tok0 + 512], op=ALU.mult),
          reads=[xr_b, G["ropeC"]], writes=[rt_b])
    kb.op("dve", lambda e: e.tensor_tensor(out=rt_b[:, 1, :], in0=psw[:, :],
                                           in1=G["ropeS"][:, tok0:tok0 + 512], op=ALU.mult),
          reads=[psw, G["ropeS"], rt_b], writes=[rt_b])
    ov, ob = s_out_view_fn(0, 32)
    kb.op("dve", lambda e: e.tensor_tensor(
        out=ov, in0=rt_b.t[:, 0, :].rearrange("p (m r) -> p r m", r=D),
        in1=rt_b.t[:, 1, :].rearrange("p (m r) -> p r m", r=D), op=ALU.add),
        reads=[rt_b], writes=[ob])


def layer0_inproj(kb, G, half):
    t0 = half * HALF
    with ExitStack() as c:
        hT = kb.sb(c, "hT", [128, 32, HALF], BF16)
        phase_norm_T(kb, G, G["xin"], t0, G["even_norm"], hT)
        wbufs = [kb.sb(c, "wb%d" % i, [128, 32, 256], BF16) for i in range(3)]
        pacc = [kb.ps(c, "pacc%d" % i, [128, 512], F32) for i in range(4)]
        psw = [kb.ps(c, "psw%d" % i, [32, 512], F32) for i in range(4)]
        stg = [kb.sb(c, "stg%d" % i, [128, HALF], BF16) for i in range(3)]
        vst = [kb.sb(c, "vst%d" % i, [128, 256], BF16) for i in range(3)]
        xr = [kb.sb(c, "xr%d" % i, [32, 512], F32) for i in range(4)]
        rt = [kb.sb(c, "rt%d" % i, [32, 2, 512], F32) for i in range(4)]
        st = {"np": 0, "ns": 0, "nr": 0, "nv": 0}

        def mm_cm(wb, off, ps, tc):
            for k in range(32):
                kb.op("pe", lambda e: e.matmul(ps[:, :], lhsT=wb[:, k, off:off + 128],
                                               rhs=hT[:, k, tc * 512:(tc + 1) * 512],
                                               start=(k == 0), stop=(k == 31)),
                      reads=[wb, hT], writes=[ps])

        def cm_simple(c0, func, dst, ct):
            def fn(wb, off):
                s = stg[st["ns"] % 3]
                st["ns"] += 1
                for tc in range(2):
                    ps = pacc[st["np"] % 4]
                    st["np"] += 1
                    mm_cm(wb, off, ps, tc)
                    kb.op("act", lambda e: e.activation(out=s[:, tc * 512:(tc + 1) * 512], in_=ps[:, :],
                                                        func=func), reads=[ps], writes=[s])
                kb.dma("sp", dst, dst.t[ct, :, t0:t0 + HALF], s, s[:, :])
            return (c0, c0 + 128, fn)

        def cm_rope(c0, dst, g, hd):
            D = DILS[g]
            nl = HALF // D

            def fn(wb, off):
                s = stg[st["ns"] % 3]
                st["ns"] += 1
                sv = s.t[:, :].rearrange("p (r m) -> p r m", r=D)
                for tc in range(2):
                    ps = pacc[st["np"] % 4]
                    st["np"] += 1
                    mm_cm(wb, off, ps, tc)
                    m0 = tc * 512 // D
                    for (lo, hi) in ((32, 64), (64, 128)):
                        kb.op("act", lambda e: e.activation(
                            out=sv[lo:hi, :, m0:m0 + 512 // D],
                            in_=ps.t[lo:hi, :].rearrange("p (m r) -> p r m", r=D), func=AF.Copy),
                            reads=[ps], writes=[s])
                    i = st["nr"] % 4
                    st["nr"] += 1
                    rope_evac(kb, G, (ps[0:32, :], ps), lambda lo, hi: (sv[lo:hi, :, m0:m0 + 512 // D], s),
                              t0 + tc * 512, xr[i], psw[i], rt[i], D)
                dv = dst.t[g, hd, :, :].rearrange("p (r m) -> p r m", r=D)
                kb.dma("sp", dst, dv[:, :, half * nl:(half + 1) * nl], s, sv)
            return (c0, c0 + 128, fn)

        def tm_v(c0, g, cc):
            D = DILS[g]
            nl = HALF // D

            def fn(wb, off):
                units = []
                for vt in range(8):
                    if D == 1:
                        units.append((lambda k, vt=vt: hT[:, k, vt * 128:(vt + 1) * 128], 128, t0 + vt * 128))
                    elif D == 4:
                        r, ml0 = vt // 2, 128 * (vt % 2)
                        units.append((lambda k, r=r, ml0=ml0: hT.t[:, k, :].rearrange(
                            "p (m r) -> p r m", r=4)[:, r, ml0:ml0 + 128], 128, r * 512 + half * 256 + ml0))
                    else:
                        for a in range(2):
                            r = 2 * vt + a
                            units.append((lambda k, r=r: hT.t[:, k, :].rearrange(
                                "p (m r) -> p r m", r=16)[:, r, :], 64, r * 128 + half * 64))
                for ui, (ltf, nr, row) in enumerate(units):
                    ps = pacc[st["np"] % 4]
                    st["np"] += 1
                    for k in range(32):
                        kb.op("pe", lambda e: e.matmul(ps[0:nr, 0:256], lhsT=ltf(k), rhs=wb[:, k, off:off + 256],
                                                       start=(k == 0), stop=(k == 31)),
                              reads=[wb, hT], writes=[ps])
                    vs = vst[st["nv"] % 3]
                    st["nv"] += 1
                    if ui % 2 == 0:
                        kb.op("act", lambda e: e.activation(out=vs[0:nr, :], in_=ps[0:nr, 0:256], func=AF.Copy),
                              reads=[ps], writes=[vs])
                    else:
                        kb.op("dve", lambda e: e.tensor_copy(out=vs[0:nr, :], in_=ps[0:nr, 0:256]),
                              reads=[ps], writes=[vs])
                    dv = G["v"].t[g, :, cc * 256:(cc + 1) * 256]
                    kb.dma("sp", G["v"], dv[row:row + nr, :], vs, vs[0:nr, :])
            return (c0, c0 + 256, fn)

        jobs = []
        for ct in range(8):
            jobs.append(cm_simple(ct * 128, AF.Copy, G["uT"], ct))
        for ct in range(8):
            jobs.append(cm_simple(1024 + ct * 128, AF.Silu, G["zaT"], ct))
        for g in range(3):
            base = L0_QKV0 + g * 3072
            for hd in range(8):
                jobs.append(cm_rope(base + hd * 128, G["qT"], g, hd))
            for hd in range(8):
                jobs.append(cm_rope(base + 1024 + hd * 128, G["kT"], g, hd))
            for cc in range(4):
                jobs.append(tm_v(base + 2048 + cc * 256, g, cc))
        for ct in range(8):
            jobs.append(cm_simple(L0_ZB0 + ct * 128, AF.Silu, G["zbT"], ct))
        stream_proj(kb, G["even_w_in"], 32, jobs, wbufs)
    kb.barrier()


IN_SPECS = [
    ("xin", [SEQ, D_MODEL]),
    ("even_norm", [D_MODEL]), ("even_w_in", [D_MODEL, 12288]),
    ("s5_lambda_re", [64, 64]), ("s5_lambda_im", [64, 64]), ("s5_log_step", [64]),
    ("s5_b_re", [64, 64, 16]), ("s5_b_im", [64, 64, 16]), ("s5_c_re", [64, 16, 64]), ("s5_c_im", [64, 16, 64]),
    ("s5_d", [1024]), ("s5_glu_w", [1024, 1024]), ("s5_glu_b", [1024]), ("even_w_out", [2048, D_MODEL]),
    ("odd_norm", [D_MODEL]), ("odd_w_in", [D_MODEL, 6336]), ("mla_q_norm", [1536]), ("mla_kv_norm", [512]),
    ("idx_k_norm", [128]), ("mla_w_uq", [1536, 4096]), ("mla_w_uk", [512, 32, 96]), ("mla_w_uv", [512, 32, 128]),
    ("idx_w_q", [1536, 4096]), ("odd_w_out", [4096, D_MODEL]), ("final_norm", [D_MODEL]),
    ("c_ident", [128, 128]), ("c_perm32", [32, 32]), ("c_ropeC", [32, SEQ]), ("c_ropeS", [32, SEQ]),
    ("c_maskT", [128, 2, 128]), ("c_rowmask4", [128, 4]), ("c_negmask", [128, 128]),
]


def host_consts():
    half = 16
    inv = (500000.0 ** (-np.arange(0, 32, 2, dtype=np.float32) / np.float32(32))).astype(np.float32)
    ang = np.arange(SEQ, dtype=np.float32)[None, :] * inv[:, None]
    cos = np.cos(ang).astype(np.float32)
    sin = np.sin(ang).astype(np.float32)
    ropeC = np.concatenate([cos, cos], 0)
    ropeS = np.concatenate([-sin, sin], 0)
    perm = np.zeros((32, 32), np.float32)
    for m in range(32):
        perm[(m + 16) % 32, m] = 1.0
    k = np.arange(128)[:, None]
    q = np.arange(128)[None, :]
    maskT = np.stack([(k >= q), (k <= q)], 1).astype(np.float32)
    rowmask4 = (np.arange(128)[:, None] // 32 == np.arange(4)[None, :]).astype(np.float32)
    negmask = np.where(q.T >= k.T, 0.0, -1.0e30).astype(np.float32)
    return {"c_ident": np.eye(128, dtype=np.float32), "c_perm32": perm, "c_ropeC": ropeC, "c_ropeS": ropeS,
            "c_maskT": maskT, "c_rowmask4": rowmask4, "c_negmask": negmask}


def build_program(stop_after=None, debug_out=(), dbg=None, only=None):
    nc = bass.Bass("TRN2", target_bir_lowering=False)
    with ExitStack() as stack:
        kb = KB(nc, stack)
        G = {"_dbg": dbg, "_eng": "dve"}
        for name, shape in IN_SPECS:
            b = Buf(name, nc.dram_tensor(name, list(shape), F32, kind="ExternalInput").ap())
            b.track = False
            G[name] = b

        def scratch(name, shape, dt):
            kind = "ExternalOutput" if name in debug_out else "Internal"
            G[name] = kb.dram(name, shape, dt, kind=kind)

        scratch("uT", [8, 128, SEQ], BF16)
        scratch("zaT", [8, 128, SEQ], BF16)
        scratch("qT", [3, 8, 128, SEQ], BF16)
        scratch("kT", [3, 8, 128, SEQ], BF16)
        scratch("v", [3, SEQ, 1024], BF16)
        scratch("zbT", [8, 128, SEQ], BF16)
        scratch("mixT", [16, 128, SEQ], BF16)
        scratch("x1", [SEQ, D_MODEL], F32)
        scratch("cqT", [12, 128, SEQ], BF16)
        scratch("ckvT", [4, 128, SEQ], BF16)
        scratch("krT", [32, SEQ], BF16)
        scratch("kiT", [128, SEQ], BF16)
        scratch("wi", [SEQ, 32], F32)
        scratch("sgT", [32, 128, SEQ], BF16)
        scratch("q1T", [32, 128, SEQ], BF16)
        scratch("qiT", [32, 128, SEQ], BF16)
        scratch("selT", [16, 128, SEQ], BF16)
        scratch("ogT", [32, 128, SEQ], BF16)
        scratch("x2", [SEQ, D_MODEL], F32)
        G["out"] = kb.dram("out", [SEQ, D_MODEL], F32, kind="ExternalOutput")

        G["ident"] = kb.sb(stack, "ident", [128, 128], BF16)
        G["identf"] = kb.sb(stack, "identf", [128, 128], F32)
        G["perm32"] = kb.sb(stack, "perm32", [32, 32], F32)
        G["ropeC"] = kb.sb(stack, "ropeC", [32, SEQ], F32)
        G["ropeS"] = kb.sb(stack, "ropeS", [32, SEQ], F32)
        G["maskT"] = kb.sb(stack, "maskT", [128, 2, 128], BF16)
        G["rowmask4"] = kb.sb(stack, "rowmask4", [128, 4], F32)
        G["onesb"] = kb.sb(stack, "onesb", [128, 128], BF16)
        G["onesf"] = kb.sb(stack, "onesf", [128, 128], F32)
        kb.dma("pool", G["ident"], G["ident"][:, :], G["c_ident"], G["c_ident"].t[:, :])
        kb.dma("sp", G["identf"], G["identf"][:, :], G["c_ident"], G["c_ident"].t[:, :])
        kb.dma("sp", G["perm32"], G["perm32"][:, :], G["c_perm32"], G["c_perm32"].t[:, :])
        kb.dma("sp", G["ropeC"], G["ropeC"][:, :], G["c_ropeC"], G["c_ropeC"].t[:, :])
        kb.dma("sp", G["ropeS"], G["ropeS"][:, :], G["c_ropeS"], G["c_ropeS"].t[:, :])
        kb.dma("pool", G["maskT"], G["maskT"][:, :, :], G["c_maskT"], G["c_maskT"].t[:, :, :])
        kb.dma("sp", G["rowmask4"], G["rowmask4"][:, :], G["c_rowmask4"], G["c_rowmask4"].t[:, :])
        kb.op("pool", lambda e: e.memset(G["onesb"][:, :], 1.0), writes=[G["onesb"]])
        kb.op("pool", lambda e: e.memset(G["onesf"][:, :], 1.0), writes=[G["onesf"]])

        phases = [
            ("l0_in0", lambda: layer0_inproj(kb, G, 0)),
            ("l0_in1", lambda: layer0_inproj(kb, G, 1)),
            ("l0_s5", lambda: layer0_s5(kb, G)),
            ("l0_attn", lambda: layer0_attn(kb, G)),
            ("l0_out0", lambda: outproj(kb, G, 0, "mixT", 16, "even_w_out", "xin", "x1")),
            ("l0_out1", lambda: outproj(kb, G, 1, "mixT", 16, "even_w_out", "xin", "x1")),
            ("l1_in0", lambda: layer1_inproj(kb, G, 0)),
            ("l1_in1", lambda: layer1_inproj(kb, G, 1)),
            ("l1_q0", lambda: layer1_qproj(kb, G, 0)),
            ("l1_q1", lambda: layer1_qproj(kb, G, 1)),
            ("l1_idx", lambda: layer1_index(kb, G)),
            ("l1_attn", lambda: layer1_attn(kb, G)),
            ("l1_out0", lambda: outproj(kb, G, 0, "ogT", 32, "odd_w_out", "x1", "x2")),
            ("l1_out1", lambda: outproj(kb, G, 1, "ogT", 32, "odd_w_out", "x1", "x2")),
            ("final", lambda: final_norm(kb, G)),
        ]
        for name, fn in phases:
            if only is None or name in only:
                fn()
            if stop_after == name:
                break
        kb.barrier()
    return nc, kb


def make_in_maps(inputs, cores):
    consts = host_consts()
    shared = {}
    for name, shape in IN_SPECS:
        if name == "xin" or name.startswith("c_"):
            continue
        a = np.asarray(inputs[name], dtype=np.float32)
        if name != "final_norm":
            a = a[0]
        shared[name] = np.ascontiguousarray(a).reshape(shape)
    x = np.asarray(inputs["x"], dtype=np.float32)
    maps = []
    for c in cores:
        m = dict(shared)
        m.update(consts)
        m["xin"] = np.ascontiguousarray(x[c % BATCH])
        maps.append(m)
    return maps


def kernel(**inputs):
    nc, kb = build_program()
    cores = list(range(8))
    res = run_bass_kernel_spmd(nc, make_in_maps(inputs, cores), core_ids=cores)
    out = np.stack([np.asarray(res.results[b]["out"], dtype=np.float32) for b in range(BATCH)], 0)
    return out


def TT(kb, eng, out, a, b, op):
    kb.op(eng, lambda e: e.tensor_tensor(out=out[0], in0=a[0], in1=b[0], op=op),
          reads=[a[1], b[1]], writes=[out[1]])


def TS(kb, eng, out, a, s1, op0, s2=None, op1=None):
    rd = [a[1]]
    s1v, s2v = s1, s2
    if isinstance(s1, tuple):
        rd.append(s1[1])
        s1v = s1[0]
    if isinstance(s2, tuple):
        rd.append(s2[1])
        s2v = s2[0]
    if op1 is None:
        kb.op(eng, lambda e: e.tensor_scalar(out=out[0], in0=a[0], scalar1=s1v, scalar2=None, op0=op0),
              reads=rd, writes=[out[1]])
    else:
        kb.op(eng, lambda e: e.tensor_scalar(out=out[0], in0=a[0], scalar1=s1v, scalar2=s2v, op0=op0, op1=op1),
              reads=rd, writes=[out[1]])


def ACTF(kb, out, a, func, scale=None, bias=None):
    rd = [a[1]]
    kw = {}
    if scale is not None:
        if isinstance(scale, tuple):
            rd.append(scale[1])
            kw["scale"] = scale[0]
        else:
            kw["scale"] = scale
    if bias is not None:
        rd.append(bias[1])
        kw["bias"] = bias[0]
    kb.op("act", lambda e: e.activation(out=out[0], in_=a[0], func=func, **kw), reads=rd, writes=[out[1]])


def W(b, ap=None):
    if ap is None:
        nd = len(b.t.shape)
        ap = b.t[tuple(slice(None) for _ in range(nd))]
    return (ap, b)


def layer0_s5(kb, G):
    TWO_PI = 2.0 * math.pi
    with ExitStack() as c:
        def T(name, shape=(128, 32), dt=F32):
            return kb.sb(c, name, list(shape), dt)
        EBre = T("EBre", [128, 32, 128], BF16)
        EBim = T("EBim", [128, 32, 128], BF16)
        ECre = T("ECre", [128, 32, 128], BF16)
        ECim = T("ECim", [128, 32, 128], BF16)
        Ec = T("Ecos", [128, 32, 128])
        Es = T("Esin", [128, 32, 128])
        mag = T("mag")
        dsk = T("dsk", [128, 8])
        glb = T("glb", [128, 8])
        gw = T("gw", [128, 8, 1024], BF16)
        carry = [T("carry%d" % i, [128, 4, 2]) for i in range(8)]
        kb.dma("pool", gw, gw[:, :, :], G["s5_glu_w"], G["s5_glu_w"].t.rearrange("(k p) n -> p k n", p=128))
        for cb_ in carry:
            kb.op("pool", lambda e: e.memset(cb_[:, :, :], 0.0), writes=[cb_])

        with ExitStack() as c2:
            def T2(name, shape=(128, 32), dt=F32):
                return kb.sb(c2, name, list(shape), dt)
            lamR, lamI, dtl = T2("lamR"), T2("lamI"), T2("dtl")
            PT = T2("PT", [32, 3, 128])
            LS = T2("LS", [32, 2])
            PD = T2("PD", [8, 2, 128])
            kb.dma("sp", PT, PT[:, 0, :], G["s5_lambda_re"],
                   G["s5_lambda_re"].t.rearrange("(pr g2) p -> pr (g2 p)", g2=2))
            kb.dma("sp", PT, PT[:, 1, :], G["s5_lambda_im"],
                   G["s5_lambda_im"].t.rearrange("(pr g2) p -> pr (g2 p)", g2=2))
            kb.dma("sp", LS, LS[:, :], G["s5_log_step"], G["s5_log_step"].t.rearrange("(pr g2) -> pr g2", g2=2))
            kb.dma("sp", PD, PD[:, 0, :], G["s5_d"], G["s5_d"].t.rearrange("(ct c) -> ct c", c=128))
            kb.dma("sp", PD, PD[:, 1, :], G["s5_glu_b"], G["s5_glu_b"].t.rearrange("(ct c) -> ct c", c=128))
            kb.op("dve", lambda e: e.tensor_copy(
                out=PT.t[:, 2, :].rearrange("pr (g2 p) -> pr g2 p", g2=2),
                in_=LS.t[:, :].unsqueeze(2).broadcast_to([32, 2, 64])), reads=[LS, PT], writes=[PT])
            ppt = kb.ps(c2, "ppt", [128, 3, 32], F32)
            ppd = kb.ps(c2, "ppd", [128, 2, 8], F32)
            for j in range(3):
                kb.op("pe", lambda e: e.transpose(out=ppt[:, j, :], in_=PT[:, j, :], identity=G["identf"][0:32, 0:32]),
                      reads=[PT, G["identf"]], writes=[ppt])
            for j in range(2):
                kb.op("pe", lambda e: e.transpose(out=ppd[:, j, :], in_=PD[:, j, :], identity=G["identf"][0:8, 0:8]),
                      reads=[PD, G["identf"]], writes=[ppd])
            for j, dst in enumerate((lamR, lamI, dtl)):
                kb.op("dve", lambda e: e.tensor_copy(out=dst[:, :], in_=ppt[:, j, :]), reads=[ppt], writes=[dst])
            for j, dst in enumerate((dsk, glb)):
                kb.op("dve", lambda e: e.tensor_copy(out=dst[:, :], in_=ppd[:, j, :]), reads=[ppd], writes=[dst])
            lr, dt_, xx, pp, ang, ff, kf, th, ath = (T2(n) for n in
                                                     ("lr", "dt", "xx", "pp", "ang", "ff", "kf", "th", "ath"))
            ki = T2("ki", dt=mybir.dt.int32)
            cth, sth, ar, ai, den, rden, nr, t1, t2, cr, ci = (T2(n) for n in (
                "cth", "sth", "ar", "ai", "den", "rden", "nr", "t1", "t2", "cr", "ci"))
            TS(kb, "dve", W(lr), W(lamR), -1e-4, ALU.min)
            TS(kb, "dve", W(xx), W(dtl), 0.125, ALU.mult)
            TS(kb, "dve", W(pp), W(xx), 1.0 / 12.0, ALU.mult, 1.0, ALU.add)
            for kk in range(11, 0, -1):
                TT(kb, "dve", W(pp), W(pp), W(xx), ALU.mult)
                TS(kb, "dve", W(pp), W(pp), 1.0 / kk, ALU.mult, 1.0, ALU.add)
            TT(kb, "dve", W(dt_), W(pp), W(pp), ALU.mult)
            TT(kb, "dve", W(dt_), W(dt_), W(dt_), ALU.mult)
            TT(kb, "dve", W(dt_), W(dt_), W(dt_), ALU.mult)
            TT(kb, "dve", W(xx), W(lr), W(dt_), ALU.mult)
            TS(kb, "dve", W(pp), W(xx), 1.0 / 7.0, ALU.mult, 1.0, ALU.add)
            for kk in (6, 5, 4, 3, 2, 1):
                TT(kb, "dve", W(pp), W(pp), W(xx), ALU.mult)
                TS(kb, "dve", W(pp), W(pp), 1.0 / kk, ALU.mult, 1.0, ALU.add)
            kb.op("dve", lambda e: e.tensor_copy(out=mag[:, :], in_=pp[:, :]), reads=[pp], writes=[mag])
            TT(kb, "dve", W(ang), W(lamI), W(dt_), ALU.mult)
            TS(kb, "dve", W(ff), W(ang), 1.0 / TWO_PI, ALU.mult)
            kb.op("dve", lambda e: e.tensor_copy(out=ki[:, :], in_=ff[:, :]), reads=[ff], writes=[ki])
            kb.op("dve", lambda e: e.tensor_copy(out=kf[:, :], in_=ki[:, :]), reads=[ki], writes=[kf])
            TT(kb, "dve", W(ff), W(ff), W(kf), ALU.subtract)
            TS(kb, "dve", W(th), W(ff), TWO_PI, ALU.mult, math.pi, ALU.min)
            TS(kb, "dve", W(th), W(th), -math.pi, ALU.max)
            ACTF(kb, W(sth), W(th), AF.Sin)
            ACTF(kb, W(ath), W(th), AF.Abs)
            TS(kb, "dve", W(ath), W(ath), -1.0, ALU.mult, math.pi / 2.0, ALU.add)
            ACTF(kb, W(cth), W(ath), AF.Sin)
            TT(kb, "dve", W(ar), W(mag), W(cth), ALU.mult)
            TT(kb, "dve", W(ai), W(mag), W(sth), ALU.mult)
            TT(kb, "dve", W(den), W(lr), W(lr), ALU.mult)
            TT(kb, "dve", W(t1), W(lamI), W(lamI), ALU.mult)
            TT(kb, "dve", W(den), W(den), W(t1), ALU.add)
            kb.op("dve", lambda e: e.reciprocal(out=rden[:, :], in_=den[:, :]), reads=[den], writes=[rden])
            TS(kb, "dve", W(nr), W(ar), -1.0, ALU.add)
            TT(kb, "dve", W(t1), W(nr), W(lr), ALU.mult)
            TT(kb, "dve", W(t2), W(ai), W(lamI), ALU.mult)
            TT(kb, "dve", W(t1), W(t1), W(t2), ALU.add)
            TT(kb, "dve", W(cr), W(t1), W(rden), ALU.mult)
            TT(kb, "dve", W(t1), W(ai), W(lr), ALU.mult)
            TT(kb, "dve", W(t2), W(nr), W(lamI), ALU.mult)
            TT(kb, "dve", W(t1), W(t1), W(t2), ALU.subtract)
            TT(kb, "dve", W(ci), W(t1), W(rden), ALU.mult)
            bre, bim = T2("bre", [128, 32, 16]), T2("bim", [128, 32, 16])
            for (dst, nm) in ((bre, "s5_b_re"), (bim, "s5_b_im")):
                kb.dma("sp", dst, dst[:, :, :], G[nm], G[nm].t.rearrange("(pr g2) p j -> (g2 p) pr j", g2=2))
            u1, u2 = T2("u1", [128, 32, 16]), T2("u2", [128, 32, 16])
            X2r, X2i = T2("X2r", [128, 32, 32], BF16), T2("X2i", [128, 32, 32], BF16)
            kb.op("pool", lambda e: e.memset(X2r[:, :, :], 0.0), writes=[X2r])
            kb.op("pool", lambda e: e.memset(X2i[:, :, :], 0.0), writes=[X2i])
            crb = (cr.t[:, :].unsqueeze(2).broadcast_to([128, 32, 16]), cr)
            cib = (ci.t[:, :].unsqueeze(2).broadcast_to([128, 32, 16]), ci)
            for (X2, pa, pb, opc) in ((X2r, bre, bim, ALU.subtract), (X2i, bim, bre, ALU.add)):
                TT(kb, "dve", W(u1), W(pa), crb, ALU.mult)
                TT(kb, "dve", W(u2), W(pb), cib, ALU.mult)
                for g2 in range(2):
                    lo, hi = 64 * g2, 64 * g2 + 64
                    TT(kb, "dve", (X2[lo:hi, :, 16 * g2:16 * g2 + 16], X2), (u1[lo:hi, :, :], u1),
                       (u2[lo:hi, :, :], u2), opc)
            ptp = [kb.ps(c2, "s5tp%d" % i, [128, 128], BF16) for i in range(2)]
            ntp = 0
            for (X2, EB) in ((X2r, EBre), (X2i, EBim)):
                for ct in range(8):
                    p_b = ptp[ntp % 2]
                    ntp += 1
                    kb.op("pe", lambda e: e.transpose(out=p_b[:, :], in_=X2.t.rearrange("q pr x -> q (pr x)")[:, ct * 128:(ct + 1) * 128],
                                                      identity=G["ident"][:, :]),
                          reads=[X2, G["ident"]], writes=[p_b])
                    for a in range(4):
                        TS(kb, "dve", (EB[:, 4 * ct + a, :], EB), W(p_b),
                           (G["rowmask4"][:, a:a + 1], G["rowmask4"]), ALU.mult)
            Cn = T2("Cn", [128, 8, 2, 64])
            Cb = T2("Cb", [128, 8, 128], BF16)
            for (nm, EC, sc) in (("s5_c_re", ECre, 1.0), ("s5_c_im", ECim, -1.0)):
                srcv = G[nm].t.rearrange("g j p -> (g j) p").rearrange("(ct c) p -> c ct p", c=128)
                for dup in range(2):
                    kb.dma("sp", Cn, Cn[:, :, dup, :], G[nm], srcv)
                kb.op("act", lambda e: e.activation(out=Cb[:, :, :], in_=Cn.t.rearrange("c ct d p -> c ct (d p)"),
                                                    func=AF.Copy, scale=sc), reads=[Cn], writes=[Cb])
                kb.op("pool", lambda e: e.memset(EC[:, :, :], 0.0), writes=[EC])
                for ct in range(8):
                    p_b = ptp[ntp % 2]
                    ntp += 1
                    kb.op("pe", lambda e: e.transpose(out=p_b[:, :], in_=Cb[:, ct, :], identity=G["ident"][:, :]),
                          reads=[Cb, G["ident"]], writes=[p_b])
                    for a in range(4):
                        for g2 in range(2):
                            lo, hi = 64 * g2, 64 * g2 + 64
                            c0 = 32 * a + 16 * g2
                            kb.op("act" if g2 == 0 else "dve",
                                  (lambda e: e.activation(out=EC[lo:hi, 4 * ct + a, c0:c0 + 16],
                                                          in_=p_b[lo:hi, c0:c0 + 16], func=AF.Copy)) if g2 == 0 else
                                  (lambda e: e.tensor_copy(out=EC[lo:hi, 4 * ct + a, c0:c0 + 16],
                                                           in_=p_b[lo:hi, c0:c0 + 16])),
                                  reads=[p_b], writes=[EC])
            wc, ws, wt = T2("wc"), T2("ws"), T2("wt")
            m1, m2 = T2("m1", [128, 32, 64]), T2("m2", [128, 32, 64])
            kb.op("dve", lambda e: e.tensor_copy(out=wc[:, :], in_=cth[:, :]), reads=[cth], writes=[wc])
            kb.op("dve", lambda e: e.tensor_copy(out=ws[:, :], in_=sth[:, :]), reads=[sth], writes=[ws])
            kb.op("dve", lambda e: e.tensor_copy(out=Ec[:, :, 0], in_=cth[:, :]), reads=[cth], writes=[Ec])
            kb.op("dve", lambda e: e.tensor_copy(out=Es[:, :, 0], in_=sth[:, :]), reads=[sth], writes=[Es])
            n = 1
            while n < 128:
                wcb = (wc.t[:, :].unsqueeze(2).broadcast_to([128, 32, n]), wc)
                wsb = (ws.t[:, :].unsqueeze(2).broadcast_to([128, 32, n]), ws)
                TT(kb, "dve", (m1[:, :, 0:n], m1), (Ec[:, :, 0:n], Ec), wcb, ALU.mult)
                TT(kb, "dve", (m2[:, :, 0:n], m2), (Es[:, :, 0:n], Es), wsb, ALU.mult)
                TT(kb, "dve", (Ec[:, :, n:2 * n], Ec), (m1[:, :, 0:n], m1), (m2[:, :, 0:n], m2), ALU.subtract)
                TT(kb, "dve", (m1[:, :, 0:n], m1), (Ec[:, :, 0:n], Ec), wsb, ALU.mult)
                TT(kb, "dve", (m2[:, :, 0:n], m2), (Es[:, :, 0:n], Es), wcb, ALU.mult)
                TT(kb, "dve", (Es[:, :, n:2 * n], Es), (m1[:, :, 0:n], m1), (m2[:, :, 0:n], m2), ALU.add)
                TT(kb, "dve", W(wt), W(wc), W(ws), ALU.mult)
                TT(kb, "dve", W(wc), W(wc), W(wc), ALU.mult)
                TT(kb, "dve", W(ws), W(ws), W(ws), ALU.mult)
                TT(kb, "dve", W(wc), W(wc), W(ws), ALU.subtract)
                TS(kb, "dve", W(ws), W(wt), 2.0, ALU.mult)
                n *= 2
        kb.barrier()
        if G.get("_dbg") == "s5prep":
            for nm, tl in (("EBre", EBre), ("EBim", EBim), ("ECre", ECre), ("ECim", ECim), ("Ecos", Ec), ("Esin", Es)):
                d = kb.dram("dbg_" + nm, list(tl.t.shape), tl.t.dtype, kind="ExternalOutput")
                kb.dma("sp", d, d.t[:, :, :], tl, tl[:, :, :])
            for nm, tl in (("mag", mag), ("dsk", dsk), ("glb", glb)):
                d = kb.dram("dbg_" + nm, list(tl.t.shape), tl.t.dtype, kind="ExternalOutput")
                kb.dma("sp", d, d.t[:, :], tl, tl[:, :])
            kb.barrier()
            return

        uTc = [T("uTc%d" % i, [128, 8, 512], BF16) for i in range(2)]
        zac = [T("zac%d" % i, [128, 8, 512], BF16) for i in range(2)]
        Yt = [T("Yt%d" % i, [128, 8, 512], BF16) for i in range(2)]
        psR = [kb.ps(c, "psR%d" % i, [128, 4, 128]) for i in range(2)]
        psI = [kb.ps(c, "psI%d" % i, [128, 4, 128]) for i in range(2)]
        psY = [kb.ps(c, "psY%d" % i, [128, 128]) for i in range(2)]
        psG = [kb.ps(c, "psG%d" % i, [128, 512]) for i in range(2)]
        NB = 2
        A = [[T("A%d_%d" % (j, i), [128, 4, 128]) for j in range(4)] for i in range(NB)]
        Rin = [[T("Rin%d_%d" % (j, i), [128, 4, 128]) for j in range(2)] for i in range(NB)]
        Rs = [[T("Rs%d_%d" % (j, i), [128, 4, 128]) for j in range(2)] for i in range(NB)]
        Bm = A
        S = [[T("S%d_%d" % (j, i), [128, 4, 128]) for j in range(2)] for i in range(NB)]
        Sb = [[T("Sb%d_%d" % (j, i), [128, 4, 128], BF16) for j in range(2)] for i in range(NB)]
        y1 = [T("y1_%d" % i, [128, 128]) for i in range(2)]
        sg = [T("sg%d" % i, [128, 512]) for i in range(2)]
        a1 = [T("a1_%d" % i, [128, 512]) for i in range(2)]
        ost = [T("ost%d" % i, [128, 512], BF16) for i in range(3)]
        it = 0
        ny = 0
        ng = 0
        for blk in range(4):
            ub, zb_, yb = uTc[blk % 2], zac[blk % 2], Yt[blk % 2]
            tsl = slice(blk * 512, (blk + 1) * 512)
            kb.dma("sp", ub, ub[:, :, :], G["uT"], G["uT"].t[:, :, tsl].rearrange("ct p t -> p ct t"))
            kb.dma("sp", zb_, zb_[:, :, :], G["zaT"], G["zaT"].t[:, :, tsl].rearrange("ct p t -> p ct t"))
            its = [(ch, pg) for ch in range(4) for pg in range(8)]

            def s1(idx):
                ch, pg = its[idx]
                csl = slice(ch * 128, (ch + 1) * 128)
                i = idx % NB
                pR, pI = psR[i], psI[i]
                Ec4, Es4 = (Ec[:, 4 * pg:4 * pg + 4, :], Ec), (Es[:, 4 * pg:4 * pg + 4, :], Es)
                for a in range(4):
                    pr = 4 * pg + a
                    kb.op("pe", lambda e: e.matmul(pR[:, a, :], lhsT=EBre[:, pr, :], rhs=ub[:, pg, csl],
                                                   start=True, stop=True), reads=[EBre, ub], writes=[pR])
                for a in range(4):
                    pr = 4 * pg + a
                    kb.op("pe", lambda e: e.matmul(pI[:, a, :], lhsT=EBim[:, pr, :], rhs=ub[:, pg, csl],
                                                   start=True, stop=True), reads=[EBim, ub], writes=[pI])
                A0, A1, A2, A3 = A[i]
                TT(kb, "dve", W(A0), W(pR), Ec4, ALU.mult)
                TT(kb, "dve", W(A1), W(pI), Es4, ALU.mult)
                TT(kb, "dve", W(A2), W(pI), Ec4, ALU.mult)
                TT(kb, "dve", W(A3), W(pR), Es4, ALU.mult)
                TT(kb, "pool", W(Rin[i][0]), W(A0), W(A1), ALU.add)
                TT(kb, "pool", W(Rin[i][1]), W(A2), W(A3), ALU.subtract)

            def s2(idx):
                ch, pg = its[idx]
                i = idx % NB
                Ec4, Es4 = (Ec[:, 4 * pg:4 * pg + 4, :], Ec), (Es[:, 4 * pg:4 * pg + 4, :], Es)
                for a in range(4):
                    pr = 4 * pg + a
                    for j in range(2):
                        kb.op("dve", lambda e: e.tensor_tensor_scan(
                            out=Rs[i][j][:, a, :], data0=mag.t[:, pr:pr + 1].to_broadcast([128, 128]),
                            data1=Rin[i][j][:, a, :], initial=carry[pg][:, a, j:j + 1],
                            op0=ALU.mult, op1=ALU.add), reads=[mag, Rin[i][j], carry[pg]], writes=[Rs[i][j]])
                B0, B1, B2, B3 = Bm[i]
                TT(kb, "pool", W(B0), W(Rs[i][0]), Ec4, ALU.mult)
                TT(kb, "pool", W(B1), W(Rs[i][1]), Es4, ALU.mult)
                TT(kb, "pool", W(B2), W(Rs[i][0]), Es4, ALU.mult)
                TT(kb, "pool", W(B3), W(Rs[i][1]), Ec4, ALU.mult)

            def s3(idx):
                ch, pg = its[idx]
                i = idx % NB
                B0, B1, B2, B3 = Bm[i]
                TT(kb, "dve", W(S[i][0]), W(B0), W(B1), ALU.subtract)
                TT(kb, "dve", W(S[i][1]), W(B2), W(B3), ALU.add)
                for j in range(2):
                    kb.op("act", lambda e: e.activation(out=carry[pg][:, :, j], in_=S[i][j][:, :, 127],
                                                        func=AF.Copy), reads=[S[i][j]], writes=[carry[pg]])
                    kb.op("act", lambda e: e.activation(out=Sb[i][j][:, :, :], in_=S[i][j][:, :, :],
                                                        func=AF.Copy), reads=[S[i][j]], writes=[Sb[i][j]])
                pY = psY[idx % 2]
                for a in range(4):
                    pr = 4 * pg + a
                    kb.op("pe", lambda e: e.matmul(pY[:, :], lhsT=ECre[:, pr, :], rhs=Sb[i][0][:, a, :],
                                                   start=(a == 0), stop=False), reads=[ECre, Sb[i][0]], writes=[pY])
                    kb.op("pe", lambda e: e.matmul(pY[:, :], lhsT=ECim[:, pr, :], rhs=Sb[i][1][:, a, :],
                                                   start=False, stop=(a == 3)), reads=[ECim, Sb[i][1]], writes=[pY])

            def s4(idx):
                ch, pg = its[idx]
                csl = slice(ch * 128, (ch + 1) * 128)
                pY = psY[idx % 2]
                yy = y1[idx % 2]
                kb.op("dve", lambda e: e.scalar_tensor_tensor(out=yy[:, :], in0=ub[:, pg, csl],
                                                              scalar=dsk[:, pg:pg + 1], in1=pY[:, :],
                                                              op0=ALU.mult, op1=ALU.add),
                      reads=[ub, dsk, pY], writes=[yy])
                kb.op("act", lambda e: e.activation(out=yb[:, pg, csl], in_=yy[:, :], func=AF.Gelu_apprx_tanh),
                      reads=[yy], writes=[yb])

            nit = len(its)
            for step in range(nit + 3):
                if 0 <= step - 3 < nit:
                    s4(step - 3)
                if 0 <= step - 2 < nit:
                    s3(step - 2)
                if 0 <= step - 1 < nit:
                    s2(step - 1)
                if step < nit:
                    s1(step)
            if G.get("_dbg") in ("s5a", "s5b", "s5c", "s5d"):
                continue
            for co in range(8):
                pG = psG[ng % 2]
                sgb, a1b = sg[ng % 2], a1[ng % 2]
                ob = ost[ng % 3]
                ng += 1
                for ct in range(8):
                    kb.op("pe", lambda e: e.matmul(pG[:, :], lhsT=gw[:, ct, co * 128:(co + 1) * 128], rhs=yb[:, ct, :],
                                                   start=(ct == 0), stop=(ct == 7)), reads=[gw, yb], writes=[pG])
                kb.op("act", lambda e: e.activation(out=sgb[:, :], in_=pG[:, :], func=AF.Sigmoid,
                                                    bias=glb[:, co:co + 1]), reads=[pG, glb], writes=[sgb])
                if G.get("_dbg") == "s5e":
                    continue
                TT(kb, "pool", W(a1b), W(sgb), (yb[:, co, :], yb), ALU.mult)
                TT(kb, G.get("_eng", "pool"), W(ob), W(a1b), (zb_[:, co, :], zb_), ALU.mult)
                if G.get("_dbg") == "s5f":
                    continue
                kb.dma("sp", G["mixT"], G["mixT"].t[co, :, tsl], ob, ob[:, :])
    kb.barrier()


def layer0_attn(kb, G):
    with ExitStack() as c:
        QT = [[kb.sb(c, "QT%d_%d" % (g, i), [128, SEQ], BF16) for g in range(3)] for i in range(2)]
        KT = [[kb.sb(c, "KT%d_%d" % (g, i), [128, SEQ], BF16) for g in range(3)] for i in range(2)]
        VV = [[kb.sb(c, "VV%d_%d" % (g, i), [128, 16, 128], BF16) for g in range(3)] for i in range(2)]
        zbh = [kb.sb(c, "zbh%d" % i, [128, SEQ], BF16) for i in range(2)]
        Oacc = [kb.sb(c, "Oacc%d" % i, [128, SEQ], F32) for i in range(2)]
        Dacc = [kb.sb(c, "Dacc%d" % i, [1, SEQ], F32) for i in range(2)]
        rD = [kb.sb(c, "rD%d" % i, [1, SEQ], F32) for i in range(2)]
        ao = [kb.sb(c, "ao%d" % i, [128, SEQ], BF16) for i in range(2)]
        pT = [kb.sb(c, "pT%d" % i, [128, 2, 128], BF16) for i in range(3)]
        ps_s = [kb.ps(c, "pss%d" % i, [128, 2, 128]) for i in range(2)]
        ps_o = [kb.ps(c, "pso%d" % i, [128, 4, 128]) for i in range(2)]
        ps_d = [kb.ps(c, "psd%d" % i, [1, 4, 128]) for i in range(2)]
        ps_b = [kb.ps(c, "psb%d" % i, [128, 512]) for i in range(2)]
        nu = 0
        nb_ = 0
        nbb = 0
        for hd in range(8):
            s = hd % 2
            for g in range(3):
                kb.dma("sp", QT[s][g], QT[s][g][:, :], G["qT"], G["qT"].t[g, hd, :, :])
                kb.dma("sp", KT[s][g], KT[s][g][:, :], G["kT"], G["kT"].t[g, hd, :, :])
                kb.dma("sp", VV[s][g], VV[s][g][:, :, :], G["v"],
                       G["v"].t[g, :, hd * 128:(hd + 1) * 128].rearrange("(t p) d -> p t d", p=128))
            kb.dma("sp", zbh[s], zbh[s][:, :], G["zbT"], G["zbT"].t[hd, :, :])
            O, Dn = Oacc[s], Dacc[s]
            for g in range(3):
                Q, Kt, V = QT[s][g], KT[s][g], VV[s][g]
                for bt in range(4):
                    po, pd = ps_o[nb_ % 2], ps_d[nb_ % 2]
                    nb_ += 1
                    for j in range(4):
                        if g == 0:
                            cur = 4 * bt + j
                            prev = cur - 1 if cur > 0 else None
                        elif g == 1:
                            cur = 4 * bt + j
                            prev = cur - 1 if j > 0 else None
                        else:
                            cur = 4 * bt + j
                            prev = None
                        qs = slice(cur * 128, (cur + 1) * 128)
                        ss = ps_s[nu % 2]
                        p = pT[nu % 3]
                        nu += 1
                        kb.op("pe", lambda e: e.matmul(ss[:, 1, :], lhsT=Kt[:, qs], rhs=Q[:, qs], start=True, stop=True),
                              reads=[Kt, Q], writes=[ss])
                        lo = 1
                        if prev is not None:
                            lo = 0
                            kb.op("pe", lambda e: e.matmul(ss[:, 0, :], lhsT=Kt[:, prev * 128:(prev + 1) * 128],
                                                           rhs=Q[:, qs], start=True, stop=True),
                                  reads=[Kt, Q], writes=[ss])
                        kb.op("act", lambda e: e.activation(out=p[:, lo:2, :], in_=ss[:, lo:2, :], func=AF.Exp,
                                                            scale=ATT_SCALE), reads=[ss], writes=[p])
                        kb.op("dve", lambda e: e.tensor_tensor(out=p[:, lo:2, :], in0=p[:, lo:2, :],
                                                               in1=G["maskT"][:, lo:2, :], op=ALU.mult),
                              reads=[p, G["maskT"]], writes=[p])
                        kb.op("pe", lambda e: e.matmul(po[:, j, :], lhsT=V[:, cur, :], rhs=p[:, 1, :],
                                                       start=True, stop=(prev is None)), reads=[V, p], writes=[po])
                        if prev is not None:
                            kb.op("pe", lambda e: e.matmul(po[:, j, :], lhsT=V[:, prev, :], rhs=p[:, 0, :],
                                                           start=False, stop=True), reads=[V, p], writes=[po])
                        kb.op("pe", lambda e: e.matmul(pd[0:1, j, :], lhsT=G["onesb"][:, 0:1], rhs=p[:, 1, :],
                                                       start=True, stop=(prev is None)), reads=[G["onesb"], p], writes=[pd])
                        if prev is not None:
                            kb.op("pe", lambda e: e.matmul(pd[0:1, j, :], lhsT=G["onesb"][:, 0:1], rhs=p[:, 0, :],
                                                           start=False, stop=True), reads=[G["onesb"], p], writes=[pd])
                    pof = po.t.rearrange("p a q -> p (a q)")
                    pdf = pd.t.rearrange("p a q -> p (a q)")
                    if g == 0:
                        cs = slice(bt * 512, (bt + 1) * 512)
                        kb.op("act", lambda e: e.activation(out=O[:, cs], in_=pof, func=AF.Copy), reads=[po], writes=[O])
                        kb.op("act", lambda e: e.activation(out=Dn[0:1, cs], in_=pdf, func=AF.Copy),
                              reads=[pd], writes=[Dn])
                    elif g == 1:
                        ov = O.t.rearrange("p (m r) -> p r m", r=4)[:, bt, :]
                        dv = Dn.t.rearrange("p (m r) -> p r m", r=4)[:, bt, :]
                        kb.op("dve", lambda e: e.tensor_tensor(out=ov, in0=pof, in1=ov, op=ALU.add),
                              reads=[po, O], writes=[O])
                        kb.op("dve", lambda e: e.tensor_tensor(out=dv, in0=pdf, in1=dv, op=ALU.add),
                              reads=[pd, Dn], writes=[Dn])
                    else:
                        ov = O.t.rearrange("p (m r) -> p r m", r=16)[:, 4 * bt:4 * bt + 4, :]
                        dv = Dn.t.rearrange("p (m r) -> p r m", r=16)[:, 4 * bt:4 * bt + 4, :]
                        kb.op("dve", lambda e: e.tensor_tensor(out=ov, in0=po[:, :, :], in1=ov, op=ALU.add),
                              reads=[po, O], writes=[O])
                        kb.op("dve", lambda e: e.tensor_tensor(out=dv, in0=pd[0:1, :, :], in1=dv, op=ALU.add),
                              reads=[pd, Dn], writes=[Dn])
            r_ = rD[s]
            kb.op("dve", lambda e: e.reciprocal(out=r_[0:1, :], in_=Dn[0:1, :]), reads=[Dn], writes=[r_])
            a_ = ao[s]
            for tcx in range(4):
                cs = slice(tcx * 512, (tcx + 1) * 512)
                pb = ps_b[nbb % 2]
                nbb += 1
                kb.op("pe", lambda e: e.matmul(pb[:, :], lhsT=G["onesf"][0:1, 0:128], rhs=r_[0:1, cs],
                                               start=True, stop=True), reads=[G["onesf"], r_], writes=[pb])
                kb.op("dve", lambda e: e.tensor_tensor(out=O[:, cs], in0=O[:, cs], in1=pb[:, :], op=ALU.mult),
                      reads=[O, pb], writes=[O])
                kb.op("dve", lambda e: e.tensor_tensor(out=a_[:, cs], in0=O[:, cs], in1=zbh[s][:, cs], op=ALU.mult),
                      reads=[O, zbh[s]], writes=[a_])
            kb.dma("sp", G["mixT"], G["mixT"].t[8 + hd, :, :], a_, a_[:, :])
    kb.barrier()


def outproj(kb, G, half, src, KT, wname, resid, dst):
    t0 = half * HALF
    with ExitStack() as c:
        mT = kb.sb(c, "mT", [128, KT, HALF], BF16)
        kb.dma("sp", mT, mT[:, :, :], G[src], G[src].t[:, :, t0:t0 + HALF].rearrange("k p t -> p k t"))
        wbufs = [kb.sb(c, "wo%d" % i, [128, KT, 512], BF16) for i in range(2)]
        xr = [kb.sb(c, "xo%d" % i, [128, 512], F32) for i in range(3)]
        pacc = [kb.ps(c, "po%d" % i, [128, 512]) for i in range(4)]
        wv = G[wname].t.rearrange("(k p) n -> p k n", p=128)
        n = 0
        for cc in range(8):
            wb = wbufs[cc % 2]
            cs = slice(cc * 512, (cc + 1) * 512)
            kb.dma("pool", wb, wb[:, :, :], G[wname], wv[:, :, cs])
            for ti in range(8):
                x_b = xr[n % 3]
                ps = pacc[n % 4]
                n += 1
                rows = slice(t0 + ti * 128, t0 + (ti + 1) * 128)
                kb.dma("sp", x_b, x_b[:, :], G[resid], G[resid].t[rows, cs])
                for k in range(KT):
                    kb.op("pe", lambda e: e.matmul(ps[:, :], lhsT=mT[:, k, ti * 128:(ti + 1) * 128], rhs=wb[:, k, :],
                                                   start=(k == 0), stop=(k == KT - 1)), reads=[mT, wb], writes=[ps])
                kb.op("dve", lambda e: e.tensor_tensor(out=x_b[:, :], in0=ps[:, :], in1=x_b[:, :], op=ALU.add),
                      reads=[ps, x_b], writes=[x_b])
                kb.dma("sp", G[dst], G[dst].t[rows, cs], x_b, x_b[:, :])
    kb.barrier()


Q_RANK, KV_RANK = 1536, 512
L1_KR0, L1_KI0, L1_WI0, L1_GATE0 = 2048, 2080, 2208, 2240
NEG = -1.0e30


def load_cols_T(kb, c, srcs, ncol):
    outs = [kb.sb(c, "lc_o%d" % j, [128, ncol], F32) for j in range(len(srcs))]
    with ExitStack() as c2:
        st = kb.sb(c2, "lc_st", [ncol, len(srcs), 128], F32)
        pp = kb.ps(c2, "lc_ps", [128, len(srcs), ncol], F32)
        for j, sb_ in enumerate(srcs):
            kb.dma("sp", st, st[:, j, :], sb_, sb_.t.rearrange("(ct c) -> ct c", c=128))
        for j in range(len(srcs)):
            kb.op("pe", lambda e: e.transpose(out=pp[:, j, :], in_=st[:, j, :], identity=G_IDENTF[0][0:ncol, 0:ncol]),
                  reads=[st, G_IDENTF[0]], writes=[pp])
            o = outs[j]
            kb.op("dve", lambda e: e.tensor_copy(out=o[:, :], in_=pp[:, j, :]), reads=[pp], writes=[o])
        kb.barrier()
    return outs


G_IDENTF = [None]


def layer1_inproj(kb, G, half):
    t0 = half * HALF
    G_IDENTF[0] = G["identf"]
    with ExitStack() as c:
        hT = kb.sb(c, "hT1", [128, 32, HALF], BF16)
        phase_norm_T(kb, G, G["x1"], t0, G["odd_norm"], hT)
        (gq,) = load_cols_T(kb, c, [G["mla_q_norm"]], 12)
        (gkv,) = load_cols_T(kb, c, [G["mla_kv_norm"]], 4)
        (gki,) = load_cols_T(kb, c, [G["idx_k_norm"]], 1)
        wbufs = [kb.sb(c, "wb%d" % i, [128, 32, 256], BF16) for i in range(3)]
        pacc = [kb.ps(c, "pacc%d" % i, [128, 512], F32) for i in range(3)]
        pssq = [kb.ps(c, "pssq%d" % i, [128, 512], F32) for i in range(2)]
        psw = kb.ps(c, "psw", [32, 512], F32)
        ptr = kb.ps(c, "ptr", [128, 4, 32], F32)
        raw = kb.sb(c, "raw", [128, 12, HALF], F32)
        sq = [kb.sb(c, "sq%d" % i, [128, 512], BF16) for i in range(2)]
        rstd = kb.sb(c, "rstd", [128, HALF], F32)
        stg = [kb.sb(c, "stg%d" % i, [128, HALF], BF16) for i in range(3)]
        xr = kb.sb(c, "xr", [32, 512], F32)
        rt = kb.sb(c, "rt", [32, 2, 512], F32)
        kin = kb.sb(c, "kin", [128, HALF], F32)
        wis = kb.sb(c, "wis", [32, HALF], F32)
        wit = kb.sb(c, "wit", [128, 8, 32], F32)
        st = {"np": 0, "ns": 0, "nq": 0}

        def mm(wb, off, M, ps, tc):
            for k in range(32):
                kb.op("pe", lambda e: e.matmul(ps[0:M, :], lhsT=wb[:, k, off:off + M],
                                               rhs=hT[:, k, tc * 512:(tc + 1) * 512],
                                               start=(k == 0), stop=(k == 31)), reads=[wb, hT], writes=[ps])

        def finish_norm(ntile, rank, gcol, dst):
            for tc in range(2):
                cs = slice(tc * 512, (tc + 1) * 512)
                kb.op("dve", lambda e: e.tensor_scalar(out=rstd[:, cs], in0=pssq[tc][:, :], scalar1=1.0 / rank,
                                                       scalar2=EPS, op0=ALU.mult, op1=ALU.add),
                      reads=[pssq[tc]], writes=[rstd])
            kb.op("act", lambda e: e.activation(out=rstd[:, :], in_=rstd[:, :], func=AF.Sqrt), reads=[rstd], writes=[rstd])
            kb.op("dve", lambda e: e.reciprocal(out=rstd[:, :], in_=rstd[:, :]), reads=[rstd], writes=[rstd])
            for ct in range(ntile):
                s = stg[st["ns"] % 3]
                st["ns"] += 1
                kb.op("dve", lambda e: e.scalar_tensor_tensor(out=s[:, :], in0=raw[:, ct, :], scalar=gcol[:, ct:ct + 1],
                                                              in1=rstd[:, :], op0=ALU.mult, op1=ALU.mult),
                      reads=[raw, gcol, rstd], writes=[s])
                kb.dma("sp", dst, dst.t[ct, :, t0:t0 + HALF], s, s[:, :])

        def norm_tile(c0, ct, ntile, rank, gcol, dst):
            def fn(wb, off):
                for tc in range(2):
                    ps = pacc[st["np"] % 3]
                    st["np"] += 1
                    mm(wb, off, 128, ps, tc)
                    cs = slice(tc * 512, (tc + 1) * 512)
                    kb.op("act", lambda e: e.activation(out=raw[:, ct, cs], in_=ps[:, :], func=AF.Copy),
                          reads=[ps], writes=[raw])
                    sq_b = sq[st["nq"] % 2]
                    st["nq"] += 1
                    kb.op("act", lambda e: e.activation(out=sq_b[:, :], in_=ps[:, :], func=AF.Square),
                          reads=[ps], writes=[sq_b])
                    kb.op("pe", lambda e: e.matmul(pssq[tc][:, :], lhsT=G["onesb"][:, :], rhs=sq_b[:, :],
                                                   start=(ct == 0), stop=(ct == ntile - 1)),
                          reads=[G["onesb"], sq_b], writes=[pssq[tc]])
                if ct == ntile - 1:
                    finish_norm(ntile, rank, gcol, dst)
            return (c0, c0 + 128, fn)

        def kr_job():
            def fn(wb, off):
                s = stg[st["ns"] % 3]
                st["ns"] += 1
                for tc in range(2):
                    ps = pacc[st["np"] % 3]
                    st["np"] += 1
                    mm(wb, off, 32, ps, tc)
                    rope_evac(kb, G, (ps[0:32, :], ps),
                              lambda lo, hi: (s.t[lo:hi, tc * 512:(tc + 1) * 512].rearrange("p (r m) -> p r m", r=1), s),
                              t0 + tc * 512, xr, psw, rt, 1)
                kb.dma("sp", G["krT"], G["krT"].t[:, t0:t0 + HALF], s, s[0:32, :])
            return (L1_KR0, L1_KR0 + 32, fn)

        def ki_job():
            def fn(wb, off):
                s = stg[st["ns"] % 3]
                st["ns"] += 1
                for tc in range(2):
                    ps = pacc[st["np"] % 3]
                    st["np"] += 1
                    mm(wb, off, 128, ps, tc)
                    cs = slice(tc * 512, (tc + 1) * 512)
                    kb.op("act", lambda e: e.activation(out=kin[:, cs], in_=ps[:, :], func=AF.Copy),
                          reads=[ps], writes=[kin])
                    sq_b = sq[st["nq"] % 2]
                    st["nq"] += 1
                    kb.op("act", lambda e: e.activation(out=sq_b[:, :], in_=ps[:, :], func=AF.Square),
                          reads=[ps], writes=[sq_b])
                    kb.op("pe", lambda e: e.matmul(pssq[tc][:, :], lhsT=G["onesb"][:, :], rhs=sq_b[:, :],
                                                   start=True, stop=True), reads=[G["onesb"], sq_b], writes=[pssq[tc]])
                    kb.op("dve", lambda e: e.tensor_scalar(out=rstd[:, cs], in0=pssq[tc][:, :], scalar1=1.0 / 128,
                                                           scalar2=EPS, op0=ALU.mult, op1=ALU.add),
                          reads=[pssq[tc]], writes=[rstd])
                    kb.op("act", lambda e: e.activation(out=rstd[:, cs], in_=rstd[:, cs], func=AF.Sqrt),
                          reads=[rstd], writes=[rstd])
                    kb.op("dve", lambda e: e.reciprocal(out=rstd[:, cs], in_=rstd[:, cs]), reads=[rstd], writes=[rstd])
                    kb.op("dve", lambda e: e.scalar_tensor_tensor(out=kin[:, cs], in0=kin[:, cs], scalar=gki[:, 0:1],
                                                                  in1=rstd[:, cs], op0=ALU.mult, op1=ALU.mult),
                          reads=[kin, gki, rstd], writes=[kin])
                    for (lo, hi) in ((32, 64), (64, 128)):
                        kb.op("act", lambda e: e.activation(out=s[lo:hi, cs], in_=kin[lo:hi, cs], func=AF.Copy),
                              reads=[kin], writes=[s])
                    rope_evac(kb, G, (kin[0:32, cs], kin),
                              lambda lo, hi: (s.t[lo:hi, cs].rearrange("p (r m) -> p r m", r=1), s),
                              t0 + tc * 512, xr, psw, rt, 1)
                kb.dma("sp", G["kiT"], G["kiT"].t[:, t0:t0 + HALF], s, s[:, :])
            return (L1_KI0, L1_KI0 + 128, fn)

        def wi_job():
            def fn(wb, off):
                for tc in range(2):
                    ps = pacc[st["np"] % 3]
                    st["np"] += 1
                    mm(wb, off, 32, ps, tc)
                    kb.op("act", lambda e: e.activation(out=wis[:, tc * 512:(tc + 1) * 512], in_=ps[0:32, :],
                                                        func=AF.Copy, scale=float((128 * 32) ** -0.5)),
                          reads=[ps], writes=[wis])
                for tg in range(2):
                    for j in range(4):
                        ti = 4 * tg + j
                        kb.op("pe", lambda e: e.transpose(out=ptr[:, j, :], in_=wis[:, ti * 128:(ti + 1) * 128],
                                                          identity=G["identf"][0:32, 0:32]),
                              reads=[wis, G["identf"]], writes=[ptr])
                    kb.op("dve", lambda e: e.tensor_copy(out=wit[:, 4 * tg:4 * tg + 4, :], in_=ptr[:, :, :]),
                          reads=[ptr], writes=[wit])
                kb.dma("sp", G["wi"], G["wi"].t[t0:t0 + HALF, :].rearrange("(t p) h -> p t h", p=128), wit, wit[:, :, :])
            return (L1_WI0, L1_WI0 + 32, fn)

        def gate_job(ct):
            c0 = L1_GATE0 + ct * 128

            def fn(wb, off):
                s = stg[st["ns"] % 3]
                st["ns"] += 1
                for tc in range(2):
                    ps = pacc[st["np"] % 3]
                    st["np"] += 1
                    mm(wb, off, 128, ps, tc)
                    kb.op("act", lambda e: e.activation(out=s[:, tc * 512:(tc + 1) * 512], in_=ps[:, :], func=AF.Silu),
                          reads=[ps], writes=[s])
                kb.dma("sp", G["sgT"], G["sgT"].t[ct, :, t0:t0 + HALF], s, s[:, :])
            return (c0, c0 + 128, fn)

        jobs = [norm_tile(ct * 128, ct, 12, Q_RANK, gq, G["cqT"]) for ct in range(12)]
        jobs += [norm_tile(Q_RANK + ct * 128, ct, 4, KV_RANK, gkv, G["ckvT"]) for ct in range(4)]
        jobs += [kr_job(), ki_job(), wi_job()]
        jobs += [gate_job(ct) for ct in range(32)]
        stream_proj(kb, G["odd_w_in"], 32, jobs, wbufs)
    kb.barrier()


def layer1_qproj(kb, G, half):
    t0 = half * HALF
    with ExitStack() as c:
        cq = kb.sb(c, "cq", [128, 12, HALF], BF16)
        kb.dma("sp", cq, cq[:, :, :], G["cqT"], G["cqT"].t[:, :, t0:t0 + HALF].rearrange("k p t -> p k t"))
        wbufs = [kb.sb(c, "wq%d" % i, [128, 12, 256], BF16) for i in range(3)]
        pacc = [kb.ps(c, "pacc%d" % i, [128, 512], F32) for i in range(4)]
        psw = [kb.ps(c, "psw%d" % i, [32, 512], F32) for i in range(4)]
        stg = [kb.sb(c, "stg%d" % i, [128, HALF], BF16) for i in range(4)]
        xr = [kb.sb(c, "xr%d" % i, [32, 512], F32) for i in range(4)]
        rt = [kb.sb(c, "rt%d" % i, [32, 2, 512], F32) for i in range(4)]
        st = {"np": 0, "ns": 0, "nr": 0}

        def job(h, dst):
            def fn(wb, off):
                s = stg[st["ns"] % 4]
                st["ns"] += 1
                for tc in range(2):
                    ps = pacc[st["np"] % 4]
                    st["np"] += 1
                    cs = slice(tc * 512, (tc + 1) * 512)
                    for k in range(12):
                        kb.op("pe", lambda e: e.matmul(ps[:, :], lhsT=wb[:, k, off:off + 128], rhs=cq[:, k, cs],
                                                       start=(k == 0), stop=(k == 11)), reads=[wb, cq], writes=[ps])
                    for (lo, hi) in ((32, 64), (64, 128)):
                        kb.op("act", lambda e: e.activation(out=s[lo:hi, cs], in_=ps[lo:hi, :], func=AF.Copy),
                              reads=[ps], writes=[s])
                    i = st["nr"] % 4
                    st["nr"] += 1
                    rope_evac(kb, G, (ps[0:32, :], ps),
                              lambda lo, hi: (s.t[lo:hi, cs].rearrange("p (r m) -> p r m", r=1), s),
                              t0 + tc * 512, xr[i], psw[i], rt[i], 1)
                kb.dma("sp", dst, dst.t[h, :, t0:t0 + HALF], s, s[:, :])
            return (h * 128, (h + 1) * 128, fn)

        stream_proj(kb, G["mla_w_uq"], 12, [job(h, G["q1T"]) for h in range(32)], wbufs)
        stream_proj(kb, G["idx_w_q"], 12, [job(h, G["qiT"]) for h in range(32)], wbufs)
    kb.barrier()


def layer1_index(kb, G):
    POOL_HEADS = (5, 11, 17, 23, 29)
    with ExitStack() as c:
        kidx = kb.sb(c, "kidx", [128, SEQ], BF16)
        kb.dma("sp", kidx, kidx[:, :], G["kiT"], G["kiT"].t[:, :])
        negm = kb.sb(c, "negm", [128, 128], F32)
        kb.dma("sp", negm, negm[:, :], G["c_negmask"], G["c_negmask"].t[:, :])
        qi = [kb.sb(c, "qi%d" % i, [128, 32, 128], BF16) for i in range(2)]
        wq = [kb.sb(c, "wq%d" % i, [128, 32], F32) for i in range(2)]
        acc = [kb.sb(c, "acc%d" % i, [128, SEQ], F32) for i in range(2)]
        acc2 = [kb.sb(c, "accp%d" % i, [128, SEQ], F32) for i in range(2)]
        accP = [[Buf("accP%d_%d" % (i, k), acc[i].t) for k in range(4)] for i in range(2)]
        acc2P = [[Buf("acc2P%d_%d" % (i, k), acc2[i].t) for k in range(4)] for i in range(2)]
        rl = [kb.sb(c, "rl%d" % i, [128, 512], F32) for i in range(6)]
        tmpp = [kb.sb(c, "tmpp%d" % i, [128, 512], F32) for i in range(2)]
        wk = [kb.sb(c, "wk%d" % i, [128, SEQ], F32) for i in range(2)]
        m8 = [kb.sb(c, "m8_%d" % i, [128, 8], F32) for i in range(2)]
        msk = [kb.sb(c, "msk%d" % i, [128, SEQ], BF16) for i in range(2)]
        mTs = [kb.sb(c, "mTs%d" % i, [128, 16, 128], BF16) for i in range(2)]
        ps = [kb.ps(c, "pix%d" % i, [128, 512], F32) for i in range(6)]
        ptp = [kb.ps(c, "pmt%d" % i, [128, 8, 128], BF16) for i in range(2)]
        st = {"nps": 0, "nrl": 0, "ntp": 0, "ntm": 0}

        def accumulate(i):
            n = 128 * (i + 1)
            s = i % 2
            q_b, w_b, a_b, a2_b = qi[s], wq[s], acc[s], acc2[s]
            qs = slice(i * 128, (i + 1) * 128)
            kb.dma("sp", q_b, q_b[:, :, :], G["qiT"], G["qiT"].t[:, :, qs].rearrange("h d q -> d h q"))
            kb.dma("sp", w_b, w_b[:, :], G["wi"], G["wi"].t[qs, :])
            nch = (n + 511) // 512
            first = {"dve": True, "pool": True}
            for h in range(32):
                onpool = h in POOL_HEADS
                for kc in range(nch):
                    N = min(512, n - 512 * kc)
                    ks = slice(512 * kc, 512 * kc + N)
                    p_b = ps[st["nps"] % 6]
                    st["nps"] += 1
                    r_b = rl[st["nrl"] % 6]
                    st["nrl"] += 1
                    kb.op("pe", lambda e: e.matmul(p_b[:, 0:N], lhsT=q_b[:, h, :], rhs=kidx[:, ks], start=True, stop=True),
                          reads=[q_b, kidx], writes=[p_b])
                    kb.op("act", lambda e: e.activation(out=r_b[:, 0:N], in_=p_b[:, 0:N], func=AF.Relu),
                          reads=[p_b], writes=[r_b])
                    if not onpool:
                        ap_ = accP[s][kc]
                        if first["dve"]:
                            kb.op("dve", lambda e: e.tensor_scalar(out=a_b[:, ks], in0=r_b[:, 0:N], scalar1=w_b[:, h:h + 1],
                                                                   scalar2=None, op0=ALU.mult),
                                  reads=[r_b, w_b], writes=[ap_])
                        else:
                            kb.op("dve", lambda e: e.scalar_tensor_tensor(out=a_b[:, ks], in0=r_b[:, 0:N],
                                                                          scalar=w_b[:, h:h + 1], in1=a_b[:, ks],
                                                                          op0=ALU.mult, op1=ALU.add),
                                  reads=[r_b, w_b, ap_], writes=[ap_])
                    else:
                        ap_ = acc2P[s][kc]
                        if first["pool"]:
                            kb.op("pool", lambda e: e.tensor_scalar(out=a2_b[:, ks], in0=r_b[:, 0:N], scalar1=w_b[:, h:h + 1],
                                                                    scalar2=None, op0=ALU.mult),
                                  reads=[r_b, w_b], writes=[ap_])
                        else:
                            t_b = tmpp[st["ntm"] % 2]
                            st["ntm"] += 1
                            kb.op("pool", lambda e: e.tensor_scalar(out=t_b[:, 0:N], in0=r_b[:, 0:N],
                                                                    scalar1=w_b[:, h:h + 1], scalar2=None, op0=ALU.mult),
                                  reads=[r_b, w_b], writes=[t_b])
                            kb.op("pool", lambda e: e.tensor_tensor(out=a2_b[:, ks], in0=a2_b[:, ks], in1=t_b[:, 0:N],
                                                                    op=ALU.add), reads=[ap_, t_b], writes=[ap_])
                first["pool" if onpool else "dve"] = False
            for kc in range(nch):
                N = min(512, n - 512 * kc)
                ks = slice(512 * kc, 512 * kc + N)
                kb.op("dve", lambda e: e.tensor_tensor(out=a_b[:, ks], in0=a_b[:, ks], in1=a2_b[:, ks], op=ALU.add),
                      reads=[accP[s][kc], acc2P[s][kc]], writes=[accP[s][kc]])
            kd = i // 4
            kb.op("dve", lambda e: e.tensor_tensor(out=a_b[:, qs], in0=a_b[:, qs], in1=negm[:, :], op=ALU.add),
                  reads=[accP[s][kd], negm], writes=[accP[s][kd]])

        def topk_ops(i):
            n = 128 * (i + 1)
            s = i % 2
            nch = (n + 511) // 512
            a_b, w_, m_ = acc[s], wk[s], m8[s]
            ops = []
            ops.append(lambda: kb.op("dve", lambda e: e.tensor_copy(out=w_[:, 0:n], in_=a_b[:, 0:n]),
                                     reads=accP[s][0:nch], writes=[w_]))
            for r in range(32):
                ops.append(lambda: kb.op("dve", lambda e: e.max(out=m_[:, :], in_=w_[:, 0:n]), reads=[w_], writes=[m_]))
                if r < 31:
                    ops.append(lambda: kb.op("dve", lambda e: e.match_replace(out=w_[:, 0:n], in_to_replace=m_[:, :],
                                                                              in_values=w_[:, 0:n], imm_value=NEG),
                                             reads=[w_, m_], writes=[w_]))
            return ops

        def finish(i):
            n = 128 * (i + 1)
            s = i % 2
            nch = (n + 511) // 512
            qs = slice(i * 128, (i + 1) * 128)
            a_b, m_b, m_ = acc[s], msk[s], m8[s]
            if i >= 2:
                kb.op("dve", lambda e: e.tensor_scalar(out=m_b[:, 0:n], in0=a_b[:, 0:n], scalar1=m_[:, 7:8],
                                                       scalar2=None, op0=ALU.is_ge),
                      reads=accP[s][0:nch] + [m_], writes=[m_b])
            else:
                kb.op("dve", lambda e: e.tensor_scalar(out=m_b[:, 0:n], in0=a_b[:, 0:n], scalar1=-1.0e29,
                                                       scalar2=None, op0=ALU.is_ge), reads=accP[s][0:nch], writes=[m_b])
            mt_b = mTs[s]
            for j0 in range(0, i + 1, 8):
                nj = min(8, i + 1 - j0)
                t_b = ptp[st["ntp"] % 2]
                st["ntp"] += 1
                for jj in range(nj):
                    j = j0 + jj
                    kb.op("pe", lambda e: e.transpose(out=t_b[:, jj, :], in_=m_b[:, j * 128:(j + 1) * 128],
                                                      identity=G["ident"][:, :]), reads=[m_b, G["ident"]], writes=[t_b])
                kb.op("act", lambda e: e.activation(out=mt_b[:, j0:j0 + nj, :], in_=t_b[:, 0:nj, :], func=AF.Copy),
                      reads=[t_b], writes=[mt_b])
            kb.dma("sp", G["selT"], G["selT"].t[0:i + 1, :, qs].rearrange("j k q -> k j q"), mt_b, mt_b[:, 0:i + 1, :])

        for i0 in range(0, 16, 2):
            accumulate(i0)
            accumulate(i0 + 1)
            if i0 >= 2:
                oa, ob_ = topk_ops(i0), topk_ops(i0 + 1)
                for x, y in zip(oa, ob_):
                    x()
                    y()
            finish(i0)
            finish(i0 + 1)
    kb.barrier()


def layer1_attn(kb, G):
    with ExitStack() as c:
        ckv = kb.sb(c, "ckv", [128, 4, SEQ], BF16)
        kb.dma("sp", ckv, ckv[:, :, :], G["ckvT"], G["ckvT"].t[:, :, :].rearrange("k p t -> p k t"))
        krp = kb.sb(c, "krp", [128, SEQ], BF16)
        kb.op("dve", lambda e: e.memset(krp[:, :], 0.0), writes=[krp])
        kb.dma("sp", krp, krp[0:32, :], G["krT"], G["krT"].t[:, :])
        sel = kb.sb(c, "sel", [128, 16, SEQ], BF16)
        for j in range(16):
            kb.dma("sp", sel, sel[:, j, j * 128:SEQ], G["selT"], G["selT"].t[j, :, j * 128:SEQ])
        wuk = [kb.sb(c, "wuk%d" % i, [128, 4, 128], BF16) for i in range(2)]
        wuv = [kb.sb(c, "wuv%d" % i, [128, 4, 128], BF16) for i in range(2)]
        Qh = [kb.sb(c, "Qh%d" % i, [128, SEQ], BF16) for i in range(2)]
        sgh = [kb.sb(c, "sgh%d" % i, [128, SEQ], BF16) for i in range(2)]
        KhT = [kb.sb(c, "KhT%d" % i, [128, SEQ], BF16) for i in range(2)]
        Va = [kb.sb(c, "Va%d" % i, [128, 16, 130], BF16) for i in range(2)]
        ogh = [kb.sb(c, "ogh%d" % i, [128, SEQ], BF16) for i in range(2)]
        PT = [kb.sb(c, "PT%d" % i, [128, 16, 128], BF16) for i in range(2)]
        on = [kb.sb(c, "on%d" % i, [128, 128], BF16) for i in range(2)]
        rd = [kb.sb(c, "rd%d" % i, [128, 2], F32) for i in range(2)]
        pk = [kb.ps(c, "pk%d" % i, [128, 512], F32) for i in range(2)]
        pst = [kb.ps(c, "pst%d" % i, [128, 4, 128], F32) for i in range(2)]
        po = [kb.ps(c, "pov%d" % i, [128, 130], F32) for i in range(2)]
        ptr = [kb.ps(c, "ptr%d" % i, [128, 128], BF16) for i in range(2)]
        for i in range(2):
            kb.op("dve", lambda e: e.memset(wuk[i][:, :, :], 0.0), writes=[wuk[i]])
            kb.op("dve", lambda e: e.memset(Va[i][:, :, 128:130], 1.0), writes=[Va[i]])
        ukv = G["mla_w_uk"].t.rearrange("(k p) h d -> p k h d", p=128)
        uvv = G["mla_w_uv"].t.rearrange("(k p) h d -> p k h d", p=128)
        npk = 0
        nst = 0
        npo = 0
        nq = 0
        for h in range(32):
            s = h % 2
            kb.dma("pool", wuk[s], wuk[s][:, :, 32:128], G["mla_w_uk"], ukv[:, :, h, :])
            kb.dma("pool", wuv[s], wuv[s][:, :, :], G["mla_w_uv"], uvv[:, :, h, :])
            kb.dma("sp", Qh[s], Qh[s][:, :], G["q1T"], G["q1T"].t[h, :, :])
            kb.dma("sp", sgh[s], sgh[s][:, :], G["sgT"], G["sgT"].t[h, :, :])
            K_, V_ = KhT[s], Va[s]
            for kc in range(4):
                cs = slice(kc * 512, (kc + 1) * 512)
                p_b = pk[npk % 2]
                npk += 1
                for c4 in range(4):
                    kb.op("pe", lambda e: e.matmul(p_b[:, :], lhsT=wuk[s][:, c4, :], rhs=ckv[:, c4, cs],
                                                   start=(c4 == 0), stop=(c4 == 3)), reads=[wuk[s], ckv], writes=[p_b])
                kb.op("dve", lambda e: e.tensor_tensor(out=K_[:, cs], in0=p_b[:, :], in1=krp[:, cs], op=ALU.add),
                      reads=[p_b, krp], writes=[K_])
            for j0 in range(0, 16, 4):
                p_b = pk[npk % 2]
                npk += 1
                for jj in range(4):
                    j = j0 + jj
                    for c4 in range(4):
                        kb.op("pe", lambda e: e.matmul(p_b[:, jj * 128:(jj + 1) * 128], lhsT=ckv[:, c4, j * 128:(j + 1) * 128],
                                                       rhs=wuv[s][:, c4, :], start=(c4 == 0), stop=(c4 == 3)),
                              reads=[ckv, wuv[s]], writes=[p_b])
                kb.op("act", lambda e: e.activation(out=V_[:, j0:j0 + 4, 0:128],
                                                    in_=p_b.t[:, :].rearrange("p (a v) -> p a v", a=4), func=AF.Copy),
                      reads=[p_b], writes=[V_])
            og = ogh[s]

            def scores(i):
                nonlocal nst
                qs = slice(i * 128, (i + 1) * 128)
                P_ = PT[i % 2]
                for j0 in range(0, i + 1, 4):
                    nj = min(4, i + 1 - j0)
                    s_b = pst[nst % 2]
                    nst += 1
                    for jj in range(nj):
                        j = j0 + jj
                        kb.op("pe", lambda e: e.matmul(s_b[:, jj, :], lhsT=K_[:, j * 128:(j + 1) * 128], rhs=Qh[s][:, qs],
                                                       start=True, stop=True), reads=[K_, Qh[s]], writes=[s_b])
                    kb.op("act", lambda e: e.activation(out=P_[:, j0:j0 + nj, :], in_=s_b[:, 0:nj, :], func=AF.Exp,
                                                        scale=ATT_SCALE), reads=[s_b], writes=[P_])
                kb.op("dve", lambda e: e.tensor_tensor(out=P_[:, 0:i + 1, :], in0=P_[:, 0:i + 1, :],
                                                       in1=sel[:, 0:i + 1, qs], op=ALU.mult), reads=[P_, sel], writes=[P_])

            def pv(i):
                P_ = PT[i % 2]
                o_b, on_b, r_b = po[i % 2], on[i % 2], rd[i % 2]
                for j in range(i + 1):
                    kb.op("pe", lambda e: e.matmul(o_b[:, 0:129], lhsT=P_[:, j, :], rhs=V_[:, j, 0:129],
                                                   start=(j == 0), stop=(j == i)), reads=[P_, V_], writes=[o_b])
                kb.op("dve", lambda e: e.reciprocal(out=r_b[:, 0:1], in_=o_b[:, 128:129]), reads=[o_b], writes=[r_b])
                kb.op("act", lambda e: e.activation(out=on_b[:, :], in_=o_b[:, 0:128], func=AF.Copy, scale=r_b[:, 0:1]),
                      reads=[o_b, r_b], writes=[on_b])

            def outT(i):
                qs = slice(i * 128, (i + 1) * 128)
                t_b, on_b = ptr[i % 2], on[i % 2]
                kb.op("pe", lambda e: e.transpose(out=t_b[:, :], in_=on_b[:, :], identity=G["ident"][:, :]),
                      reads=[on_b, G["ident"]], writes=[t_b])
                kb.op("dve", lambda e: e.tensor_tensor(out=og[:, qs], in0=t_b[:, :], in1=sgh[s][:, qs], op=ALU.mult),
                      reads=[t_b, sgh[s]], writes=[og])

            scores(0)
            for i in range(16):
                if i + 1 < 16:
                    scores(i + 1)
                pv(i)
                if i >= 1:
                    outT(i - 1)
            outT(15)
            kb.dma("sp", G["ogT"], G["ogT"].t[h, :, :], og, og[:, :])
    kb.barrier()


def final_norm(kb, G):
    with ExitStack() as c:
        gbc = kb.sb(c, "fgbc", [128, D_MODEL], F32)
        xt = [kb.sb(c, "fxt%d" % i, [128, D_MODEL], F32) for i in range(2)]
        junk = kb.sb(c, "fjunk", [128, D_MODEL], BF16)
        yo = [kb.sb(c, "fyo%d" % i, [128, D_MODEL], F32) for i in range(2)]
        ss = [kb.sb(c, "fss%d" % i, [128, 4], F32) for i in range(2)]
        kb.dma("sp", gbc, gbc[:, :], G["final_norm"], G["final_norm"].t.partition_broadcast(128))
        for ti in range(16):
            x_b, y_b, s_b = xt[ti % 2], yo[ti % 2], ss[ti % 2]
            rows = slice(ti * 128, (ti + 1) * 128)
            kb.dma("sp", x_b, x_b[:, :], G["x2"], G["x2"].t[rows, :])
            kb.op("act", lambda e: e.activation(out=junk[:, :], in_=x_b[:, :], func=AF.Square, accum_out=s_b[:, 0:1]),
                  reads=[x_b], writes=[junk, s_b])
            kb.op("dve", lambda e: e.tensor_scalar(out=s_b[:, 1:2], in0=s_b[:, 0:1], scalar1=1.0 / D_MODEL, scalar2=EPS,
                                                   op0=ALU.mult, op1=ALU.add), reads=[s_b], writes=[s_b])
            kb.op("act", lambda e: e.activation(out=s_b[:, 2:3], in_=s_b[:, 1:2], func=AF.Sqrt), reads=[s_b], writes=[s_b])
            kb.op("dve", lambda e: e.reciprocal(out=s_b[:, 3:4], in_=s_b[:, 2:3]), reads=[s_b], writes=[s_b])
            kb.op("dve", lambda e: e.scalar_tensor_tensor(out=y_b[:, :], in0=x_b[:, :], scalar=s_b[:, 3:4], in1=gbc[:, :],
                                                          op0=ALU.mult, op1=ALU.mult), reads=[x_b, s_b, gbc], writes=[y_b])
            kb.dma("sp", G["out"], G["out"].t[rows, :], y_b, y_b[:, :])
    kb.barrier()
```
